# Optimizing a Trainium2 kernel written in Bass

```python
import math
import jax, jax.numpy as jnp
from jax import lax
import numpy as np

D_MODEL = 1024
BATCH = 8
SEQ = 4096
DEPTH = 2

HEAD_DIM = 64
N_HEADS_MOBA = 8
N_HEADS_DIL = 8
N_HEADS_EVEN = N_HEADS_MOBA + N_HEADS_DIL
EVEN_WIDTH = N_HEADS_EVEN * HEAD_DIM
MOBA_BLOCK = 256
MOBA_TOPK = 3
MOBA_Q_CHUNK = 32
DIL_PAIRS = ((128, 1), (512, 4), (2048, 16))
N_HEADS_DIFF = 8
DIFF_HEAD_DIM = 64
DIFF_WIDTH = N_HEADS_DIFF * 2 * DIFF_HEAD_DIM
DIFF_Q_BLOCK = 128
DIFF_EPS = 1e-5
D_FF = 2816
CONV_WIDTH = 3
ROPE_THETA = 10000.0
NORM_EPS = 1e-6
N_EVEN = (DEPTH + 1) // 2
N_ODD = DEPTH // 2

kernel_name = 'hybrid_moba_dilated_diffattn_convffn'

f32 = jnp.float32


def rms_norm(x, g, eps=NORM_EPS):
    xf = x.astype(f32)
    y = xf * lax.rsqrt(jnp.mean(xf * xf, axis=-1, keepdims=True) + eps)
    return (y * g.astype(f32)).astype(x.dtype)


def rope(t):
    S, dim = t.shape[-2], t.shape[-1]
    half = dim // 2
    inv = 1.0 / (ROPE_THETA ** (jnp.arange(half, dtype=f32) * 2.0 / dim))
    ang = jnp.arange(S, dtype=f32)[:, None] * inv[None, :]
    cos, sin = jnp.cos(ang), jnp.sin(ang)
    tf = t.astype(f32)
    t1, t2 = tf[..., :half], tf[..., half:]
    return jnp.concatenate([t1 * cos - t2 * sin, t2 * cos + t1 * sin], axis=-1).astype(t.dtype)


def moba_attention(q, k, v):
    B, H, S, hd = q.shape
    nb = -(-S // MOBA_BLOCK)
    Sp = nb * MOBA_BLOCK
    pad = ((0, 0), (0, 0), (0, Sp - S), (0, 0))
    kb = jnp.pad(k, pad).reshape(B, H, nb, MOBA_BLOCK, hd)
    vb = jnp.pad(v, pad).reshape(B, H, nb, MOBA_BLOCK, hd)
    kmean = jnp.mean(kb.astype(f32), axis=3)
    n_sel = min(MOBA_TOPK, nb - 1)
    scale = hd ** -0.5
    n_chunks = S // MOBA_Q_CHUNK

    def chunk(c):
        q0 = c * MOBA_Q_CHUNK
        qc = lax.dynamic_slice_in_dim(q, q0, MOBA_Q_CHUNK, axis=2)
        blk = q0 // MOBA_BLOCK
        qpos = q0 + jnp.arange(MOBA_Q_CHUNK)
        own_k = lax.dynamic_index_in_dim(kb, blk, axis=2, keepdims=False)
        own_v = lax.dynamic_index_in_dim(vb, blk, axis=2, keepdims=False)
        kpos = blk * MOBA_BLOCK + jnp.arange(MOBA_BLOCK)
        s_own = jnp.einsum('bhqd,bhkd->bhqk', qc, own_k).astype(f32) * scale
        s_own = jnp.where(kpos[None, :] <= qpos[:, None], s_own, -jnp.inf)
        if n_sel == 0:
            p = jax.nn.softmax(s_own, axis=-1)
            return jnp.einsum('bhqk,bhkd->bhqd', p.astype(v.dtype), own_v)
        gate = jnp.einsum('bhqd,bhnd->bhqn', qc.astype(f32), kmean)
        gate = jnp.where(jnp.arange(nb) < blk, gate, -jnp.inf)
        _, idx = lax.top_k(gate, n_sel)
        valid = jnp.arange(n_sel) < blk
        take = jax.vmap(jax.vmap(lambda tb, ib: tb[ib]))
        sel_k = take(kb, idx)
        sel_v = take(vb, idx)
        s_sel = jnp.einsum('bhqd,bhqjkd->bhqjk', qc, sel_k).astype(f32) * scale
        s_sel = jnp.where(valid[:, None], s_sel, -jnp.inf)
        s_all = jnp.concatenate(
            [s_own, s_sel.reshape(B, H, MOBA_Q_CHUNK, n_sel * MOBA_BLOCK)], axis=-1)
        p = jax.nn.softmax(s_all, axis=-1).astype(v.dtype)
        p_own = p[..., :MOBA_BLOCK]
        p_sel = p[..., MOBA_BLOCK:].reshape(B, H, MOBA_Q_CHUNK, n_sel, MOBA_BLOCK)
        return (jnp.einsum('bhqk,bhkd->bhqd', p_own, own_v)
                + jnp.einsum('bhqjk,bhqjkd->bhqd', p_sel, sel_v))

    out = lax.map(chunk, jnp.arange(n_chunks))
    return out.transpose(1, 2, 0, 3, 4).reshape(B, H, S, hd)


def dilated_window_attention(q, k, v, window, dilation):
    B, H, S, hd = q.shape
    n = window // dilation
    L = S // dilation
    nblk = -(-L // n)
    Lp = nblk * n

    def to_residue(t):
        t = t.reshape(B, H, L, dilation, hd).transpose(0, 1, 3, 2, 4)
        return jnp.pad(t, ((0, 0), (0, 0), (0, 0), (0, Lp - L), (0, 0)))

    def band(t):
        own = t.reshape(B, H, dilation, nblk, n, hd)
        prev = jnp.pad(t, ((0, 0), (0, 0), (0, 0), (n, 0), (0, 0)))[..., :Lp, :]
        prev = prev.reshape(B, H, dilation, nblk, n, hd)
        return jnp.concatenate([prev, own], axis=-2)

    qb = to_residue(q).reshape(B, H, dilation, nblk, n, hd)
    kw = band(to_residue(k))
    vw = band(to_residue(v))
    s = jnp.einsum('bhrnqd,bhrnkd->bhrnqk', qb, kw).astype(f32) * (hd ** -0.5)
    i = jnp.arange(n)[:, None]
    j = jnp.arange(2 * n)[None, :]
    blk = jnp.arange(nblk)[:, None, None]
    dist = n + i - j
    kidx = (blk - 1) * n + j
    mask = (dist >= 0) & (dist <= n) & (kidx >= 0)
    s = jnp.where(mask, s, -jnp.inf)
    m = jnp.max(s, axis=-1, keepdims=True)
    p = jnp.exp(s - m)
    den = jnp.sum(p, axis=-1, keepdims=True)
    o = jnp.einsum('bhrnqk,bhrnkd->bhrnqd', (p / den).astype(v.dtype), vw)
    lse = (m + jnp.log(den))[..., 0]
    o = o.reshape(B, H, dilation, Lp, hd)[:, :, :, :L, :]
    o = o.transpose(0, 1, 3, 2, 4).reshape(B, H, S, hd)
    lse = lse.reshape(B, H, dilation, Lp)[:, :, :, :L].transpose(0, 1, 3, 2).reshape(B, H, S)
    return o, lse


def dilated_mixture(q, k, v):
    results = [dilated_window_attention(q, k, v, w, r) for (w, r) in DIL_PAIRS]
    outs = jnp.stack([o for (o, _) in results]).astype(f32)
    lses = jnp.stack([l for (_, l) in results])
    wts = jax.nn.softmax(lses, axis=0)
    return jnp.einsum('gbhs,gbhsd->bhsd', wts, outs).astype(q.dtype)


def even_mixer(h, w_in, w_out):
    B, S, _ = h.shape
    qkv = (h @ w_in).reshape(B, S, 3, N_HEADS_EVEN, HEAD_DIM).transpose(2, 0, 3, 1, 4)
    q, k, v = rope(qkv[0]), rope(qkv[1]), qkv[2]
    na = N_HEADS_MOBA
    a = moba_attention(q[:, :na], k[:, :na], v[:, :na])
    b = dilated_mixture(q[:, na:], k[:, na:], v[:, na:])
    o = jnp.concatenate([a, b], axis=1).transpose(0, 2, 1, 3).reshape(B, S, EVEN_WIDTH)
    return o @ w_out


def diff_attention(q, k, v, lam):
    B, H, _, S, d = q.shape
    nq = S // DIFF_Q_BLOCK
    scale = d ** -0.5
    qb = q.reshape(B, H, 2, nq, DIFF_Q_BLOCK, d).transpose(3, 0, 1, 2, 4, 5)
    kpos = jnp.arange(S)

    def block(args):
        qblk, bi = args
        qpos = bi * DIFF_Q_BLOCK + jnp.arange(DIFF_Q_BLOCK)
        s = jnp.einsum('bhcqd,bhckd->bhcqk', qblk, k).astype(f32) * scale
        s = jnp.where(kpos[None, :] <= qpos[:, None], s, -jnp.inf)
        p = jax.nn.softmax(s, axis=-1)
        a = p[:, :, 0] - lam * p[:, :, 1]
        return jnp.einsum('bhqk,bhkd->bhqd', a.astype(v.dtype), v)

    out = lax.map(block, (qb, jnp.arange(nq)))
    return out.transpose(1, 2, 0, 3, 4).reshape(B, H, S, 2 * d)


def diff_mixer(h, w_qkv, lq1, lk1, lq2, lk2, subln, w_out, lambda_init):
    B, S, _ = h.shape
    H, d = N_HEADS_DIFF, DIFF_HEAD_DIM
    qkv = h @ w_qkv
    q = qkv[..., :2 * H * d].reshape(B, S, H, 2, d).transpose(0, 2, 3, 1, 4)
    k = qkv[..., 2 * H * d:4 * H * d].reshape(B, S, H, 2, d).transpose(0, 2, 3, 1, 4)
    v = qkv[..., 4 * H * d:].reshape(B, S, H, 2 * d).transpose(0, 2, 1, 3)
    q, k = rope(q), rope(k)
    lam = (jnp.exp(jnp.sum(lq1.astype(f32) * lk1.astype(f32)))
           - jnp.exp(jnp.sum(lq2.astype(f32) * lk2.astype(f32))) + lambda_init)
    o = diff_attention(q, k, v, lam)
    o = rms_norm(o, subln, DIFF_EPS) * (1.0 - lambda_init)
    return o.transpose(0, 2, 1, 3).reshape(B, S, DIFF_WIDTH) @ w_out


def causal_depthwise_conv(u, w, b):
    C = u.shape[-1]
    y = lax.conv_general_dilated(
        u, w[:, None, :].astype(u.dtype), window_strides=(1,),
        padding=((CONV_WIDTH - 1, 0),), dimension_numbers=('NWC', 'WIO', 'NWC'),
        feature_group_count=C)
    return y + b


def conv_ffn(h, w_up, conv_w, conv_b, w_down):
    u = causal_depthwise_conv(h @ w_up, conv_w, conv_b)
    gate, val = u[..., :D_FF], u[..., D_FF:]
    return (jax.nn.silu(gate) * val) @ w_down


def setup_inputs(seed: int = 0) -> dict:
    key = jax.random.key(seed)
    ks = jax.random.split(key, 18)

    def nrm(kk, shape, scale):
        return jax.random.normal(kk, shape, f32) * scale

    d = DIFF_HEAD_DIM
    return {
        'x': nrm(ks[0], (BATCH, SEQ, D_MODEL), 1.0),
        'even_norm': 1.0 + nrm(ks[1], (N_EVEN, D_MODEL), 0.1),
        'even_w_in': nrm(ks[2], (N_EVEN, D_MODEL, 3 * EVEN_WIDTH), D_MODEL ** -0.5),
        'even_w_out': nrm(ks[3], (N_EVEN, EVEN_WIDTH, D_MODEL), EVEN_WIDTH ** -0.5),
        'odd_norm': 1.0 + nrm(ks[4], (N_ODD, D_MODEL), 0.1),
        'odd_w_qkv': nrm(ks[5], (N_ODD, D_MODEL, 3 * DIFF_WIDTH), D_MODEL ** -0.5),
        'odd_lambda_q1': nrm(ks[6], (N_ODD, d), 0.1),
        'odd_lambda_k1': nrm(ks[7], (N_ODD, d), 0.1),
        'odd_lambda_q2': nrm(ks[8], (N_ODD, d), 0.1),
        'odd_lambda_k2': nrm(ks[9], (N_ODD, d), 0.1),
        'odd_subln': 1.0 + nrm(ks[10], (N_ODD, 2 * d), 0.1),
        'odd_w_out': nrm(ks[11], (N_ODD, DIFF_WIDTH, D_MODEL), DIFF_WIDTH ** -0.5),
        'ffn_norm': 1.0 + nrm(ks[12], (DEPTH, D_MODEL), 0.1),
        'ffn_w_up': nrm(ks[13], (DEPTH, D_MODEL, 2 * D_FF), D_MODEL ** -0.5),
        'ffn_conv_w': nrm(ks[14], (DEPTH, CONV_WIDTH, 2 * D_FF), CONV_WIDTH ** -0.5),
        'ffn_conv_b': nrm(ks[15], (DEPTH, 2 * D_FF), 0.02),
        'ffn_w_down': nrm(ks[16], (DEPTH, D_FF, D_MODEL), D_FF ** -0.5),
        'final_norm': 1.0 + nrm(ks[17], (D_MODEL,), 0.1),
    }


def reference(x, even_norm, even_w_in, even_w_out, odd_norm, odd_w_qkv,
              odd_lambda_q1, odd_lambda_k1, odd_lambda_q2, odd_lambda_k2, odd_subln,
              odd_w_out, ffn_norm, ffn_w_up, ffn_conv_w, ffn_conv_b, ffn_w_down,
              final_norm):
    for layer in range(DEPTH):
        i = layer // 2
        if layer % 2 == 0:
            x = x + even_mixer(rms_norm(x, even_norm[i]), even_w_in[i], even_w_out[i])
        else:
            lambda_init = 0.8 - 0.6 * math.exp(-0.3 * layer)
            x = x + diff_mixer(rms_norm(x, odd_norm[i]), odd_w_qkv[i],
                               odd_lambda_q1[i], odd_lambda_k1[i],
                               odd_lambda_q2[i], odd_lambda_k2[i],
                               odd_subln[i], odd_w_out[i], lambda_init)
        x = x + conv_ffn(rms_norm(x, ffn_norm[layer]), ffn_w_up[layer],
                         ffn_conv_w[layer], ffn_conv_b[layer], ffn_w_down[layer])
    return rms_norm(x, final_norm)
```

```python
import math
from contextlib import ExitStack

import numpy as np
import ml_dtypes

import concourse.bass as bass
import concourse.mybir as mybir
from concourse.bass_utils import run_bass_kernel_spmd

F32 = mybir.dt.float32
BF16 = mybir.dt.bfloat16
AF = mybir.ActivationFunctionType
ALU = mybir.AluOpType
AX = mybir.AxisListType

D = 1024
DFF = 2816
NK = D // 128
NFF = DFF // 128
SEQ = 4096
NCORES = 8
LAMBDA_INIT = 0.8 - 0.6 * math.exp(-0.3 * 1)
MASKBIG = 30000.0
NEG = -1.0e30


class Res:
    __slots__ = ("w", "r", "excl")

    def __init__(self, excl=False):
        self.w = {}
        self.r = {}
        self.excl = excl


class DmaSem:
    def __init__(self, key, handle, main=True):
        self.key = key
        self.handle = handle
        self.count = 0
        self.main = main
        self.refs = set()


def _merge(d, s):
    for k, v in s.items():
        if d.get(k, 0) < v:
            d[k] = v


class Sched:
    CE = ("pe", "act", "dve")

    def __init__(self, nc, stack):
        self.nc = nc
        self.stack = stack
        self.eng = {"pe": nc.tensor, "act": nc.scalar, "dve": nc.vector, "pool": nc.gpsimd, "sp": nc.sync}
        self.epoch = {e: 0 for e in ("pe", "act", "dve", "pool")}
        self.cnt = {e: 0 for e in ("pe", "act", "dve", "pool")}
        self.semh = {}
        self.waited = {e: {} for e in self.eng}
        self.dsems = {}
        self.all_res = []
        self.n_inst = 0

    def res(self, excl=False):
        r = Res(excl)
        self.all_res.append(r)
        return r

    def pres(self):
        return self.res(excl=True)

    def _sem(self, key):
        h = self.semh.get(key)
        if h is None:
            h = self.stack.enter_context(self.nc.semaphore("s" + "_".join(str(x) for x in key)))
            self.semh[key] = h
        return h

    def ekey(self, e):
        return ("e", e, self.epoch[e])

    def dsem(self, name, main=True):
        d = self.dsems.get(name)
        if d is None:
            key = ("d", name)
            d = DmaSem(key, self._sem(key), main)
            self.dsems[name] = d
        return d

    def _wait(self, eng, deps):
        for k, v in deps.items():
            if eng == "pe" and k[0] == "e" and k[1] == "pe":
                continue
            if self.waited[eng].get(k, 0) >= v:
                continue
            self.waited[eng][k] = v
            self.eng[eng].wait_ge(self.semh[k], v)
            self.n_inst += 1

    def _deps(self, reads, writes):
        deps = {}
        for r in reads:
            _merge(deps, r.w)
        for w in writes:
            _merge(deps, w.w)
            _merge(deps, w.r)
        return deps

    def op(self, eng, fn, reads=(), writes=(), pool_main=False):
        if pool_main:
            self.pool_dirty = True
        deps = self._deps(reads, writes)
        for r in reads:
            if r.excl:
                for k, v in r.r.items():
                    if k[1] != eng and deps.get(k, 0) < v:
                        deps[k] = v
        self._wait(eng, deps)
        self.cnt[eng] += 1
        k = self.ekey(eng)
        v = self.cnt[eng]
        fn(self.eng[eng]).then_inc(self._sem(k), 1)
        self.n_inst += 1
        for r in reads:
            if r.r.get(k, 0) < v:
                r.r[k] = v
        for w in writes:
            w.w = {k: v}
            w.r = {}

    def dma(self, q, out, in_, dsem, reads=(), writes=()):
        self._wait(q, self._deps(reads, writes))
        dsem.count += 16
        self.eng[q].dma_start(out=out, in_=in_).then_inc(dsem.handle, 16)
        self.n_inst += 1
        k = dsem.key
        v = dsem.count
        for r in reads:
            if r.r.get(k, 0) < v:
                r.r[k] = v
            dsem.refs.add(r)
        for w in writes:
            w.w = {k: v}
            w.r = {}
            dsem.refs.add(w)
        dead = []
        for r in (dsem.refs if dsem.main else ()):
            hit = False
            if k in r.w:
                r.w[k] = v
                hit = True
            if k in r.r:
                r.r[k] = v
                hit = True
            if not hit:
                dead.append(r)
        for r in dead:
            dsem.refs.discard(r)

    def barrier(self, new_epoch=False):
        deps = {}
        for e in self.CE:
            if self.cnt[e] > 0:
                deps[self.ekey(e)] = self.cnt[e]
        for d in self.dsems.values():
            if d.main and d.count > 0:
                deps[d.key] = d.count
        if getattr(self, "pool_dirty", False):
            deps[self.ekey("pool")] = self.cnt["pool"]
            self.pool_dirty = False
        for e in ("pe", "act", "dve", "sp"):
            self._wait(e, deps)
        if new_epoch:
            old = set()
            for e in self.CE:
                old.add(self.ekey(e))
                self.epoch[e] += 1
                self.cnt[e] = 0
            for r in self.all_res:
                for k in list(r.w.keys()):
                    if k in old:
                        del r.w[k]
                for k in list(r.r.keys()):
                    if k in old:
                        del r.r[k]

    def final_wait(self, eng="sp"):
        deps = {}
        for e in ("pe", "act", "dve", "pool"):
            if self.cnt[e] > 0:
                deps[self.ekey(e)] = self.cnt[e]
        for d in self.dsems.values():
            if d.count > 0:
                deps[d.key] = d.count
        self._wait(eng, deps)


def host_constants(S):
    half = 32
    inv = (np.float32(1.0) / (np.float32(10000.0) ** (np.arange(half, dtype=np.float32) * np.float32(2.0) / np.float32(64)))).astype(np.float32)
    ang = (np.arange(S, dtype=np.float32)[:, None] * inv[None, :]).astype(np.float32)
    cos = np.cos(ang).astype(np.float32)
    sin = np.sin(ang).astype(np.float32)
    j = np.arange(128) % 32
    cosT = np.ascontiguousarray(cos[:, j].T)
    sinT = np.ascontiguousarray(sin[:, j].T)
    R = np.zeros((128, 128), np.float32)
    for m in range(128):
        d = m % 64
        base = m - d
        if d < 32:
            R[base + d + 32, m] = -1.0
        else:
            R[base + d - 32, m] = 1.0
    ident = np.eye(128, dtype=np.float32)
    n = np.arange(16)[:, None]
    kk = np.arange(S)[None, :]
    erows = (MASKBIG * ((kk // 256) == n)).astype(np.float32)
    ki = np.arange(128)[:, None]
    jj = np.arange(2048 + 384 + 512)[None, :]
    Dm = jj - 384 - ki
    c = ((Dm >= 0) & (Dm <= 128)).astype(np.float32)
    c += ((Dm >= 0) & (Dm <= 512) & (Dm % 4 == 0)).astype(np.float32)
    c += ((Dm >= 0) & (Dm <= 2048) & (Dm % 16 == 0)).astype(np.float32)
    qi = np.arange(128)[None, :]
    tri = (qi >= ki).astype(np.float32)
    bf = ml_dtypes.bfloat16
    return dict(cosT=cosT, sinT=sinT, rperm=R.astype(bf), ident=ident.astype(bf),
                erows=erows.astype(bf), strip_d=c.astype(bf), tri=tri.astype(bf))


class Builder:
    def __init__(self, nc, S, debug=False, nlayers=2):
        self.nc = nc
        self.S = S
        self.NT = S // 512
        self.NS = S // 128
        self.debug = debug
        self.nlayers = nlayers

    def dram_in(self, name, shape, dt):
        return self.nc.dram_tensor(name, list(shape), dt, kind="ExternalInput").ap()

    def dram_scr(self, name, shape, dt, dbg=False):
        kind = "ExternalOutput" if (dbg and self.debug) else "Internal"
        return self.nc.dram_tensor(name, list(shape), dt, kind=kind).ap()

    def _uid(self, name):
        self._n = getattr(self, "_n", 0) + 1
        return "t%d_%s" % (self._n, name)

    def sb(self, st, name, shape, dt):
        return st.enter_context(self.nc.sbuf_tensor(self._uid(name), list(shape), dt))

    def ps(self, st, name, shape, dt):
        return st.enter_context(self.nc.psum_tensor(self._uid(name), list(shape), dt))

    def build(self):
        nc, S = self.nc, self.S
        with ExitStack() as top:
            self.top = top
            self.sc = Sched(nc, top)
            self.declare_io()
            self.setup_constants()
            stop = getattr(self, "stop_after", None)
            if stop == "const":
                self.sc.final_wait("sp")
                return
            self.weight_prep()
            if stop == "prep":
                self.sc.final_wait("sp")
                return
            for l in range(self.nlayers):
                last = (l == self.nlayers - 1)
                xsrc = self.x if l == 0 else self.xs
                xdst = self.out if last else self.xs
                with ExitStack() as st12:
                    hT = self.sb(st12, "hT", [128, NK, S], BF16)
                    R_hT = [self.sc.res() for _ in range(self.NT)]
                    self.phase_norm(l, xsrc, hT, R_hT)
                    if stop == "norm":
                        self.sc.final_wait("sp")
                        return
                    self.phase_proj(l, hT, R_hT)
                    self.sc.barrier()
                if stop == "proj":
                    self.sc.final_wait("sp")
                    return
                self.phase_attn(l)
                if stop == "attn":
                    self.sc.final_wait("sp")
                    return
                self.sc.barrier(new_epoch=True)
                self.phase_ffn(l, xsrc, xdst, last)
                self.sc.barrier(new_epoch=True)
            self.sc.final_wait("sp")

    def declare_io(self):
        S = self.S
        sc = self.sc
        di = self.dram_in
        self.x = di("x", [S, D], F32)
        self.w_in = [di("w_in0", [D, 3 * D], F32), di("w_in1", [D, 3 * D], F32)]
        self.w_out = [di("w_out0", [D, D], F32), di("w_out1", [D, D], F32)]
        self.w_up = [di("w_up0", [D, 2 * DFF], F32), di("w_up1", [D, 2 * DFF], F32)]
        self.w_dn = [di("w_dn0", [DFF, D], F32), di("w_dn1", [DFF, D], F32)]
        self.gin_d = di("gin_t", [128, 16], F32)
        self.gffn_d = di("gffn_t", [128, 16], F32)
        self.subl_d = di("subl_t", [128, 1], F32)
        self.lamq_d = di("lamq", [4, 64], F32)
        self.cw_d = di("cw_t", [128, 2 * 3 * 44], F32)
        self.cb_d = di("cb_t", [128, 2 * 44], F32)
        self.gfin_d = di("gfin", [1, D], F32)
        self.cos_d = di("cosT", [128, S], F32)
        self.sin_d = di("sinT", [128, S], F32)
        self.rperm_d = di("rperm", [128, 128], BF16)
        self.ident_d = di("ident", [128, 128], BF16)
        self.erows_d = di("erows", [16, S], BF16)
        self.stripd_d = di("strip_d", [128, 2944], BF16)
        self.tri_d = di("tri", [128, 128], BF16)
        self.out = self.nc.dram_tensor("out", [S, D], F32, kind="ExternalOutput").ap()
        ds = self.dram_scr
        self.xs = ds("xs", [S, D], F32, dbg=True)
        self.wqkv_s = [ds("wqkv_s%d" % l, [128, NK, 3 * D], BF16) for l in range(2)]
        self.wo_s = [ds("wo_s%d" % l, [128, NK, D], BF16) for l in range(2)]
        self.wup_s = [ds("wup_s%d" % l, [128, NFF, 2, NK, 128], BF16) for l in range(2)]
        self.wdn_s = [ds("wdn_s%d" % l, [128, NFF, D], BF16) for l in range(2)]
        self.qaug_s = ds("qaug_s", [8, 80, S], BF16, dbg=True)
        self.kaug_s = ds("kaug_s", [8, 80, S], BF16, dbg=True)
        self.qT_s = ds("qT_s", [8, 128, S], BF16, dbg=True)
        self.kT_s = ds("kT_s", [8, 128, S], BF16, dbg=True)
        self.v_s = ds("v_s", [8, S, 256], BF16, dbg=True)
        self.oT_s = ds("oT_s", [D, S], BF16, dbg=True)
        self.R_x = [sc.res() for _ in range(self.NS)]
        self.R_xs = [sc.res() for _ in range(self.NS)]
        self.R_out = [sc.res() for _ in range(self.NS)]
        self.R_wqkv = [[[sc.res() for _ in range(NK)] for _ in range(3)] for _ in range(2)]
        self.R_wo = [[sc.res() for _ in range(NK)] for _ in range(2)]
        self.R_wup = [[sc.res() for _ in range(NK * 6)] for _ in range(2)]
        self.R_wdn = [[sc.res() for _ in range(NFF)] for _ in range(2)]
        self.R_qaug = [sc.res() for _ in range(8)]
        self.R_kaug = [sc.res() for _ in range(8)]
        self.R_qT = [sc.res() for _ in range(8)]
        self.R_kT = [sc.res() for _ in range(8)]
        self.R_v = [sc.res() for _ in range(8)]
        self.R_oT = [sc.res() for _ in range(8)]

    def setup_constants(self):
        sc, top = self.sc, self.top
        sb = self.sb
        self.ident = sb(top, "ident", [128, 128], BF16)
        self.rperm = sb(top, "rperm", [128, 128], BF16)
        self.gin = sb(top, "gin", [128, 16], F32)
        self.gffn = sb(top, "gffn", [128, 16], F32)
        self.subl = sb(top, "subl", [128, 1], F32)
        self.cw = sb(top, "cw", [128, 2 * 3 * 44], F32)
        self.cb = sb(top, "cb", [128, 2 * 44], F32)
        self.lamneg = sb(top, "lamneg", [128, 1], F32)
        lq = sb(top, "lq", [128, 4, 64], F32)
        lp = sb(top, "lp", [128, 2, 64], F32)
        ls = sb(top, "ls", [128, 2], F32)
        le = sb(top, "le", [128, 2], F32)
        self.R_const = sc.res()
        Rc = self.R_const
        dsm = sc.dsem("const")
        for dst, src in ((self.ident, self.ident_d), (self.rperm, self.rperm_d), (self.gin, self.gin_d),
                         (self.gffn, self.gffn_d), (self.subl, self.subl_d), (self.cw, self.cw_d),
                         (self.cb, self.cb_d)):
            sc.dma("sp", dst[:], src[:, :], dsm, writes=[Rc])
        for i in range(4):
            sc.dma("sp", lq[:, i, :], self.lamq_d[i:i + 1, :].partition_broadcast(128), dsm, writes=[Rc])
        for h in range(8):
            sc.dma("sp", self.kaug_s[h, 64:80, :], self.erows_d[:, :], sc.dsem("erow"), writes=[self.R_kaug[h]])
        Rl = sc.res()
        sc.op("dve", lambda e: e.tensor_tensor(out=lp[:, 0, :], in0=lq[:, 0, :], in1=lq[:, 1, :], op=ALU.mult), reads=[Rc], writes=[Rl])
        sc.op("dve", lambda e: e.tensor_tensor(out=lp[:, 1, :], in0=lq[:, 2, :], in1=lq[:, 3, :], op=ALU.mult), reads=[Rc, Rl], writes=[Rl])
        sc.op("dve", lambda e: e.tensor_reduce(out=ls[:, :], in_=lp[:, :, :], axis=AX.X, op=ALU.add), reads=[Rl], writes=[Rl])
        sc.op("act", lambda e: e.activation(out=le[:, :], in_=ls[:, :], func=AF.Exp), reads=[Rl], writes=[Rl])
        sc.op("dve", lambda e: e.tensor_tensor(out=ls[:, 0:1], in0=le[:, 1:2], in1=le[:, 0:1], op=ALU.subtract), reads=[Rl], writes=[Rl])
        sc.op("dve", lambda e: e.tensor_scalar(out=self.lamneg[:, :], in0=ls[:, 0:1], scalar1=-LAMBDA_INIT, scalar2=None, op0=ALU.add), reads=[Rl], writes=[Rl])
        self.R_lam = Rl

    def weight_prep(self):
        sc, top = self.sc, self.top
        NB = 3
        stf = [self.sb(top, "stf%d" % i, [128, 1024], F32) for i in range(NB)]
        stb = [self.sb(top, "stb%d" % i, [128, 1024], BF16) for i in range(NB)]
        Rf = [sc.res() for _ in range(NB)]
        Rb = [sc.res() for _ in range(NB)]
        units = []
        for l in range(self.nlayers):
            for k in range(NK):
                for b in range(3):
                    units.append((self.w_in[l][k * 128:(k + 1) * 128, b * 1024:(b + 1) * 1024], 1024,
                                  self.wqkv_s[l][:, k, b * 1024:(b + 1) * 1024], None,
                                  self.gin[:, l * 8 + k:l * 8 + k + 1], 1.0, self.R_wqkv[l][b][k]))
            for k in range(NK):
                s1 = self.subl[:, 0:1] if l == 1 else 1.0
                s2 = (1.0 - LAMBDA_INIT) if l == 1 else 1.0
                units.append((self.w_out[l][k * 128:(k + 1) * 128, :], 1024, self.wo_s[l][:, k, :], None,
                              s1, s2, self.R_wo[l][k]))
            for k in range(NK):
                ci = 0
                for isval in range(2):
                    for (c0, n) in ((0, 1024), (1024, 1024), (2048, 768)):
                        col = isval * DFF + c0
                        pj0 = c0 // 128
                        dst = self.wup_s[l][:, pj0:pj0 + n // 128, isval, k, :]
                        units.append((self.w_up[l][k * 128:(k + 1) * 128, col:col + n], n, dst, n // 128,
                                      self.gffn[:, l * 8 + k:l * 8 + k + 1], 1.0, self.R_wup[l][k * 6 + ci]))
                        ci += 1
            for j in range(NFF):
                units.append((self.w_dn[l][j * 128:(j + 1) * 128, :], 1024, self.wdn_s[l][:, j, :], None,
                              1.0, 1.0, self.R_wdn[l][j]))
        for i, (src, n, dst, nchunk, s1, s2, Rw) in enumerate(units):
            b = i % NB
            sc.dma("pool", stf[b][:, 0:n], src, sc.dsem("pl%d" % b, main=False), writes=[Rf[b]])
            rd = [Rf[b], self.R_const]
            sc.op("pool", lambda e, b=b, n=n, s1=s1, s2=s2: e.tensor_scalar(
                out=stb[b][:, 0:n], in0=stf[b][:, 0:n], scalar1=s1, scalar2=s2, op0=ALU.mult, op1=ALU.mult),
                reads=rd, writes=[Rb[b]])
            if nchunk is None:
                srcb = stb[b][:, 0:n]
            else:
                srcb = stb[b][:, 0:n].rearrange("p (c w) -> p c w", w=128)
            sc.dma("pool", dst, srcb, sc.dsem("ps%d" % b, main=False), reads=[Rb[b]], writes=[Rw])

    def phase_norm(self, l, xsrc, hT, R_hT):
        sc = self.sc
        R_src = self.R_x if l == 0 else self.R_xs
        with ExitStack() as st:
            xin = [self.sb(st, "p1x%d" % i, [128, 4, D], F32) for i in range(2)]
            hb = [self.sb(st, "p1h%d" % i, [128, D], BF16) for i in range(2)]
            junk = self.sb(st, "p1j", [128, D], BF16)
            ssq = self.sb(st, "p1ss", [128, self.NS], F32)
            rt = self.sb(st, "p1rt", [128, self.NS], F32)
            rstd = self.sb(st, "p1rs", [128, self.NS], F32)
            psT = [self.ps(st, "p1T%d" % i, [128, NK, 128], BF16) for i in range(2)]
            R_xin = [sc.res() for _ in range(2)]
            R_hb = [sc.res() for _ in range(2)]
            R_j = sc.res()
            R_st = [sc.res() for _ in range(self.NT)]
            R_psT = [sc.pres() for _ in range(2)]
            for g in range(self.NT):
                b = g % 2
                sc.dma("sp", xin[b][:, :, :], xsrc[g * 512:(g + 1) * 512, :].rearrange("(s p) d -> p s d", p=128),
                       sc.dsem("p1x%d" % b), reads=R_src[g * 4:(g + 1) * 4], writes=[R_xin[b]])
                for s in range(4):
                    sub = g * 4 + s
                    sc.op("act", lambda e, b=b, s=s, sub=sub: e.activation(
                        out=junk[:, :], in_=xin[b][:, s, :], func=AF.Square, accum_out=ssq[:, sub:sub + 1]),
                        reads=[R_xin[b]], writes=[R_j, R_st[g]])
                sc.op("act", lambda e, g=g: e.activation(out=rt[:, g * 4:(g + 1) * 4], in_=ssq[:, g * 4:(g + 1) * 4],
                                                         func=AF.Sqrt, scale=1.0 / D, bias=1e-6),
                      reads=[R_st[g]], writes=[R_st[g]])
                sc.op("dve", lambda e, g=g: e.reciprocal(out=rstd[:, g * 4:(g + 1) * 4], in_=rt[:, g * 4:(g + 1) * 4]),
                      reads=[R_st[g]], writes=[R_st[g]])
                for s in range(4):
                    sub = g * 4 + s
                    hbi = sub % 2
                    sc.op("dve", lambda e, b=b, s=s, sub=sub, hbi=hbi: e.tensor_scalar(
                        out=hb[hbi][:, :], in0=xin[b][:, s, :], scalar1=rstd[:, sub:sub + 1], scalar2=None, op0=ALU.mult),
                        reads=[R_xin[b], R_st[g]], writes=[R_hb[hbi]])
                    for k in range(NK):
                        sc.op("pe", lambda e, hbi=hbi, k=k: e.transpose(
                            out=psT[hbi][:, k, :], in_=hb[hbi][:, k * 128:(k + 1) * 128], identity=self.ident[:, :]),
                            reads=[R_hb[hbi], self.R_const], writes=[R_psT[hbi]])
                    sc.op("act", lambda e, hbi=hbi, sub=sub: e.activation(
                        out=hT[:, :, sub * 128:(sub + 1) * 128], in_=psT[hbi][:, :, :], func=AF.Copy),
                        reads=[R_psT[hbi]], writes=[R_hT[g]])
            sc.barrier()

    def phase_proj(self, l, hT, R_hT):
        sc = self.sc
        S, NT, NS = self.S, self.NT, self.NS
        with ExitStack() as st:
            cosT = self.sb(st, "p2cos", [128, S], F32)
            sinT = self.sb(st, "p2sin", [128, S], F32)
            R_tab = sc.res()
            sc.dma("sp", cosT[:, :], self.cos_d[:, :], sc.dsem("p2tab"), writes=[R_tab])
            sc.dma("sp", sinT[:, :], self.sin_d[:, :], sc.dsem("p2tab"), writes=[R_tab])
            NW = 4
            wch = [self.sb(st, "p2w%d" % i, [128, NK, 128], BF16) for i in range(NW)]
            R_w = [sc.res() for _ in range(NW)]
            qb = [self.sb(st, "p2qb%d" % i, [128, 512], BF16) for i in range(2)]
            R_qb = [sc.res() for _ in range(2)]
            Af = [self.sb(st, "p2A%d" % i, [128, 512], F32) for i in range(2)]
            Bf = [self.sb(st, "p2B%d" % i, [128, 512], F32) for i in range(2)]
            R_A = [sc.res() for _ in range(2)]
            R_B = [sc.res() for _ in range(2)]
            tmpf = [self.sb(st, "p2t%d" % i, [128, 512], F32) for i in range(2)]
            R_tmp = [sc.res() for _ in range(2)]
            stg = [self.sb(st, "p2stg%d" % i, [128, S], BF16) for i in range(2)]
            R_stg = [sc.res() for _ in range(2)]
            vsb = [self.sb(st, "p2v%d" % i, [128, NS, 256 if l == 0 else 130], BF16) for i in range(2)]
            R_vsb = [sc.res() for _ in range(2)]
            ksum = self.sb(st, "p2ks", [128, 16], F32)
            R_ks = sc.res()
            kpad = self.sb(st, "p2kpad", [128, 2, 16], F32)
            R_kpad = sc.res()
            sc.op("dve", lambda e: e.memset(kpad[:, :, :], 0.0), writes=[R_kpad])
            gate = self.sb(st, "p2g", [128, 4, 2, 16], F32)
            top8 = self.sb(st, "p2t8", [128, 4, 2, 8], F32)
            nmp = self.sb(st, "p2nm", [128, 4, 128], BF16)
            sc.op("dve", lambda e: e.memset(nmp[:, :, :], 0.0), writes=[sc.res()])
            nm = nmp[:, :, 0:32].rearrange("p s (h n) -> p s h n", h=2)
            R_gate = sc.res()
            R_nm = sc.res()
            R_t8 = [[sc.res() for _ in range(2)] for _ in range(4)]
            R_nm2 = [[sc.res() for _ in range(2)] for _ in range(4)]
            nmT = self.sb(st, "p2nmT", [32, S], BF16)
            R_nmT = sc.res()
            ps1 = [self.ps(st, "p2ps1_%d" % i, [128, 512], F32) for i in range(2)]
            ps2 = [self.ps(st, "p2ps2_%d" % i, [128, 512], F32) for i in range(2)]
            psV = [self.ps(st, "p2psV%d" % i, [128, 4, 128], F32) for i in range(2)]
            psG = self.ps(st, "p2psG", [128, 512], F32)
            psN = self.ps(st, "p2psN", [128, 512], F32)
            R_ps1 = [sc.pres() for _ in range(2)]
            R_ps2 = [sc.pres() for _ in range(2)]
            R_psV = [sc.pres() for _ in range(2)]
            R_psG = sc.pres()
            R_psN = sc.pres()
            for i in range(2):
                if l == 0:
                    v4 = vsb[i][:, :, :].rearrange("p s (h e) -> p s h e", h=2)
                    sc.op("dve", lambda e, v4=v4: e.memset(v4[:, :, :, 64:128], 1.0), writes=[R_vsb[i]])
                else:
                    sc.op("dve", lambda e, i=i: e.memset(vsb[i][:, :, 128:129], 1.0), writes=[R_vsb[i]])

            wctr = [0]
            tctr = [0]
            sctr = [0]
            import os as _os
            dbg = _os.environ.get("KDBG", "")
            if "p2a" in dbg:
                return
            if "waitpool" in dbg:
                deps = {sc.ekey("pool"): sc.cnt["pool"]}
                for d_ in sc.dsems.values():
                    if d_.count > 0:
                        deps[d_.key] = d_.count
                for e_ in ("pe", "act", "dve", "sp"):
                    sc._wait(e_, deps)

            def load_w(blk, c):
                i = wctr[0] % NW
                wctr[0] += 1
                col = blk * 1024 + c * 128
                sc.dma("sp", wch[i][:, :, :], self.wqkv_s[l][:, :, col:col + 128], sc.dsem("p2w%d" % i),
                       reads=self.R_wqkv[l][blk], writes=[R_w[i]])
                return i

            def qk_chunk(blk, c, moba, wi):
                si = sctr[0] % 2
                sctr[0] += 1
                if moba and blk == 0:
                    sc.op("dve", lambda e: e.memset(gate[:, :, :, :], NEG), writes=[R_gate])
                    sc.op("dve", lambda e: e.tensor_copy(out=kpad[0:64, 0, :], in_=ksum[0:64, :]), reads=[R_ks], writes=[R_kpad])
                    sc.op("dve", lambda e: e.tensor_copy(out=kpad[64:128, 1, :], in_=ksum[64:128, :]), reads=[R_ks], writes=[R_kpad])
                if moba and blk == 1:
                    sc.op("dve", lambda e: e.memset(ksum[:, :], 0.0), writes=[R_ks])
                def stA(t):
                    i = t % 2
                    cols = slice(t * 512, (t + 1) * 512)
                    for k in range(NK):
                        sc.op("pe", lambda e, i=i, k=k, wi=wi, cols=cols: e.matmul(
                            ps1[i][:, :], lhsT=wch[wi][:, k, :], rhs=hT[:, k, cols], start=(k == 0), stop=(k == NK - 1)),
                            reads=[R_w[wi], R_hT[t]], writes=[R_ps1[i]])
                    sc.op("act", lambda e, i=i: e.activation(out=qb[i][:, :], in_=ps1[i][:, :], func=AF.Copy),
                          reads=[R_ps1[i]], writes=[R_qb[i]])

                def stB(t):
                    i = t % 2
                    cols = slice(t * 512, (t + 1) * 512)
                    sc.op("pe", lambda e, i=i: e.matmul(ps2[i][:, :], lhsT=self.rperm[:, :], rhs=qb[i][:, :], start=True, stop=True),
                          reads=[R_qb[i], self.R_const], writes=[R_ps2[i]])
                    sc.op("dve", lambda e, i=i, cols=cols: e.tensor_tensor(out=Af[i][:, :], in0=ps1[i][:, :], in1=cosT[:, cols], op=ALU.mult),
                          reads=[R_ps1[i], R_tab], writes=[R_A[i]])
                    sc.op("dve", lambda e, i=i, cols=cols: e.tensor_tensor(out=Bf[i][:, :], in0=ps2[i][:, :], in1=sinT[:, cols], op=ALU.mult),
                          reads=[R_ps2[i], R_tab], writes=[R_B[i]])
                    if not moba:
                        sc.op("dve", lambda e, i=i, cols=cols: e.tensor_tensor(out=stg[si][:, cols], in0=Af[i][:, :], in1=Bf[i][:, :], op=ALU.add),
                              reads=[R_A[i], R_B[i]], writes=[R_stg[si]])
                        return
                    sc.op("dve", lambda e, i=i: e.tensor_tensor(out=tmpf[i][:, :], in0=Af[i][:, :], in1=Bf[i][:, :], op=ALU.add),
                          reads=[R_A[i], R_B[i]], writes=[R_tmp[i]])
                    sc.op("act", lambda e, i=i, cols=cols: e.activation(out=stg[si][:, cols], in_=tmpf[i][:, :], func=AF.Copy),
                          reads=[R_tmp[i]], writes=[R_stg[si]])
                    if blk == 1:
                        sc.op("dve", lambda e, i=i, t=t: e.tensor_reduce(
                            out=ksum[:, 2 * t:2 * t + 2], in_=tmpf[i][:, :].rearrange("p (b w) -> p b w", w=256), axis=AX.X, op=ALU.add),
                            reads=[R_tmp[i]], writes=[R_ks])

                def stC(t):
                    i = t % 2
                    for s_ in range(4):
                        for h in range(2):
                            sc.op("pe", lambda e, i=i, s_=s_, h=h: e.matmul(
                                psG[:, (s_ * 2 + h) * 16:(s_ * 2 + h) * 16 + 16], lhsT=tmpf[i][:, s_ * 128:(s_ + 1) * 128],
                                rhs=kpad[:, h, :], start=True, stop=True),
                                reads=[R_tmp[i], R_kpad], writes=[R_psG])
                    psG4 = psG[:, 0:128].rearrange("p (s h n) -> p s h n", s=4, h=2)
                    sc.op("dve", lambda e: e.memset(nm[:, :, :, :], -1.0),
                          writes=[R_nm] + [R_nm2[a][b_] for a in range(4) for b_ in range(2)])
                    for half in range(2):
                        b = 2 * t + half
                        ss = slice(2 * half, 2 * half + 2)
                        sc.op("dve", lambda e, ss=ss, b=b: e.memset(nm[:, ss, :, b:b + 1], 0.0), writes=[R_nm])
                        if b > 0:
                            sc.op("dve", lambda e, ss=ss, b=b, psG4=psG4: e.tensor_copy(out=gate[:, ss, :, 0:b], in_=psG4[:, ss, :, 0:b]),
                                  reads=[R_psG], writes=[R_gate])
                    for s_ in range(4):
                        b = 2 * t + s_ // 2
                        if b == 0:
                            continue
                        for h in range(2):
                            sc.op("dve", lambda e, s_=s_, h=h: e.max(out=top8[:, s_, h, :], in_=gate[:, s_, h, :]),
                                  reads=[R_gate], writes=[R_t8[s_][h]])
                            sc.op("dve", lambda e, s_=s_, h=h, b=b: e.tensor_scalar(
                                out=nm[:, s_, h, 0:b], in0=gate[:, s_, h, 0:b], scalar1=top8[:, s_, h, 2:3], scalar2=1.0,
                                op0=ALU.is_ge, op1=ALU.subtract), reads=[R_gate, R_t8[s_][h], R_nm], writes=[R_nm2[s_][h]])

                def stD(t):
                    cols = slice(t * 512, (t + 1) * 512)
                    allnm = [R_nm] + [R_nm2[a][b_] for a in range(4) for b_ in range(2)]
                    for s_ in range(4):
                        sc.op("pe", lambda e, s_=s_: e.matmul(
                            psN[:, s_ * 128:(s_ + 1) * 128], lhsT=nmp[:, s_, :], rhs=self.ident[:, :],
                            start=True, stop=True), reads=allnm + [self.R_const], writes=[R_psN])
                    sc.op("act", lambda e, cols=cols: e.activation(out=nmT[:, cols], in_=psN[0:32, :], func=AF.Copy),
                          reads=[R_psN], writes=[R_nmT])

                if moba and "mobaseq" in dbg:
                    stages = [(stA, 0), (stB, 0)]
                    if blk == 0:
                        stages += [(stC, 0), (stD, 0)]
                else:
                    stages = [(stA, 0), (stB, 1)]
                    if moba and blk == 0:
                        stages += [(stC, 2), (stD, 3)]
                for step in range(NT + len(stages) - 1):
                    for fn_, lag in reversed(stages):
                        t = step - lag
                        if 0 <= t < NT:
                            fn_(t)
                if "nostore" in dbg:
                    return
                ds_ = sc.dsem("p2st%d" % si)
                if moba:
                    dstt = self.qaug_s if blk == 0 else self.kaug_s
                    Rd = self.R_qaug if blk == 0 else self.R_kaug
                    for h in range(2):
                        sc.dma("sp", dstt[2 * c + h, 0:64, :], stg[si][h * 64:(h + 1) * 64, :], ds_, reads=[R_stg[si]], writes=[Rd[2 * c + h]])
                    if blk == 0:
                        for h in range(2):
                            sc.dma("sp", self.qaug_s[2 * c + h, 64:80, :], nmT[h * 16:(h + 1) * 16, :], sc.dsem("p2nm"),
                                   reads=[R_nmT], writes=[self.R_qaug[2 * c + h]])
                else:
                    dstt = self.qT_s if blk == 0 else self.kT_s
                    Rd = self.R_qT if blk == 0 else self.R_kT
                    sc.dma("sp", dstt[c, :, :], stg[si][:, :], ds_, reads=[R_stg[si]], writes=[Rd[c]])

            vctr = [0]

            def v_chunk(c, wi):
                vi = vctr[0] % 2
                vctr[0] += 1
                for g in range(NS // 4):
                    pi = g % 2
                    for s in range(4):
                        sub = g * 4 + s
                        for k in range(NK):
                            sc.op("pe", lambda e, pi=pi, s=s, k=k, sub=sub, wi=wi: e.matmul(
                                psV[pi][:, s, :], lhsT=hT[:, k, sub * 128:(sub + 1) * 128], rhs=wch[wi][:, k, :],
                                start=(k == 0), stop=(k == NK - 1)),
                                reads=[R_hT[sub // 4], R_w[wi]], writes=[R_psV[pi]])
                    if l == 0:
                        dst = vsb[vi][:, g * 4:(g + 1) * 4, :].rearrange("p s (h e) -> p s h e", h=2)[:, :, :, 0:64]
                        src = psV[pi][:, :, :].rearrange("p s (h d) -> p s h d", h=2)
                    else:
                        dst = vsb[vi][:, g * 4:(g + 1) * 4, 0:128]
                        src = psV[pi][:, :, :]
                    sc.op("act", lambda e, dst=dst, src=src: e.activation(out=dst, in_=src, func=AF.Copy),
                          reads=[R_psV[pi]], writes=[R_vsb[vi]])
                vd = self.v_s[c].rearrange("(s p) e -> p s e", p=128)
                ve = 256 if l == 0 else 129
                for q4 in range(0, NS, 8):
                    sc.dma("sp", vd[:, q4:q4 + 8, 0:ve], vsb[vi][:, q4:q4 + 8, 0:ve], sc.dsem("p2vst%d" % vi),
                           reads=[R_vsb[vi]], writes=[self.R_v[c]])

            tasks = []
            for c in range(8):
                tasks += [(1, c), (0, c), (2, c)]
            wi_of = {}
            for i_ in range(3):
                wi_of[i_] = load_w(*tasks[i_])
            for i_, (blk_, c) in enumerate(tasks):
                if i_ + 3 < len(tasks):
                    wi_of[i_ + 3] = load_w(*tasks[i_ + 3])
                moba = (l == 0 and c < 4)
                if blk_ == 2:
                    v_chunk(c, wi_of[i_])
                else:
                    qk_chunk(blk_, c, moba, wi_of[i_])

    def phase_attn(self, l):
        sc = self.sc
        S, NT, NS = self.S, self.NT, self.NS
        with ExitStack() as st:
            NSET = 2
            qa = [self.sb(st, "p3qa%d" % i, [128, S], BF16) for i in range(NSET)]
            ka = [self.sb(st, "p3ka%d" % i, [128, S], BF16) for i in range(NSET)]
            if l == 0:
                qbt = [self.sb(st, "p3qb%d" % i, [128, S], BF16) for i in range(NSET)]
            kbt = [self.sb(st, "p3kb%d" % i, [128, S], BF16) for i in range(NSET)]
            vt = [self.sb(st, "p3v%d" % i, [128, NS, 256 if l == 0 else 130], BF16) for i in range(NSET)]
            R_in = [sc.res() for _ in range(NSET)]
            if l == 1:
                osb = [self.sb(st, "p3o%d" % i, [128, NS, 128], BF16) for i in range(2)]
            R_osb = [sc.res() for _ in range(2)]
            otsb = [self.sb(st, "p3ot%d" % i, [128, S], BF16) for i in range(2)]
            R_ot = [sc.res() for _ in range(2)]
            NPT = 5
            pt = [self.sb(st, "p3pt%d" % i, [128, 512], BF16) for i in range(NPT)]
            R_pt = [sc.res() for _ in range(NPT)]
            tri = self.sb(st, "p3tri", [128, 128], BF16)
            R_msk = sc.res()
            sc.dma("sp", tri[:, :], self.tri_d[:, :], sc.dsem("p3msk"), writes=[R_msk])
            if l == 0:
                strip = self.sb(st, "p3strip", [128, 2944], BF16)
                sc.dma("sp", strip[:, :], self.stripd_d[:, :], sc.dsem("p3msk"), writes=[R_msk])
                rec = [self.sb(st, "p3rec%d" % i, [64, 512], F32) for i in range(2)]
                R_rec = [sc.res() for _ in range(2)]
            else:
                rec1 = self.sb(st, "p3r1", [128, 2, 2, 1], F32)
                rec2 = self.sb(st, "p3r2", [128, 2, 2, 1], F32)
                tb = [self.sb(st, "p3tb%d" % i, [128, 128], F32) for i in range(2)]
                R_tb = [sc.res() for _ in range(2)]
                ob = [self.sb(st, "p3ob%d" % i, [128, 4, 128], F32) for i in range(2)]
                R_ob = [sc.res() for _ in range(2)]
                junk = self.sb(st, "p3junk", [128, 128], BF16)
                R_junk = sc.res()
                ssd = self.sb(st, "p3ssd", [128, 4], F32)
                rtd = self.sb(st, "p3rtd", [128, 4], F32)
                rsd = self.sb(st, "p3rsd", [128, 4], F32)
                R_fin = sc.res()
            NSB = 3
            psS = [self.ps(st, "p3S%d" % i, [128, 512], F32) for i in range(NSB)]
            R_S = [sc.pres() for _ in range(NSB)]
            psO = [self.ps(st, "p3O%d" % i, [128, 512], F32) for i in range(4)]
            R_O = [sc.pres() for _ in range(4)]
            psT = self.ps(st, "p3T", [128, 8, 128], BF16)
            R_T = sc.pres()

            zeroed = set()

            def load_chunk(c):
                i = c % NSET
                R = R_in[i]
                dq = sc.dsem("p3in%d" % i)
                if l == 0 and c < 4:
                    sc.dma("sp", qa[i][0:80, :], self.qaug_s[2 * c, :, :], dq, reads=[self.R_qaug[2 * c]], writes=[R])
                    sc.dma("sp", qbt[i][0:80, :], self.qaug_s[2 * c + 1, :, :], dq, reads=[self.R_qaug[2 * c + 1]], writes=[R])
                    sc.dma("sp", ka[i][0:80, :], self.kaug_s[2 * c, :, :], dq, reads=[self.R_kaug[2 * c]], writes=[R])
                    sc.dma("sp", kbt[i][0:80, :], self.kaug_s[2 * c + 1, :, :], dq, reads=[self.R_kaug[2 * c + 1]], writes=[R])
                else:
                    if i not in zeroed:
                        zeroed.add(i)
                        sc.op("dve", lambda e, i=i: e.memset(ka[i][64:128, :], 0.0), writes=[R])
                        sc.op("dve", lambda e, i=i: e.memset(kbt[i][0:64, :], 0.0), writes=[R])
                    sc.dma("sp", qa[i][:, :], self.qT_s[c, :, :], dq, reads=[self.R_qT[c]], writes=[R])
                    sc.dma("sp", ka[i][0:64, :], self.kT_s[c, 0:64, :], dq, reads=[self.R_kT[c]], writes=[R])
                    sc.dma("sp", kbt[i][64:128, :], self.kT_s[c, 64:128, :], dq, reads=[self.R_kT[c]], writes=[R])
                vd = self.v_s[c].rearrange("(s p) e -> p s e", p=128)
                ve = 256 if l == 0 else 129
                for q4 in range(0, NS, 8):
                    sc.dma("sp", vt[i][:, q4:q4 + 8, 0:ve], vd[:, q4:q4 + 8, 0:ve], dq, reads=[self.R_v[c]], writes=[R])

            sctr = [0]
            pctr = [0]
            mctr = [0]

            def run_chunk(c):
                i = c % NSET
                oi = c % 2
                Rin = R_in[i]
                if l == 0 and c < 4:
                    kind = "moba"
                    maps = [dict(q=qa[i], k=ka[i], r0=0, r1=80, h=0), dict(q=qbt[i], k=kbt[i], r0=0, r1=80, h=1)]
                elif l == 0:
                    kind = "dil"
                    maps = [dict(q=qa[i], k=ka[i], r0=0, r1=128, h=0), dict(q=qa[i], k=kbt[i], r0=0, r1=128, h=1)]
                else:
                    kind = "causal"
                    maps = [dict(q=qa[i], k=ka[i], r0=0, r1=128, h=0), dict(q=qa[i], k=kbt[i], r0=0, r1=128, h=1)]
                if l == 0:
                    v4 = vt[i][:, :, :].rearrange("p s (h e) -> p s h e", h=2)
                steps = []
                for qt in range(NT):
                    for m in range(2):
                        if kind == "dil":
                            kts = list(range(max(0, 4 * qt - 16), 4 * qt + 4))
                        else:
                            kts = list(range(0, 4 * qt + 4))
                        for kt in kts:
                            steps.append((qt, m, kt, kt == kts[0], kt == kts[-1]))
                started = {}
                pendq = []
                LAG = 2

                def emit_pv(stp):
                    qt, m, kt, first, last, pi, lo, hi = stp
                    mp = maps[m]
                    if l == 0:
                        bank = m + 2 * (qt % 2)
                        key = (bank, qt, m)
                        stt = key not in started
                        started[key] = True
                        sc.op("pe", lambda e, bank=bank, kt=kt, h=mp["h"], pi=pi, lo=lo, hi=hi, stt=stt, last=last: e.matmul(
                            psO[bank][:, lo:hi], lhsT=v4[:, kt, h, :], rhs=pt[pi][:, lo:hi], start=stt, stop=last, skip_group_check=True),
                            reads=[R_pt[pi], Rin], writes=[R_O[bank]])
                        if last:
                            finalize(qt, m)
                        return
                    for qs in range(lo // 128, hi // 128):
                        if l == 0:
                            bank = m + 2 * (qt % 2)
                            outap = psO[bank][:, qs * 65:(qs + 1) * 65]
                            rhs = v4[:, kt, mp["h"], :]
                        else:
                            bank = 2 * m + qs // 2
                            outap = psO[bank][:, (qs % 2) * 129:(qs % 2) * 129 + 129]
                            rhs = vt[i][:, kt, 0:129]
                        key = (bank, qt, m)
                        stt = key not in started
                        started[key] = True
                        sc.op("pe", lambda e, outap=outap, rhs=rhs, pi=pi, qs=qs, stt=stt, last=last: e.matmul(
                            outap, lhsT=pt[pi][:, qs * 128:(qs + 1) * 128], rhs=rhs, start=stt, stop=last, skip_group_check=True),
                            reads=[R_pt[pi], Rin], writes=[R_O[bank]])
                    if last:
                        finalize(qt, m)

                def finalize(qt, m):
                    if l == 0:
                        bank = m + 2 * (qt % 2)
                        ri = (qt * 2 + m) % 2
                        sc.op("dve", lambda e, bank=bank, ri=ri: e.reciprocal(out=rec[ri][0:64, :], in_=psO[bank][64:128, :]),
                              reads=[R_O[bank]], writes=[R_rec[ri]])
                        sc.op("dve", lambda e, bank=bank, ri=ri, m=m, qt=qt: e.tensor_tensor(
                            out=otsb[oi][m * 64:(m + 1) * 64, qt * 512:(qt + 1) * 512], in0=psO[bank][0:64, :], in1=rec[ri][0:64, :],
                            op=ALU.mult), reads=[R_O[bank], R_rec[ri]], writes=[R_ot[oi]])
                    else:
                        if m == 0:
                            return
                        for bb in range(2):
                            O1 = psO[bb][:, 0:258].rearrange("p (q e) -> p q e", e=129)
                            O2 = psO[2 + bb][:, 0:258].rearrange("p (q e) -> p q e", e=129)
                            sc.op("dve", lambda e, O1=O1, bb=bb: e.reciprocal(out=rec1[:, bb, :, :], in_=O1[:, :, 128:129]),
                                  reads=[R_O[bb]], writes=[R_fin])
                            sc.op("dve", lambda e, O2=O2, bb=bb: e.reciprocal(out=rec2[:, bb, :, :], in_=O2[:, :, 128:129]),
                                  reads=[R_O[2 + bb]], writes=[R_fin])
                        sc.op("dve", lambda e: e.tensor_scalar(out=rec2[:, :, :, :], in0=rec2[:, :, :, :], scalar1=self.lamneg[:, 0:1],
                                                               scalar2=None, op0=ALU.mult), reads=[R_fin, self.R_lam], writes=[R_fin])
                        obi = qt % 2
                        for qs in range(4):
                            bb, ii = qs // 2, qs % 2
                            O1 = psO[bb][:, 0:258].rearrange("p (q e) -> p q e", e=129)
                            O2 = psO[2 + bb][:, 0:258].rearrange("p (q e) -> p q e", e=129)
                            ti = qs % 2
                            sc.op("act", lambda e, O2=O2, ii=ii, bb=bb, ti=ti: e.activation(
                                out=tb[ti][:, :], in_=O2[:, ii, 0:128], func=AF.Copy, scale=rec2[:, bb, ii, 0:1]),
                                reads=[R_O[2 + bb], R_fin], writes=[R_tb[ti]])
                            sc.op("dve", lambda e, O1=O1, ii=ii, bb=bb, ti=ti, obi=obi, qs=qs: e.scalar_tensor_tensor(
                                out=ob[obi][:, qs, :], in0=O1[:, ii, 0:128], scalar=rec1[:, bb, ii, 0:1], in1=tb[ti][:, :],
                                op0=ALU.mult, op1=ALU.add), reads=[R_O[bb], R_fin, R_tb[ti]], writes=[R_ob[obi]])
                            sc.op("act", lambda e, obi=obi, qs=qs: e.activation(
                                out=junk[:, :], in_=ob[obi][:, qs, :], func=AF.Square, accum_out=ssd[:, qs:qs + 1]),
                                reads=[R_ob[obi]], writes=[R_junk, R_fin])
                        sc.op("act", lambda e: e.activation(out=rtd[:, :], in_=ssd[:, :], func=AF.Sqrt, scale=1.0 / 128, bias=1e-5),
                              reads=[R_fin], writes=[R_fin])
                        sc.op("dve", lambda e: e.reciprocal(out=rsd[:, :], in_=rtd[:, :]), reads=[R_fin], writes=[R_fin])
                        for qs in range(4):
                            sub = qt * 4 + qs
                            sc.op("dve", lambda e, obi=obi, qs=qs, sub=sub: e.tensor_scalar(
                                out=osb[oi][:, sub, :], in0=ob[obi][:, qs, :], scalar1=rsd[:, qs:qs + 1], scalar2=None, op0=ALU.mult),
                                reads=[R_ob[obi], R_fin], writes=[R_osb[oi]])
                        transposes(qt)

                def transposes(qt):
                    for qs in range(4):
                        sub = qt * 4 + qs
                        sc.op("pe", lambda e, qs=qs, sub=sub: e.transpose(out=psT[:, qs, :], in_=osb[oi][:, sub, :], identity=self.ident[:, :]),
                              reads=[R_osb[oi], self.R_const], writes=[R_T])
                    sc.op("act", lambda e, qt=qt: e.activation(
                        out=otsb[oi][:, qt * 512:(qt + 1) * 512].rearrange("p (s w) -> p s w", w=128), in_=psT[:, 0:4, :], func=AF.Copy),
                        reads=[R_T], writes=[R_ot[oi]])

                for (qt, m, kt, first, last) in steps:
                    mp = maps[m]
                    delta = (4 * qt - kt) * 128
                    lo = max(0, -delta)
                    hi = 512
                    if kind == "dil":
                        hi = min(512, 2048 - delta + 128)
                    si = sctr[0] % NSB
                    sctr[0] += 1
                    pi = pctr[0] % NPT
                    pctr[0] += 1
                    r0, r1 = mp["r0"], mp["r1"]
                    sc.op("pe", lambda e, si=si, mp=mp, r0=r0, r1=r1, kt=kt, qt=qt: e.matmul(
                        psS[si][:, :], lhsT=mp["k"][r0:r1, kt * 128:(kt + 1) * 128], rhs=mp["q"][r0:r1, qt * 512:(qt + 1) * 512],
                        start=True, stop=True), reads=[Rin], writes=[R_S[si]])
                    if len(pendq) >= LAG:
                        emit_pv(pendq.pop(0))
                    sc.op("act", lambda e, si=si, pi=pi, lo=lo, hi=hi: e.activation(
                        out=pt[pi][:, lo:hi], in_=psS[si][:, lo:hi], func=AF.Exp, scale=0.125),
                        reads=[R_S[si]], writes=[R_pt[pi]])
                    if kind == "dil":
                        o0 = delta + 384
                        mctr[0] += 1
                        on_pool = False
                        sc.op("pool" if on_pool else "dve", lambda e, pi=pi, lo=lo, hi=hi, o0=o0: e.tensor_tensor(
                            out=pt[pi][:, lo:hi], in0=pt[pi][:, lo:hi], in1=strip[:, o0 + lo:o0 + hi], op=ALU.mult),
                            reads=[R_pt[pi], R_msk], writes=[R_pt[pi]], pool_main=on_pool)
                    elif delta <= 0:
                        sc.op("dve", lambda e, pi=pi, lo=lo: e.tensor_tensor(
                            out=pt[pi][:, lo:lo + 128], in0=pt[pi][:, lo:lo + 128], in1=tri[:, :], op=ALU.mult),
                            reads=[R_pt[pi], R_msk], writes=[R_pt[pi]])
                    pendq.append((qt, m, kt, first, last, pi, lo, hi))
                while pendq:
                    emit_pv(pendq.pop(0))
                sc.dma("sp", self.oT_s[c * 128:(c + 1) * 128, :], otsb[oi][:, :], sc.dsem("p3ost%d" % oi),
                       reads=[R_ot[oi]], writes=[self.R_oT[c]])

            load_chunk(0)
            for c in range(8):
                if c + 1 < 8:
                    load_chunk(c + 1)
                run_chunk(c)

    def phase_ffn(self, l, xsrc, xdst, last):
        sc = self.sc
        S = self.S
        R_src = self.R_x if l == 0 else self.R_xs
        R_dst = self.R_out if last else self.R_xs
        tiles = []
        t0 = 0
        while t0 < S:
            T = min(384, S - t0)
            tiles.append((t0, T))
            t0 += T
        TM = 384
        with ExitStack() as st:
            wo = self.sb(st, "p4wo", [128, NK, D], BF16)
            wdn = self.sb(st, "p4wdn", [128, NFF, D], BF16)
            R_wres = sc.res()
            dw = sc.dsem("p4w")
            for k in range(NK):
                sc.dma("sp", wo[:, k, :], self.wo_s[l][:, k, :], dw, reads=[self.R_wo[l][k]], writes=[R_wres])
            for j in range(NFF):
                sc.dma("sp", wdn[:, j, :], self.wdn_s[l][:, j, :], dw, reads=[self.R_wdn[l][j]], writes=[R_wres])
            NWS = 3
            wup = [self.sb(st, "p4wu%d" % i, [128, 2, 2, NK, 128], BF16) for i in range(NWS)]
            R_wu = [sc.res() for _ in range(NWS)]
            xin = [self.sb(st, "p4x%d" % i, [128, 3, D], F32) for i in range(2)]
            R_x = [sc.res() for _ in range(2)]
            oT = self.sb(st, "p4oT", [128, NK, TM], BF16)
            R_oTt = sc.res()
            hb = [self.sb(st, "p4hb%d" % i, [128, D], BF16) for i in range(2)]
            R_hb = [sc.res() for _ in range(2)]
            hT2 = [self.sb(st, "p4hT%d" % i, [128, NK, TM + 2], BF16) for i in range(2)]
            R_hT2 = [sc.res() for _ in range(2)]
            g = self.sb(st, "p4g", [128, NFF, TM], BF16)
            R_g = sc.res()
            junk = self.sb(st, "p4junk", [128, D], BF16)
            R_junk = sc.res()
            stats = [[self.sb(st, "p4st%d_%d" % (a, b), [128, 4], F32) for b in range(3)] for a in range(2)]
            R_stat = [sc.res() for _ in range(2)]
            t1 = [[self.sb(st, "p4t1_%d%d" % (a, b), [128, TM], F32) for b in range(2)] for a in range(2)]
            R_t1 = [[sc.res() for _ in range(2)] for _ in range(2)]
            t2 = [self.sb(st, "p4t2_%d" % a, [128, TM], F32) for a in range(2)]
            R_t2 = [sc.res() for _ in range(2)]
            sg = [self.sb(st, "p4sg%d" % i, [128, TM], F32) for i in range(2)]
            R_sg = [sc.res() for _ in range(2)]
            R_gj = [sc.res() for _ in range(NFF)]
            if last:
                gfin = self.sb(st, "p4gf", [128, D], F32)
                R_gf = sc.res()
                sc.dma("sp", gfin[:, :], self.gfin_d[0:1, :].partition_broadcast(128), sc.dsem("p4gf"), writes=[R_gf])
            psY = [self.ps(st, "p4Y%d" % i, [128, 512], F32) for i in range(2)]
            R_Y = [sc.pres() for _ in range(2)]
            psU = [[self.ps(st, "p4U%d%d" % (a, b), [128, 512], F32) for b in range(2)] for a in range(2)]
            R_U = [[sc.pres() for _ in range(2)] for _ in range(2)]
            psT = self.ps(st, "p4T", [128, NK, 128], BF16)
            R_T = sc.pres()
            cwl = lambda j_, ch: self.cw[:, (l * 3 + j_) * 44 + ch:(l * 3 + j_) * 44 + ch + 1]
            cbl = lambda ch: self.cb[:, l * 44 + ch:l * 44 + ch + 1]
            wctr = [0]
            uctr = [0]

            def rms_rstd(xb, nsub, eps, si_):
                ssq, rt, rstd = stats[si_]
                for s in range(nsub):
                    sc.op("act", lambda e, s=s, ssq=ssq: e.activation(out=junk[:, :], in_=xin[xb][:, s, :], func=AF.Square,
                                                                      accum_out=ssq[:, s:s + 1]),
                          reads=[R_x[xb]], writes=[R_junk, R_stat[si_]])
                sc.op("act", lambda e, ssq=ssq, rt=rt: e.activation(out=rt[:, 0:nsub], in_=ssq[:, 0:nsub], func=AF.Sqrt, scale=1.0 / D, bias=eps),
                      reads=[R_stat[si_]], writes=[R_stat[si_]])
                sc.op("dve", lambda e, rt=rt, rstd=rstd: e.reciprocal(out=rstd[:, 0:nsub], in_=rt[:, 0:nsub]), reads=[R_stat[si_]], writes=[R_stat[si_]])

            def load_tile(tj):
                t0_, T_ = tiles[tj]
                xb_ = tj % 2
                sc.dma("sp", oT[:, :, 0:T_], self.oT_s[:, t0_:t0_ + T_].rearrange("(k p) s -> p k s", p=128), sc.dsem("p4oT"),
                       reads=self.R_oT, writes=[R_oTt])
                sc.dma("sp", xin[xb_][:, 0:T_ // 128, :], xsrc[t0_:t0_ + T_, :].rearrange("(s p) d -> p s d", p=128), sc.dsem("p4x%d" % xb_),
                       reads=R_src[t0_ // 128:(t0_ + T_) // 128], writes=[R_x[xb_]])

            def prologue_a(tj):
                t0_, T_ = tiles[tj]
                ns_ = T_ // 128
                xb_ = tj % 2
                for s in range(ns_):
                    for hf in range(2):
                        for k in range(NK):
                            sc.op("pe", lambda e, s=s, hf=hf, k=k: e.matmul(
                                psY[hf][:, :], lhsT=oT[:, k, s * 128:(s + 1) * 128], rhs=wo[:, k, hf * 512:(hf + 1) * 512],
                                start=(k == 0), stop=(k == NK - 1)), reads=[R_oTt, R_wres], writes=[R_Y[hf]])
                        sc.op("dve", lambda e, s=s, hf=hf, xb_=xb_: e.tensor_tensor(
                            out=xin[xb_][:, s, hf * 512:(hf + 1) * 512], in0=psY[hf][:, :], in1=xin[xb_][:, s, hf * 512:(hf + 1) * 512], op=ALU.add),
                            reads=[R_Y[hf], R_x[xb_]], writes=[R_x[xb_]])
                rms_rstd(xb_, ns_, 1e-6, 1)

            def prologue_b(tj):
                t0_, T_ = tiles[tj]
                ns_ = T_ // 128
                xb_ = tj % 2
                hi_ = tj % 2
                if tj == 0:
                    sc.op("dve", lambda e: e.memset(hT2[0][:, :, 0:2], 0.0), writes=[R_hT2[0]])
                else:
                    Tp = tiles[tj - 1][1]
                    sc.op("act", lambda e, hi_=hi_, Tp=Tp: e.activation(out=hT2[hi_][:, :, 0:2], in_=hT2[1 - hi_][:, :, Tp:Tp + 2], func=AF.Copy),
                          reads=[R_hT2[1 - hi_]], writes=[R_hT2[hi_]])
                for s in range(ns_):
                    hbi = s % 2
                    sc.op("dve", lambda e, s=s, hbi=hbi, xb_=xb_: e.tensor_scalar(
                        out=hb[hbi][:, :], in0=xin[xb_][:, s, :], scalar1=stats[1][2][:, s:s + 1], scalar2=None, op0=ALU.mult),
                        reads=[R_x[xb_], R_stat[1]], writes=[R_hb[hbi]])
                    for k in range(NK):
                        sc.op("pe", lambda e, hbi=hbi, k=k: e.transpose(out=psT[:, k, :], in_=hb[hbi][:, k * 128:(k + 1) * 128],
                                                                        identity=self.ident[:, :]),
                              reads=[R_hb[hbi], self.R_const], writes=[R_T])
                    sc.op("act", lambda e, s=s, hi_=hi_: e.activation(out=hT2[hi_][:, :, 2 + s * 128:2 + (s + 1) * 128], in_=psT[:, :, :], func=AF.Copy),
                          reads=[R_T], writes=[R_hT2[hi_]])

            load_tile(0)
            prologue_a(0)
            prologue_b(0)
            for ti, (t0, T) in enumerate(tiles):
                nsub = T // 128
                xb = ti % 2
                sub0 = t0 // 128
                hti = ti % 2
                for gi in range(NFF // 2):
                    wi = wctr[0] % NWS
                    wctr[0] += 1
                    sc.dma("sp", wup[wi][:, :, :, :, :], self.wup_s[l][:, 2 * gi:2 * gi + 2, :, :, :], sc.dsem("p4wu%d" % wi),
                           reads=self.R_wup[l], writes=[R_wu[wi]])
                    if gi == 1 and ti + 1 < len(tiles):
                        load_tile(ti + 1)
                    if gi == 4 and ti + 1 < len(tiles):
                        prologue_a(ti + 1)
                    if gi == 8 and ti + 1 < len(tiles):
                        prologue_b(ti + 1)
                    for jj in range(2):
                        j = 2 * gi + jj
                        ui = uctr[0] % 2
                        uctr[0] += 1
                        for gv in range(2):
                            for k in range(NK):
                                sc.op("pe", lambda e, ui=ui, gv=gv, k=k, wi=wi, jj=jj, T=T, hti=hti: e.matmul(
                                    psU[ui][gv][:, 0:T + 2], lhsT=wup[wi][:, jj, gv, k, :], rhs=hT2[hti][:, k, 0:T + 2],
                                    start=(k == 0), stop=(k == NK - 1)), reads=[R_wu[wi], R_hT2[hti]], writes=[R_U[ui][gv]])
                        for gv in range(2):
                            ch = gv * NFF + j
                            U = psU[ui][gv]
                            sc.op("act", lambda e, U=U, ui=ui, gv=gv, ch=ch, T=T: e.activation(
                                out=t1[ui][gv][:, 0:T], in_=U[:, 2:T + 2], func=AF.Identity, scale=cwl(2, ch), bias=cbl(ch)),
                                reads=[R_U[ui][gv], self.R_const], writes=[R_t1[ui][gv]])
                            sc.op("dve", lambda e, U=U, ui=ui, gv=gv, ch=ch, T=T: e.scalar_tensor_tensor(
                                out=t2[gv][:, 0:T], in0=U[:, 1:T + 1], scalar=cwl(1, ch), in1=t1[ui][gv][:, 0:T], op0=ALU.mult, op1=ALU.add),
                                reads=[R_U[ui][gv], R_t1[ui][gv], self.R_const], writes=[R_t2[gv]])
                            sc.op("dve", lambda e, U=U, ui=ui, gv=gv, ch=ch, T=T: e.scalar_tensor_tensor(
                                out=t1[ui][gv][:, 0:T], in0=U[:, 0:T], scalar=cwl(0, ch), in1=t2[gv][:, 0:T], op0=ALU.mult, op1=ALU.add),
                                reads=[R_U[ui][gv], R_t2[gv], self.R_const], writes=[R_t1[ui][gv]])
                        sgi = ui
                        sc.op("act", lambda e, ui=ui, T=T, sgi=sgi: e.activation(out=sg[sgi][:, 0:T], in_=t1[ui][0][:, 0:T], func=AF.Silu),
                              reads=[R_t1[ui][0]], writes=[R_sg[sgi]])
                        sc.op("pool", lambda e, ui=ui, j=j, T=T, sgi=sgi: e.tensor_tensor(out=g[:, j, 0:T], in0=sg[sgi][:, 0:T], in1=t1[ui][1][:, 0:T], op=ALU.mult),
                              reads=[R_sg[sgi], R_t1[ui][1]], writes=[R_gj[j]], pool_main=True)
                for s in range(nsub):
                    for hf in range(2):
                        for j in range(NFF):
                            sc.op("pe", lambda e, s=s, hf=hf, j=j: e.matmul(
                                psY[hf][:, :], lhsT=g[:, j, s * 128:(s + 1) * 128], rhs=wdn[:, j, hf * 512:(hf + 1) * 512],
                                start=(j == 0), stop=(j == NFF - 1)), reads=[R_gj[j], R_wres], writes=[R_Y[hf]])
                        sc.op("dve", lambda e, s=s, hf=hf, xb=xb: e.tensor_tensor(
                            out=xin[xb][:, s, hf * 512:(hf + 1) * 512], in0=psY[hf][:, :], in1=xin[xb][:, s, hf * 512:(hf + 1) * 512], op=ALU.add),
                            reads=[R_Y[hf], R_x[xb]], writes=[R_x[xb]])
                if last:
                    rms_rstd(xb, nsub, 1e-6, 0)
                    for s in range(nsub):
                        sc.op("dve", lambda e, s=s, xb=xb: e.scalar_tensor_tensor(
                            out=xin[xb][:, s, :], in0=xin[xb][:, s, :], scalar=stats[0][2][:, s:s + 1], in1=gfin[:, :], op0=ALU.mult, op1=ALU.mult),
                            reads=[R_x[xb], R_stat[0], R_gf], writes=[R_x[xb]])
                sc.dma("sp", xdst[t0:t0 + T, :].rearrange("(s p) d -> p s d", p=128), xin[xb][:, 0:nsub, :], sc.dsem("p4st%d" % xb),
                       reads=[R_x[xb]], writes=R_dst[sub0:sub0 + nsub])


def build_program(S=SEQ, debug=False, nlayers=2, stop_after=None):
    nc = bass.Bass("TRN2", target_bir_lowering=False)
    b = Builder(nc, S, debug=debug, nlayers=nlayers)
    b.stop_after = stop_after
    b.build()
    return nc, b


def make_in_maps(inputs, S, ncores):
    f = lambda a: np.ascontiguousarray(np.asarray(a, dtype=np.float32))
    consts = host_constants(S)

    def pk(v):
        return np.ascontiguousarray(np.asarray(v, np.float32).reshape(8, 128).T)

    gin_t = np.concatenate([pk(inputs["even_norm"][0]), pk(inputs["odd_norm"][0])], axis=1)
    gffn_t = np.concatenate([pk(inputs["ffn_norm"][0]), pk(inputs["ffn_norm"][1])], axis=1)
    subl_t = f(inputs["odd_subln"][0]).reshape(128, 1)
    lamq = np.stack([f(inputs["odd_lambda_q1"][0]), f(inputs["odd_lambda_k1"][0]),
                     f(inputs["odd_lambda_q2"][0]), f(inputs["odd_lambda_k2"][0])], axis=0)
    cw = f(inputs["ffn_conv_w"])
    cw_t = np.ascontiguousarray(cw.reshape(2, 3, 44, 128).transpose(3, 0, 1, 2).reshape(128, 2 * 3 * 44))
    cb = f(inputs["ffn_conv_b"])
    cb_t = np.ascontiguousarray(cb.reshape(2, 44, 128).transpose(2, 0, 1).reshape(128, 2 * 44))
    shared = dict(
        w_in0=f(inputs["even_w_in"][0]), w_in1=f(inputs["odd_w_qkv"][0]),
        w_out0=f(inputs["even_w_out"][0]), w_out1=f(inputs["odd_w_out"][0]),
        w_up0=f(inputs["ffn_w_up"][0]), w_up1=f(inputs["ffn_w_up"][1]),
        w_dn0=f(inputs["ffn_w_down"][0]), w_dn1=f(inputs["ffn_w_down"][1]),
        gin_t=gin_t, gffn_t=gffn_t, subl_t=subl_t, lamq=lamq, cw_t=cw_t, cb_t=cb_t,
        gfin=f(inputs["final_norm"]).reshape(1, D), **consts)
    x = f(inputs["x"])
    in_maps = []
    for c in range(ncores):
        m = dict(shared)
        m["x"] = np.ascontiguousarray(x[c])
        in_maps.append(m)
    return in_maps


def kernel(**inputs):
    x = np.asarray(inputs["x"])
    Bn, S, _ = x.shape
    nc, _ = build_program(S)
    in_maps = make_in_maps(inputs, S, Bn)
    res = run_bass_kernel_spmd(nc, in_maps, core_ids=list(range(Bn)))
    out = np.stack([np.asarray(r["out"], dtype=np.float32) for r in res.results], axis=0)
    return out
```

```python
import math
from contextlib import ExitStack

import numpy as np
import ml_dtypes

import concourse.bass as bass
import concourse.mybir as mybir
from concourse.bass_utils import run_bass_kernel_spmd

F32 = mybir.dt.float32
BF16 = mybir.dt.bfloat16
AF = mybir.ActivationFunctionType
ALU = mybir.AluOpType
AX = mybir.AxisListType

D = 1024
DFF = 2816
NK = D // 128
NFF = DFF // 128
SEQ = 4096
NCORES = 8
LAMBDA_INIT = 0.8 - 0.6 * math.exp(-0.3 * 1)
MASKBIG = 30000.0
NEG = -1.0e30


class Res:
    __slots__ = ("w", "r", "excl")

    def __init__(self, excl=False):
        self.w = {}
        self.r = {}
        self.excl = excl


class DmaSem:
    def __init__(self, key, handle, main=True):
        self.key = key
        self.handle = handle
        self.count = 0
        self.main = main
        self.refs = set()


def _merge(d, s):
    for k, v in s.items():
        if d.get(k, 0) < v:
            d[k] = v


class Sched:
    CE = ("pe", "act", "dve")

    def __init__(self, nc, stack):
        self.nc = nc
        self.stack = stack
        self.eng = {"pe": nc.tensor, "act": nc.scalar, "dve": nc.vector, "pool": nc.gpsimd, "sp": nc.sync}
        self.epoch = {e: 0 for e in ("pe", "act", "dve", "pool")}
        self.cnt = {e: 0 for e in ("pe", "act", "dve", "pool")}
        self.semh = {}
        self.waited = {e: {} for e in self.eng}
        self.dsems = {}
        self.all_res = []
        self.n_inst = 0

    def res(self, excl=False):
        r = Res(excl)
        self.all_res.append(r)
        return r

    def pres(self):
        return self.res(excl=True)

    def _sem(self, key):
        h = self.semh.get(key)
        if h is None:
            h = self.stack.enter_context(self.nc.semaphore("s" + "_".join(str(x) for x in key)))
            self.semh[key] = h
        return h

    def ekey(self, e):
        return ("e", e, self.epoch[e])

    def dsem(self, name, main=True):
        d = self.dsems.get(name)
        if d is None:
            key = ("d", name)
            d = DmaSem(key, self._sem(key), main)
            self.dsems[name] = d
        return d

    def _wait(self, eng, deps):
        for k, v in deps.items():
            if eng == "pe" and k[0] == "e" and k[1] == "pe":
                continue
            if self.waited[eng].get(k, 0) >= v:
                continue
            self.waited[eng][k] = v
            self.eng[eng].wait_ge(self.semh[k], v)
            self.n_inst += 1

    def _deps(self, reads, writes):
        deps = {}
        for r in reads:
            _merge(deps, r.w)
        for w in writes:
            _merge(deps, w.w)
            _merge(deps, w.r)
        return deps

    def op(self, eng, fn, reads=(), writes=(), pool_main=False):
        if pool_main:
            self.pool_dirty = True
        deps = self._deps(reads, writes)
        for r in reads:
            if r.excl:
                for k, v in r.r.items():
                    if k[1] != eng and deps.get(k, 0) < v:
                        deps[k] = v
        self._wait(eng, deps)
        self.cnt[eng] += 1
        k = self.ekey(eng)
        v = self.cnt[eng]
        fn(self.eng[eng]).then_inc(self._sem(k), 1)
        self.n_inst += 1
        for r in reads:
            if r.r.get(k, 0) < v:
                r.r[k] = v
        for w in writes:
            w.w = {k: v}
            w.r = {}

    def dma(self, q, out, in_, dsem, reads=(), writes=()):
        self._wait(q, self._deps(reads, writes))
        dsem.count += 16
        self.eng[q].dma_start(out=out, in_=in_).then_inc(dsem.handle, 16)
        self.n_inst += 1
        k = dsem.key
        v = dsem.count
        for r in reads:
            if r.r.get(k, 0) < v:
                r.r[k] = v
            dsem.refs.add(r)
        for w in writes:
            w.w = {k: v}
            w.r = {}
            dsem.refs.add(w)
        dead = []
        for r in (dsem.refs if dsem.main else ()):
            hit = False
            if k in r.w:
                r.w[k] = v
                hit = True
            if k in r.r:
                r.r[k] = v
                hit = True
            if not hit:
                dead.append(r)
        for r in dead:
            dsem.refs.discard(r)

    def barrier(self, new_epoch=False):
        deps = {}
        for e in self.CE:
            if self.cnt[e] > 0:
                deps[self.ekey(e)] = self.cnt[e]
        for d in self.dsems.values():
            if d.main and d.count > 0:
                deps[d.key] = d.count
        if getattr(self, "pool_dirty", False):
            deps[self.ekey("pool")] = self.cnt["pool"]
            self.pool_dirty = False
        for e in ("pe", "act", "dve", "sp"):
            self._wait(e, deps)
        if new_epoch:
            old = set()
            for e in self.CE:
                old.add(self.ekey(e))
                self.epoch[e] += 1
                self.cnt[e] = 0
            for r in self.all_res:
                for k in list(r.w.keys()):
                    if k in old:
                        del r.w[k]
                for k in list(r.r.keys()):
                    if k in old:
                        del r.r[k]

    def final_wait(self, eng="sp"):
        deps = {}
        for e in ("pe", "act", "dve", "pool"):
            if self.cnt[e] > 0:
                deps[self.ekey(e)] = self.cnt[e]
        for d in self.dsems.values():
            if d.count > 0:
                deps[d.key] = d.count
        self._wait(eng, deps)


def host_constants(S):
    half = 32
    inv = (np.float32(1.0) / (np.float32(10000.0) ** (np.arange(half, dtype=np.float32) * np.float32(2.0) / np.float32(64)))).astype(np.float32)
    ang = (np.arange(S, dtype=np.float32)[:, None] * inv[None, :]).astype(np.float32)
    cos = np.cos(ang).astype(np.float32)
    sin = np.sin(ang).astype(np.float32)
    j = np.arange(128) % 32
    cosT = np.ascontiguousarray(cos[:, j].T)
    sinT = np.ascontiguousarray(sin[:, j].T)
    R = np.zeros((128, 128), np.float32)
    for m in range(128):
        d = m % 64
        base = m - d
        if d < 32:
            R[base + d + 32, m] = -1.0
        else:
            R[base + d - 32, m] = 1.0
    ident = np.eye(128, dtype=np.float32)
    n = np.arange(16)[:, None]
    kk = np.arange(S)[None, :]
    erows = (MASKBIG * ((kk // 256) == n)).astype(np.float32)
    ki = np.arange(128)[:, None]
    jj = np.arange(2048 + 384 + 512)[None, :]
    Dm = jj - 384 - ki
    c = ((Dm >= 0) & (Dm <= 128)).astype(np.float32)
    c += ((Dm >= 0) & (Dm <= 512) & (Dm % 4 == 0)).astype(np.float32)
    c += ((Dm >= 0) & (Dm <= 2048) & (Dm % 16 == 0)).astype(np.float32)
    qi = np.arange(128)[None, :]
    tri = (qi >= ki).astype(np.float32)
    bf = ml_dtypes.bfloat16
    return dict(cosT=cosT, sinT=sinT, rperm=R.astype(bf), ident=ident.astype(bf),
                erows=erows.astype(bf), strip_d=c.astype(bf), tri=tri.astype(bf))


class Builder:
    def __init__(self, nc, S, debug=False, nlayers=2):
        self.nc = nc
        self.S = S
        self.NT = S // 512
        self.NS = S // 128
        self.debug = debug
        self.nlayers = nlayers

    def dram_in(self, name, shape, dt):
        return self.nc.dram_tensor(name, list(shape), dt, kind="ExternalInput").ap()

    def dram_scr(self, name, shape, dt, dbg=False):
        kind = "ExternalOutput" if (dbg and self.debug) else "Internal"
        return self.nc.dram_tensor(name, list(shape), dt, kind=kind).ap()

    def _uid(self, name):
        self._n = getattr(self, "_n", 0) + 1
        return "t%d_%s" % (self._n, name)

    def sb(self, st, name, shape, dt):
        return st.enter_context(self.nc.sbuf_tensor(self._uid(name), list(shape), dt))

    def ps(self, st, name, shape, dt):
        return st.enter_context(self.nc.psum_tensor(self._uid(name), list(shape), dt))

    def build(self):
        nc, S = self.nc, self.S
        with ExitStack() as top:
            self.top = top
            self.sc = Sched(nc, top)
            self.declare_io()
            self.setup_constants()
            stop = getattr(self, "stop_after", None)
            if stop == "const":
                self.sc.final_wait("sp")
                return
            self.weight_prep()
            if stop == "prep":
                self.sc.final_wait("sp")
                return
            for l in range(self.nlayers):
                last = (l == self.nlayers - 1)
                xsrc = self.x if l == 0 else self.xs
                xdst = self.out if last else self.xs
                with ExitStack() as st12:
                    hT = self.sb(st12, "hT", [128, NK, S], BF16)
                    R_hT = [self.sc.res() for _ in range(self.NT)]
                    self.phase_norm(l, xsrc, hT, R_hT)
                    if stop == "norm":
                        self.sc.final_wait("sp")
                        return
                    self.phase_proj(l, hT, R_hT)
                    self.sc.barrier()
                if stop == "proj":
                    self.sc.final_wait("sp")
                    return
                self.phase_attn(l)
                if stop == "attn":
                    self.sc.final_wait("sp")
                    return
                self.sc.barrier(new_epoch=True)
                self.phase_ffn(l, xsrc, xdst, last)
                self.sc.barrier(new_epoch=True)
            self.sc.final_wait("sp")

    def declare_io(self):
        S = self.S
        sc = self.sc
        di = self.dram_in
        self.x = di("x", [S, D], F32)
        self.w_in = [di("w_in0", [D, 3 * D], F32), di("w_in1", [D, 3 * D], F32)]
        self.w_out = [di("w_out0", [D, D], F32), di("w_out1", [D, D], F32)]
        self.w_up = [di("w_up0", [D, 2 * DFF], F32), di("w_up1", [D, 2 * DFF], F32)]
        self.w_dn = [di("w_dn0", [DFF, D], F32), di("w_dn1", [DFF, D], F32)]
        self.gin_d = di("gin_t", [128, 16], F32)
        self.gffn_d = di("gffn_t", [128, 16], F32)
        self.subl_d = di("subl_t", [128, 1], F32)
        self.lamq_d = di("lamq", [4, 64], F32)
        self.cw_d = di("cw_t", [128, 2 * 3 * 44], F32)
        self.cb_d = di("cb_t", [128, 2 * 44], F32)
        self.gfin_d = di("gfin", [1, D], F32)
        self.cos_d = di("cosT", [128, S], F32)
        self.sin_d = di("sinT", [128, S], F32)
        self.rperm_d = di("rperm", [128, 128], BF16)
        self.ident_d = di("ident", [128, 128], BF16)
        self.erows_d = di("erows", [16, S], BF16)
        self.stripd_d = di("strip_d", [128, 2944], BF16)
        self.tri_d = di("tri", [128, 128], BF16)
        self.out = self.nc.dram_tensor("out", [S, D], F32, kind="ExternalOutput").ap()
        ds = self.dram_scr
        self.xs = ds("xs", [S, D], F32, dbg=True)
        self.wqkv_s = [ds("wqkv_s%d" % l, [128, NK, 3 * D], BF16) for l in range(2)]
        self.wo_s = [ds("wo_s%d" % l, [128, NK, D], BF16) for l in range(2)]
        self.wup_s = [ds("wup_s%d" % l, [128, NFF, 2, NK, 128], BF16) for l in range(2)]
        self.wdn_s = [ds("wdn_s%d" % l, [128, NFF, D], BF16) for l in range(2)]
        self.qaug_s = ds("qaug_s", [8, 80, S], BF16, dbg=True)
        self.kaug_s = ds("kaug_s", [8, 80, S], BF16, dbg=True)
        self.qT_s = ds("qT_s", [8, 128, S], BF16, dbg=True)
        self.kT_s = ds("kT_s", [8, 128, S], BF16, dbg=True)
        self.v_s = ds("v_s", [8, S, 256], BF16, dbg=True)
        self.oT_s = ds("oT_s", [D, S], BF16, dbg=True)
        self.R_x = [sc.res() for _ in range(self.NS)]
        self.R_xs = [sc.res() for _ in range(self.NS)]
        self.R_out = [sc.res() for _ in range(self.NS)]
        self.R_wqkv = [[[sc.res() for _ in range(NK)] for _ in range(3)] for _ in range(2)]
        self.R_wo = [[sc.res() for _ in range(NK)] for _ in range(2)]
        self.R_wup = [[sc.res() for _ in range(NK * 6)] for _ in range(2)]
        self.R_wdn = [[sc.res() for _ in range(NFF)] for _ in range(2)]
        self.R_qaug = [sc.res() for _ in range(8)]
        self.R_kaug = [sc.res() for _ in range(8)]
        self.R_qT = [sc.res() for _ in range(8)]
        self.R_kT = [sc.res() for _ in range(8)]
        self.R_v = [sc.res() for _ in range(8)]
        self.R_oT = [sc.res() for _ in range(8)]

    def setup_constants(self):
        sc, top = self.sc, self.top
        sb = self.sb
        self.ident = sb(top, "ident", [128, 128], BF16)
        self.rperm = sb(top, "rperm", [128, 128], BF16)
        self.gin = sb(top, "gin", [128, 16], F32)
        self.gffn = sb(top, "gffn", [128, 16], F32)
        self.subl = sb(top, "subl", [128, 1], F32)
        self.cw = sb(top, "cw", [128, 2 * 3 * 44], F32)
        self.cb = sb(top, "cb", [128, 2 * 44], F32)
        self.lamneg = sb(top, "lamneg", [128, 1], F32)
        lq = sb(top, "lq", [128, 4, 64], F32)
        lp = sb(top, "lp", [128, 2, 64], F32)
        ls = sb(top, "ls", [128, 2], F32)
        le = sb(top, "le", [128, 2], F32)
        self.R_const = sc.res()
        Rc = self.R_const
        dsm = sc.dsem("const")
        for dst, src in ((self.ident, self.ident_d), (self.rperm, self.rperm_d), (self.gin, self.gin_d),
                         (self.gffn, self.gffn_d), (self.subl, self.subl_d), (self.cw, self.cw_d),
                         (self.cb, self.cb_d)):
            sc.dma("sp", dst[:], src[:, :], dsm, writes=[Rc])
        for i in range(4):
            sc.dma("sp", lq[:, i, :], self.lamq_d[i:i + 1, :].partition_broadcast(128), dsm, writes=[Rc])
        for h in range(8):
            sc.dma("sp", self.kaug_s[h, 64:80, :], self.erows_d[:, :], sc.dsem("erow"), writes=[self.R_kaug[h]])
        Rl = sc.res()
        sc.op("dve", lambda e: e.tensor_tensor(out=lp[:, 0, :], in0=lq[:, 0, :], in1=lq[:, 1, :], op=ALU.mult), reads=[Rc], writes=[Rl])
        sc.op("dve", lambda e: e.tensor_tensor(out=lp[:, 1, :], in0=lq[:, 2, :], in1=lq[:, 3, :], op=ALU.mult), reads=[Rc, Rl], writes=[Rl])
        sc.op("dve", lambda e: e.tensor_reduce(out=ls[:, :], in_=lp[:, :, :], axis=AX.X, op=ALU.add), reads=[Rl], writes=[Rl])
        sc.op("act", lambda e: e.activation(out=le[:, :], in_=ls[:, :], func=AF.Exp), reads=[Rl], writes=[Rl])
        sc.op("dve", lambda e: e.tensor_tensor(out=ls[:, 0:1], in0=le[:, 1:2], in1=le[:, 0:1], op=ALU.subtract), reads=[Rl], writes=[Rl])
        sc.op("dve", lambda e: e.tensor_scalar(out=self.lamneg[:, :], in0=ls[:, 0:1], scalar1=-LAMBDA_INIT, scalar2=None, op0=ALU.add), reads=[Rl], writes=[Rl])
        self.R_lam = Rl

    def weight_prep(self):
        sc, top = self.sc, self.top
        NB = 3
        stf = [self.sb(top, "stf%d" % i, [128, 1024], F32) for i in range(NB)]
        stb = [self.sb(top, "stb%d" % i, [128, 1024], BF16) for i in range(NB)]
        Rf = [sc.res() for _ in range(NB)]
        Rb = [sc.res() for _ in range(NB)]
        units = []
        for l in range(self.nlayers):
            for k in range(NK):
                for b in range(3):
                    units.append((self.w_in[l][k * 128:(k + 1) * 128, b * 1024:(b + 1) * 1024], 1024,
                                  self.wqkv_s[l][:, k, b * 1024:(b + 1) * 1024], None,
                                  self.gin[:, l * 8 + k:l * 8 + k + 1], 1.0, self.R_wqkv[l][b][k]))
            for k in range(NK):
                s1 = self.subl[:, 0:1] if l == 1 else 1.0
                s2 = (1.0 - LAMBDA_INIT) if l == 1 else 1.0
                units.append((self.w_out[l][k * 128:(k + 1) * 128, :], 1024, self.wo_s[l][:, k, :], None,
                              s1, s2, self.R_wo[l][k]))
            for k in range(NK):
                ci = 0
                for isval in range(2):
                    for (c0, n) in ((0, 1024), (1024, 1024), (2048, 768)):
                        col = isval * DFF + c0
                        pj0 = c0 // 128
                        dst = self.wup_s[l][:, pj0:pj0 + n // 128, isval, k, :]
                        units.append((self.w_up[l][k * 128:(k + 1) * 128, col:col + n], n, dst, n // 128,
                                      self.gffn[:, l * 8 + k:l * 8 + k + 1], 1.0, self.R_wup[l][k * 6 + ci]))
                        ci += 1
            for j in range(NFF):
                units.append((self.w_dn[l][j * 128:(j + 1) * 128, :], 1024, self.wdn_s[l][:, j, :], None,
                              1.0, 1.0, self.R_wdn[l][j]))
        for i, (src, n, dst, nchunk, s1, s2, Rw) in enumerate(units):
            b = i % NB
            sc.dma("pool", stf[b][:, 0:n], src, sc.dsem("pl%d" % b, main=False), writes=[Rf[b]])
            rd = [Rf[b], self.R_const]
            sc.op("pool", lambda e, b=b, n=n, s1=s1, s2=s2: e.tensor_scalar(
                out=stb[b][:, 0:n], in0=stf[b][:, 0:n], scalar1=s1, scalar2=s2, op0=ALU.mult, op1=ALU.mult),
                reads=rd, writes=[Rb[b]])
            if nchunk is None:
                srcb = stb[b][:, 0:n]
            else:
                srcb = stb[b][:, 0:n].rearrange("p (c w) -> p c w", w=128)
            sc.dma("pool", dst, srcb, sc.dsem("ps%d" % b, main=False), reads=[Rb[b]], writes=[Rw])

    def phase_norm(self, l, xsrc, hT, R_hT):
        sc = self.sc
        R_src = self.R_x if l == 0 else self.R_xs
        with ExitStack() as st:
            xin = [self.sb(st, "p1x%d" % i, [128, 4, D], F32) for i in range(2)]
            hb = [self.sb(st, "p1h%d" % i, [128, D], BF16) for i in range(2)]
            junk = self.sb(st, "p1j", [128, D], BF16)
            ssq = self.sb(st, "p1ss", [128, self.NS], F32)
            rt = self.sb(st, "p1rt", [128, self.NS], F32)
            rstd = self.sb(st, "p1rs", [128, self.NS], F32)
            psT = [self.ps(st, "p1T%d" % i, [128, NK, 128], BF16) for i in range(2)]
            R_xin = [sc.res() for _ in range(2)]
            R_hb = [sc.res() for _ in range(2)]
            R_j = sc.res()
            R_st = [sc.res() for _ in range(self.NT)]
            R_psT = [sc.pres() for _ in range(2)]
            for g in range(self.NT):
                b = g % 2
                sc.dma("sp", xin[b][:, :, :], xsrc[g * 512:(g + 1) * 512, :].rearrange("(s p) d -> p s d", p=128),
                       sc.dsem("p1x%d" % b), reads=R_src[g * 4:(g + 1) * 4], writes=[R_xin[b]])
                for s in range(4):
                    sub = g * 4 + s
                    sc.op("act", lambda e, b=b, s=s, sub=sub: e.activation(
                        out=junk[:, :], in_=xin[b][:, s, :], func=AF.Square, accum_out=ssq[:, sub:sub + 1]),
                        reads=[R_xin[b]], writes=[R_j, R_st[g]])
                sc.op("act", lambda e, g=g: e.activation(out=rt[:, g * 4:(g + 1) * 4], in_=ssq[:, g * 4:(g + 1) * 4],
                                                         func=AF.Sqrt, scale=1.0 / D, bias=1e-6),
                      reads=[R_st[g]], writes=[R_st[g]])
                sc.op("dve", lambda e, g=g: e.reciprocal(out=rstd[:, g * 4:(g + 1) * 4], in_=rt[:, g * 4:(g + 1) * 4]),
                      reads=[R_st[g]], writes=[R_st[g]])
                for s in range(4):
                    sub = g * 4 + s
                    hbi = sub % 2
                    sc.op("dve", lambda e, b=b, s=s, sub=sub, hbi=hbi: e.tensor_scalar(
                        out=hb[hbi][:, :], in0=xin[b][:, s, :], scalar1=rstd[:, sub:sub + 1], scalar2=None, op0=ALU.mult),
                        reads=[R_xin[b], R_st[g]], writes=[R_hb[hbi]])
                    for k in range(NK):
                        sc.op("pe", lambda e, hbi=hbi, k=k: e.transpose(
                            out=psT[hbi][:, k, :], in_=hb[hbi][:, k * 128:(k + 1) * 128], identity=self.ident[:, :]),
                            reads=[R_hb[hbi], self.R_const], writes=[R_psT[hbi]])
                    sc.op("act", lambda e, hbi=hbi, sub=sub: e.activation(
                        out=hT[:, :, sub * 128:(sub + 1) * 128], in_=psT[hbi][:, :, :], func=AF.Copy),
                        reads=[R_psT[hbi]], writes=[R_hT[g]])
            sc.barrier()

    def phase_proj(self, l, hT, R_hT):
        sc = self.sc
        S, NT, NS = self.S, self.NT, self.NS
        with ExitStack() as st:
            cosT = self.sb(st, "p2cos", [128, S], F32)
            sinT = self.sb(st, "p2sin", [128, S], F32)
            R_tab = sc.res()
            sc.dma("sp", cosT[:, :], self.cos_d[:, :], sc.dsem("p2tab"), writes=[R_tab])
            sc.dma("sp", sinT[:, :], self.sin_d[:, :], sc.dsem("p2tab"), writes=[R_tab])
            NW = 4
            wch = [self.sb(st, "p2w%d" % i, [128, NK, 128], BF16) for i in range(NW)]
            R_w = [sc.res() for _ in range(NW)]
            qb = [self.sb(st, "p2qb%d" % i, [128, 512], BF16) for i in range(2)]
            R_qb = [sc.res() for _ in range(2)]
            Af = [self.sb(st, "p2A%d" % i, [128, 512], F32) for i in range(2)]
            Bf = [self.sb(st, "p2B%d" % i, [128, 512], F32) for i in range(2)]
            R_A = [sc.res() for _ in range(2)]
            R_B = [sc.res() for _ in range(2)]
            tmpf = [self.sb(st, "p2t%d" % i, [128, 512], F32) for i in range(2)]
            R_tmp = [sc.res() for _ in range(2)]
            stg = [self.sb(st, "p2stg%d" % i, [128, S], BF16) for i in range(2)]
            R_stg = [sc.res() for _ in range(2)]
            vsb = [self.sb(st, "p2v%d" % i, [128, NS, 256 if l == 0 else 130], BF16) for i in range(2)]
            R_vsb = [sc.res() for _ in range(2)]
            ksum = self.sb(st, "p2ks", [128, 16], F32)
            R_ks = sc.res()
            kpad = self.sb(st, "p2kpad", [128, 2, 16], F32)
            R_kpad = sc.res()
            sc.op("dve", lambda e: e.memset(kpad[:, :, :], 0.0), writes=[R_kpad])
            gate = self.sb(st, "p2g", [128, 4, 2, 16], F32)
            top8 = self.sb(st, "p2t8", [128, 4, 2, 8], F32)
            nmp = self.sb(st, "p2nm", [128, 4, 128], BF16)
            sc.op("dve", lambda e: e.memset(nmp[:, :, :], 0.0), writes=[sc.res()])
            nm = nmp[:, :, 0:32].rearrange("p s (h n) -> p s h n", h=2)
            R_gate = sc.res()
            R_nm = sc.res()
            R_t8 = [[sc.res() for _ in range(2)] for _ in range(4)]
            R_nm2 = [[sc.res() for _ in range(2)] for _ in range(4)]
            nmT = self.sb(st, "p2nmT", [32, S], BF16)
            R_nmT = sc.res()
            ps1 = [self.ps(st, "p2ps1_%d" % i, [128, 512], F32) for i in range(2)]
            ps2 = [self.ps(st, "p2ps2_%d" % i, [128, 512], F32) for i in range(2)]
            psV = [self.ps(st, "p2psV%d" % i, [128, 4, 128], F32) for i in range(2)]
            psG = self.ps(st, "p2psG", [128, 512], F32)
            psN = self.ps(st, "p2psN", [128, 512], F32)
            R_ps1 = [sc.pres() for _ in range(2)]
            R_ps2 = [sc.pres() for _ in range(2)]
            R_psV = [sc.pres() for _ in range(2)]
            R_psG = sc.pres()
            R_psN = sc.pres()
            for i in range(2):
                if l == 0:
                    v4 = vsb[i][:, :, :].rearrange("p s (h e) -> p s h e", h=2)
                    sc.op("dve", lambda e, v4=v4: e.memset(v4[:, :, :, 64:128], 1.0), writes=[R_vsb[i]])
                else:
                    sc.op("dve", lambda e, i=i: e.memset(vsb[i][:, :, 128:129], 1.0), writes=[R_vsb[i]])

            wctr = [0]
            tctr = [0]
            sctr = [0]
            import os as _os
            dbg = _os.environ.get("KDBG", "")
            if "p2a" in dbg:
                return
            if "waitpool" in dbg:
                deps = {sc.ekey("pool"): sc.cnt["pool"]}
                for d_ in sc.dsems.values():
                    if d_.count > 0:
                        deps[d_.key] = d_.count
                for e_ in ("pe", "act", "dve", "sp"):
                    sc._wait(e_, deps)

            def load_w(blk, c):
                i = wctr[0] % NW
                wctr[0] += 1
                col = blk * 1024 + c * 128
                sc.dma("sp", wch[i][:, :, :], self.wqkv_s[l][:, :, col:col + 128], sc.dsem("p2w%d" % i),
                       reads=self.R_wqkv[l][blk], writes=[R_w[i]])
                return i

            def qk_chunk(blk, c, moba, wi):
                si = sctr[0] % 2
                sctr[0] += 1
                if moba and blk == 0:
                    sc.op("dve", lambda e: e.memset(gate[:, :, :, :], NEG), writes=[R_gate])
                    sc.op("dve", lambda e: e.tensor_copy(out=kpad[0:64, 0, :], in_=ksum[0:64, :]), reads=[R_ks], writes=[R_kpad])
                    sc.op("dve", lambda e: e.tensor_copy(out=kpad[64:128, 1, :], in_=ksum[64:128, :]), reads=[R_ks], writes=[R_kpad])
                if moba and blk == 1:
                    sc.op("dve", lambda e: e.memset(ksum[:, :], 0.0), writes=[R_ks])
                def stA(t):
                    i = t % 2
                    cols = slice(t * 512, (t + 1) * 512)
                    for k in range(NK):
                        sc.op("pe", lambda e, i=i, k=k, wi=wi, cols=cols: e.matmul(
                            ps1[i][:, :], lhsT=wch[wi][:, k, :], rhs=hT[:, k, cols], start=(k == 0), stop=(k == NK - 1)),
                            reads=[R_w[wi], R_hT[t]], writes=[R_ps1[i]])
                    sc.op("act", lambda e, i=i: e.activation(out=qb[i][:, :], in_=ps1[i][:, :], func=AF.Copy),
                          reads=[R_ps1[i]], writes=[R_qb[i]])

                def stB(t):
                    i = t % 2
                    cols = slice(t * 512, (t + 1) * 512)
                    sc.op("pe", lambda e, i=i: e.matmul(ps2[i][:, :], lhsT=self.rperm[:, :], rhs=qb[i][:, :], start=True, stop=True),
                          reads=[R_qb[i], self.R_const], writes=[R_ps2[i]])
                    sc.op("dve", lambda e, i=i, cols=cols: e.tensor_tensor(out=Af[i][:, :], in0=ps1[i][:, :], in1=cosT[:, cols], op=ALU.mult),
                          reads=[R_ps1[i], R_tab], writes=[R_A[i]])
                    sc.op("dve", lambda e, i=i, cols=cols: e.tensor_tensor(out=Bf[i][:, :], in0=ps2[i][:, :], in1=sinT[:, cols], op=ALU.mult),
                          reads=[R_ps2[i], R_tab], writes=[R_B[i]])
                    if not moba:
                        sc.op("dve", lambda e, i=i, cols=cols: e.tensor_tensor(out=stg[si][:, cols], in0=Af[i][:, :], in1=Bf[i][:, :], op=ALU.add),
                              reads=[R_A[i], R_B[i]], writes=[R_stg[si]])
                        return
                    sc.op("dve", lambda e, i=i: e.tensor_tensor(out=tmpf[i][:, :], in0=Af[i][:, :], in1=Bf[i][:, :], op=ALU.add),
                          reads=[R_A[i], R_B[i]], writes=[R_tmp[i]])
                    sc.op("act", lambda e, i=i, cols=cols: e.activation(out=stg[si][:, cols], in_=tmpf[i][:, :], func=AF.Copy),
                          reads=[R_tmp[i]], writes=[R_stg[si]])
                    if blk == 1:
                        sc.op("dve", lambda e, i=i, t=t: e.tensor_reduce(
                            out=ksum[:, 2 * t:2 * t + 2], in_=tmpf[i][:, :].rearrange("p (b w) -> p b w", w=256), axis=AX.X, op=ALU.add),
                            reads=[R_tmp[i]], writes=[R_ks])

                def stC(t):
                    i = t % 2
                    for s_ in range(4):
                        for h in range(2):
                            sc.op("pe", lambda e, i=i, s_=s_, h=h: e.matmul(
                                psG[:, (s_ * 2 + h) * 16:(s_ * 2 + h) * 16 + 16], lhsT=tmpf[i][:, s_ * 128:(s_ + 1) * 128],
                                rhs=kpad[:, h, :], start=True, stop=True),
                                reads=[R_tmp[i], R_kpad], writes=[R_psG])
                    psG4 = psG[:, 0:128].rearrange("p (s h n) -> p s h n", s=4, h=2)
                    sc.op("dve", lambda e: e.memset(nm[:, :, :, :], -1.0),
                          writes=[R_nm] + [R_nm2[a][b_] for a in range(4) for b_ in range(2)])
                    for half in range(2):
                        b = 2 * t + half
                        ss = slice(2 * half, 2 * half + 2)
                        sc.op("dve", lambda e, ss=ss, b=b: e.memset(nm[:, ss, :, b:b + 1], 0.0), writes=[R_nm])
                        if b > 0:
                            sc.op("dve", lambda e, ss=ss, b=b, psG4=psG4: e.tensor_copy(out=gate[:, ss, :, 0:b], in_=psG4[:, ss, :, 0:b]),
                                  reads=[R_psG], writes=[R_gate])
                    for s_ in range(4):
                        b = 2 * t + s_ // 2
                        if b == 0:
                            continue
                        for h in range(2):
                            sc.op("dve", lambda e, s_=s_, h=h: e.max(out=top8[:, s_, h, :], in_=gate[:, s_, h, :]),
                                  reads=[R_gate], writes=[R_t8[s_][h]])
                            sc.op("dve", lambda e, s_=s_, h=h, b=b: e.tensor_scalar(
                                out=nm[:, s_, h, 0:b], in0=gate[:, s_, h, 0:b], scalar1=top8[:, s_, h, 2:3], scalar2=1.0,
                                op0=ALU.is_ge, op1=ALU.subtract), reads=[R_gate, R_t8[s_][h], R_nm], writes=[R_nm2[s_][h]])

                def stD(t):
                    cols = slice(t * 512, (t + 1) * 512)
                    allnm = [R_nm] + [R_nm2[a][b_] for a in range(4) for b_ in range(2)]
                    for s_ in range(4):
                        sc.op("pe", lambda e, s_=s_: e.matmul(
                            psN[:, s_ * 128:(s_ + 1) * 128], lhsT=nmp[:, s_, :], rhs=self.ident[:, :],
                            start=True, stop=True), reads=allnm + [self.R_const], writes=[R_psN])
                    sc.op("act", lambda e, cols=cols: e.activation(out=nmT[:, cols], in_=psN[0:32, :], func=AF.Copy),
                          reads=[R_psN], writes=[R_nmT])

                if moba and "mobaseq" in dbg:
                    stages = [(stA, 0), (stB, 0)]
                    if blk == 0:
                        stages += [(stC, 0), (stD, 0)]
                else:
                    stages = [(stA, 0), (stB, 1)]
                    if moba and blk == 0:
                        stages += [(stC, 2), (stD, 3)]
                for step in range(NT + len(stages) - 1):
                    for fn_, lag in reversed(stages):
                        t = step - lag
                        if 0 <= t < NT:
                            fn_(t)
                if "nostore" in dbg:
                    return
                ds_ = sc.dsem("p2st%d" % si)
                if moba:
                    dstt = self.qaug_s if blk == 0 else self.kaug_s
                    Rd = self.R_qaug if blk == 0 else self.R_kaug
                    for h in range(2):
                        sc.dma("sp", dstt[2 * c + h, 0:64, :], stg[si][h * 64:(h + 1) * 64, :], ds_, reads=[R_stg[si]], writes=[Rd[2 * c + h]])
                    if blk == 0:
                        for h in range(2):
                            sc.dma("sp", self.qaug_s[2 * c + h, 64:80, :], nmT[h * 16:(h + 1) * 16, :], sc.dsem("p2nm"),
                                   reads=[R_nmT], writes=[self.R_qaug[2 * c + h]])
                else:
                    dstt = self.qT_s if blk == 0 else self.kT_s
                    Rd = self.R_qT if blk == 0 else self.R_kT
                    sc.dma("sp", dstt[c, :, :], stg[si][:, :], ds_, reads=[R_stg[si]], writes=[Rd[c]])

            vctr = [0]

            def v_chunk(c, wi):
                vi = vctr[0] % 2
                vctr[0] += 1
                for g in range(NS // 4):
                    pi = g % 2
                    for s in range(4):
                        sub = g * 4 + s
                        for k in range(NK):
                            sc.op("pe", lambda e, pi=pi, s=s, k=k, sub=sub, wi=wi: e.matmul(
                                psV[pi][:, s, :], lhsT=hT[:, k, sub * 128:(sub + 1) * 128], rhs=wch[wi][:, k, :],
                                start=(k == 0), stop=(k == NK - 1)),
                                reads=[R_hT[sub // 4], R_w[wi]], writes=[R_psV[pi]])
                    if l == 0:
                        dst = vsb[vi][:, g * 4:(g + 1) * 4, :].rearrange("p s (h e) -> p s h e", h=2)[:, :, :, 0:64]
                        src = psV[pi][:, :, :].rearrange("p s (h d) -> p s h d", h=2)
                    else:
                        dst = vsb[vi][:, g * 4:(g + 1) * 4, 0:128]
                        src = psV[pi][:, :, :]
                    sc.op("act", lambda e, dst=dst, src=src: e.activation(out=dst, in_=src, func=AF.Copy),
                          reads=[R_psV[pi]], writes=[R_vsb[vi]])
                vd = self.v_s[c].rearrange("(s p) e -> p s e", p=128)
                ve = 256 if l == 0 else 129
                for q4 in range(0, NS, 8):
                    sc.dma("sp", vd[:, q4:q4 + 8, 0:ve], vsb[vi][:, q4:q4 + 8, 0:ve], sc.dsem("p2vst%d" % vi),
                           reads=[R_vsb[vi]], writes=[self.R_v[c]])

            tasks = []
            for c in range(8):
                tasks += [(1, c), (0, c), (2, c)]
            wi_of = {}
            for i_ in range(3):
                wi_of[i_] = load_w(*tasks[i_])
            for i_, (blk_, c) in enumerate(tasks):
                if i_ + 3 < len(tasks):
                    wi_of[i_ + 3] = load_w(*tasks[i_ + 3])
                moba = (l == 0 and c < 4)
                if blk_ == 2:
                    v_chunk(c, wi_of[i_])
                else:
                    qk_chunk(blk_, c, moba, wi_of[i_])

    def phase_attn(self, l):
        sc = self.sc
        S, NT, NS = self.S, self.NT, self.NS
        with ExitStack() as st:
            NSET = 2
            qa = [self.sb(st, "p3qa%d" % i, [128, S], BF16) for i in range(NSET)]
            ka = [self.sb(st, "p3ka%d" % i, [128, S], BF16) for i in range(NSET)]
            if l == 0:
                qbt = [self.sb(st, "p3qb%d" % i, [128, S], BF16) for i in range(NSET)]
            kbt = [self.sb(st, "p3kb%d" % i, [128, S], BF16) for i in range(NSET)]
            vt = [self.sb(st, "p3v%d" % i, [128, NS, 256 if l == 0 else 130], BF16) for i in range(NSET)]
            R_in = [sc.res() for _ in range(NSET)]
            if l == 1:
                osb = [self.sb(st, "p3o%d" % i, [128, NS, 128], BF16) for i in range(2)]
            R_osb = [sc.res() for _ in range(2)]
            otsb = [self.sb(st, "p3ot%d" % i, [128, S], BF16) for i in range(2)]
            R_ot = [sc.res() for _ in range(2)]
            NPT = 5
            pt = [self.sb(st, "p3pt%d" % i, [128, 512], BF16) for i in range(NPT)]
            R_pt = [sc.res() for _ in range(NPT)]
            tri = self.sb(st, "p3tri", [128, 128], BF16)
            R_msk = sc.res()
            sc.dma("sp", tri[:, :], self.tri_d[:, :], sc.dsem("p3msk"), writes=[R_msk])
            if l == 0:
                strip = self.sb(st, "p3strip", [128, 2944], BF16)
                sc.dma("sp", strip[:, :], self.stripd_d[:, :], sc.dsem("p3msk"), writes=[R_msk])
                rec = [self.sb(st, "p3rec%d" % i, [64, 512], F32) for i in range(2)]
                R_rec = [sc.res() for _ in range(2)]
            else:
                rec1 = self.sb(st, "p3r1", [128, 2, 2, 1], F32)
                rec2 = self.sb(st, "p3r2", [128, 2, 2, 1], F32)
                tb = [self.sb(st, "p3tb%d" % i, [128, 128], F32) for i in range(2)]
                R_tb = [sc.res() for _ in range(2)]
                ob = [self.sb(st, "p3ob%d" % i, [128, 4, 128], F32) for i in range(2)]
                R_ob = [sc.res() for _ in range(2)]
                junk = self.sb(st, "p3junk", [128, 128], BF16)
                R_junk = sc.res()
                ssd = self.sb(st, "p3ssd", [128, 4], F32)
                rtd = self.sb(st, "p3rtd", [128, 4], F32)
                rsd = self.sb(st, "p3rsd", [128, 4], F32)
                R_fin = sc.res()
            NSB = 3
            psS = [self.ps(st, "p3S%d" % i, [128, 512], F32) for i in range(NSB)]
            R_S = [sc.pres() for _ in range(NSB)]
            psO = [self.ps(st, "p3O%d" % i, [128, 512], F32) for i in range(4)]
            R_O = [sc.pres() for _ in range(4)]
            psT = self.ps(st, "p3T", [128, 8, 128], BF16)
            R_T = sc.pres()

            zeroed = set()

            def load_chunk(c):
                i = c % NSET
                R = R_in[i]
                dq = sc.dsem("p3in%d" % i)
                if l == 0 and c < 4:
                    sc.dma("sp", qa[i][0:80, :], self.qaug_s[2 * c, :, :], dq, reads=[self.R_qaug[2 * c]], writes=[R])
                    sc.dma("sp", qbt[i][0:80, :], self.qaug_s[2 * c + 1, :, :], dq, reads=[self.R_qaug[2 * c + 1]], writes=[R])
                    sc.dma("sp", ka[i][0:80, :], self.kaug_s[2 * c, :, :], dq, reads=[self.R_kaug[2 * c]], writes=[R])
                    sc.dma("sp", kbt[i][0:80, :], self.kaug_s[2 * c + 1, :, :], dq, reads=[self.R_kaug[2 * c + 1]], writes=[R])
                else:
                    if i not in zeroed:
                        zeroed.add(i)
                        sc.op("dve", lambda e, i=i: e.memset(ka[i][64:128, :], 0.0), writes=[R])
                        sc.op("dve", lambda e, i=i: e.memset(kbt[i][0:64, :], 0.0), writes=[R])
                    sc.dma("sp", qa[i][:, :], self.qT_s[c, :, :], dq, reads=[self.R_qT[c]], writes=[R])
                    sc.dma("sp", ka[i][0:64, :], self.kT_s[c, 0:64, :], dq, reads=[self.R_kT[c]], writes=[R])
                    sc.dma("sp", kbt[i][64:128, :], self.kT_s[c, 64:128, :], dq, reads=[self.R_kT[c]], writes=[R])
                vd = self.v_s[c].rearrange("(s p) e -> p s e", p=128)
                ve = 256 if l == 0 else 129
                for q4 in range(0, NS, 8):
                    sc.dma("sp", vt[i][:, q4:q4 + 8, 0:ve], vd[:, q4:q4 + 8, 0:ve], dq, reads=[self.R_v[c]], writes=[R])

            sctr = [0]
            pctr = [0]
            mctr = [0]

            def run_chunk(c):
                i = c % NSET
                oi = c % 2
                Rin = R_in[i]
                if l == 0 and c < 4:
                    kind = "moba"
                    maps = [dict(q=qa[i], k=ka[i], r0=0, r1=80, h=0), dict(q=qbt[i], k=kbt[i], r0=0, r1=80, h=1)]
                elif l == 0:
                    kind = "dil"
                    maps = [dict(q=qa[i], k=ka[i], r0=0, r1=128, h=0), dict(q=qa[i], k=kbt[i], r0=0, r1=128, h=1)]
                else:
                    kind = "causal"
                    maps = [dict(q=qa[i], k=ka[i], r0=0, r1=128, h=0), dict(q=qa[i], k=kbt[i], r0=0, r1=128, h=1)]
                if l == 0:
                    v4 = vt[i][:, :, :].rearrange("p s (h e) -> p s h e", h=2)
                steps = []
                for qt in range(NT):
                    for m in range(2):
                        if kind == "dil":
                            kts = list(range(max(0, 4 * qt - 16), 4 * qt + 4))
                        else:
                            kts = list(range(0, 4 * qt + 4))
                        for kt in kts:
                            steps.append((qt, m, kt, kt == kts[0], kt == kts[-1]))
                started = {}
                pendq = []
                LAG = 2

                def emit_pv(stp):
                    qt, m, kt, first, last, pi, lo, hi = stp
                    mp = maps[m]
                    if l == 0:
                        bank = m + 2 * (qt % 2)
                        key = (bank, qt, m)
                        stt = key not in started
                        started[key] = True
                        sc.op("pe", lambda e, bank=bank, kt=kt, h=mp["h"], pi=pi, lo=lo, hi=hi, stt=stt, last=last: e.matmul(
                            psO[bank][:, lo:hi], lhsT=v4[:, kt, h, :], rhs=pt[pi][:, lo:hi], start=stt, stop=last, skip_group_check=True),
                            reads=[R_pt[pi], Rin], writes=[R_O[bank]])
                        if last:
                            finalize(qt, m)
                        return
                    for qs in range(lo // 128, hi // 128):
                        if l == 0:
                            bank = m + 2 * (qt % 2)
                            outap = psO[bank][:, qs * 65:(qs + 1) * 65]
                            rhs = v4[:, kt, mp["h"], :]
                        else:
                            bank = 2 * m + qs // 2
                            outap = psO[bank][:, (qs % 2) * 129:(qs % 2) * 129 + 129]
                            rhs = vt[i][:, kt, 0:129]
                        key = (bank, qt, m)
                        stt = key not in started
                        started[key] = True
                        sc.op("pe", lambda e, outap=outap, rhs=rhs, pi=pi, qs=qs, stt=stt, last=last: e.matmul(
                            outap, lhsT=pt[pi][:, qs * 128:(qs + 1) * 128], rhs=rhs, start=stt, stop=last, skip_group_check=True),
                            reads=[R_pt[pi], Rin], writes=[R_O[bank]])
                    if last:
                        finalize(qt, m)

                def finalize(qt, m):
                    if l == 0:
                        bank = m + 2 * (qt % 2)
                        ri = (qt * 2 + m) % 2
                        sc.op("dve", lambda e, bank=bank, ri=ri: e.reciprocal(out=rec[ri][0:64, :], in_=psO[bank][64:128, :]),
                              reads=[R_O[bank]], writes=[R_rec[ri]])
                        sc.op("dve", lambda e, bank=bank, ri=ri, m=m, qt=qt: e.tensor_tensor(
                            out=otsb[oi][m * 64:(m + 1) * 64, qt * 512:(qt + 1) * 512], in0=psO[bank][0:64, :], in1=rec[ri][0:64, :],
                            op=ALU.mult), reads=[R_O[bank], R_rec[ri]], writes=[R_ot[oi]])
                    else:
                        if m == 0:
                            return
                        for bb in range(2):
                            O1 = psO[bb][:, 0:258].rearrange("p (q e) -> p q e", e=129)
                            O2 = psO[2 + bb][:, 0:258].rearrange("p (q e) -> p q e", e=129)
                            sc.op("dve", lambda e, O1=O1, bb=bb: e.reciprocal(out=rec1[:, bb, :, :], in_=O1[:, :, 128:129]),
                                  reads=[R_O[bb]], writes=[R_fin])
                            sc.op("dve", lambda e, O2=O2, bb=bb: e.reciprocal(out=rec2[:, bb, :, :], in_=O2[:, :, 128:129]),
                                  reads=[R_O[2 + bb]], writes=[R_fin])
                        sc.op("dve", lambda e: e.tensor_scalar(out=rec2[:, :, :, :], in0=rec2[:, :, :, :], scalar1=self.lamneg[:, 0:1],
                                                               scalar2=None, op0=ALU.mult), reads=[R_fin, self.R_lam], writes=[R_fin])
                        obi = qt % 2
                        for qs in range(4):
                            bb, ii = qs // 2, qs % 2
                            O1 = psO[bb][:, 0:258].rearrange("p (q e) -> p q e", e=129)
                            O2 = psO[2 + bb][:, 0:258].rearrange("p (q e) -> p q e", e=129)
                            ti = qs % 2
                            sc.op("act", lambda e, O2=O2, ii=ii, bb=bb, ti=ti: e.activation(
                                out=tb[ti][:, :], in_=O2[:, ii, 0:128], func=AF.Copy, scale=rec2[:, bb, ii, 0:1]),
                                reads=[R_O[2 + bb], R_fin], writes=[R_tb[ti]])
                            sc.op("dve", lambda e, O1=O1, ii=ii, bb=bb, ti=ti, obi=obi, qs=qs: e.scalar_tensor_tensor(
                                out=ob[obi][:, qs, :], in0=O1[:, ii, 0:128], scalar=rec1[:, bb, ii, 0:1], in1=tb[ti][:, :],
                                op0=ALU.mult, op1=ALU.add), reads=[R_O[bb], R_fin, R_tb[ti]], writes=[R_ob[obi]])
                            sc.op("act", lambda e, obi=obi, qs=qs: e.activation(
                                out=junk[:, :], in_=ob[obi][:, qs, :], func=AF.Square, accum_out=ssd[:, qs:qs + 1]),
                                reads=[R_ob[obi]], writes=[R_junk, R_fin])
                        sc.op("act", lambda e: e.activation(out=rtd[:, :], in_=ssd[:, :], func=AF.Sqrt, scale=1.0 / 128, bias=1e-5),
                              reads=[R_fin], writes=[R_fin])
                        sc.op("dve", lambda e: e.reciprocal(out=rsd[:, :], in_=rtd[:, :]), reads=[R_fin], writes=[R_fin])
                        for qs in range(4):
                            sub = qt * 4 + qs
                            sc.op("dve", lambda e, obi=obi, qs=qs, sub=sub: e.tensor_scalar(
                                out=osb[oi][:, sub, :], in0=ob[obi][:, qs, :], scalar1=rsd[:, qs:qs + 1], scalar2=None, op0=ALU.mult),
                                reads=[R_ob[obi], R_fin], writes=[R_osb[oi]])
                        transposes(qt)

                def transposes(qt):
                    for qs in range(4):
                        sub = qt * 4 + qs
                        sc.op("pe", lambda e, qs=qs, sub=sub: e.transpose(out=psT[:, qs, :], in_=osb[oi][:, sub, :], identity=self.ident[:, :]),
                              reads=[R_osb[oi], self.R_const], writes=[R_T])
                    sc.op("act", lambda e, qt=qt: e.activation(
                        out=otsb[oi][:, qt * 512:(qt + 1) * 512].rearrange("p (s w) -> p s w", w=128), in_=psT[:, 0:4, :], func=AF.Copy),
                        reads=[R_T], writes=[R_ot[oi]])

                for (qt, m, kt, first, last) in steps:
                    mp = maps[m]
                    delta = (4 * qt - kt) * 128
                    lo = max(0, -delta)
                    hi = 512
                    if kind == "dil":
                        hi = min(512, 2048 - delta + 128)
                    si = sctr[0] % NSB
                    sctr[0] += 1
                    pi = pctr[0] % NPT
                    pctr[0] += 1
                    r0, r1 = mp["r0"], mp["r1"]
                    sc.op("pe", lambda e, si=si, mp=mp, r0=r0, r1=r1, kt=kt, qt=qt: e.matmul(
                        psS[si][:, :], lhsT=mp["k"][r0:r1, kt * 128:(kt + 1) * 128], rhs=mp["q"][r0:r1, qt * 512:(qt + 1) * 512],
                        start=True, stop=True), reads=[Rin], writes=[R_S[si]])
                    if len(pendq) >= LAG:
                        emit_pv(pendq.pop(0))
                    sc.op("act", lambda e, si=si, pi=pi, lo=lo, hi=hi: e.activation(
                        out=pt[pi][:, lo:hi], in_=psS[si][:, lo:hi], func=AF.Exp, scale=0.125),
                        reads=[R_S[si]], writes=[R_pt[pi]])
                    if kind == "dil":
                        o0 = delta + 384
                        mctr[0] += 1
                        on_pool = False
                        sc.op("pool" if on_pool else "dve", lambda e, pi=pi, lo=lo, hi=hi, o0=o0: e.tensor_tensor(
                            out=pt[pi][:, lo:hi], in0=pt[pi][:, lo:hi], in1=strip[:, o0 + lo:o0 + hi], op=ALU.mult),
                            reads=[R_pt[pi], R_msk], writes=[R_pt[pi]], pool_main=on_pool)
                    elif delta <= 0:
                        sc.op("dve", lambda e, pi=pi, lo=lo: e.tensor_tensor(
                            out=pt[pi][:, lo:lo + 128], in0=pt[pi][:, lo:lo + 128], in1=tri[:, :], op=ALU.mult),
                            reads=[R_pt[pi], R_msk], writes=[R_pt[pi]])
                    pendq.append((qt, m, kt, first, last, pi, lo, hi))
                while pendq:
                    emit_pv(pendq.pop(0))
                sc.dma("sp", self.oT_s[c * 128:(c + 1) * 128, :], otsb[oi][:, :], sc.dsem("p3ost%d" % oi),
                       reads=[R_ot[oi]], writes=[self.R_oT[c]])

            load_chunk(0)
            for c in range(8):
                if c + 1 < 8:
                    load_chunk(c + 1)
                run_chunk(c)

    def phase_ffn(self, l, xsrc, xdst, last):
        sc = self.sc
        S = self.S
        R_src = self.R_x if l == 0 else self.R_xs
        R_dst = self.R_out if last else self.R_xs
        tiles = []
        t0 = 0
        while t0 < S:
            T = min(384, S - t0)
            tiles.append((t0, T))
            t0 += T
        TM = 384
        with ExitStack() as st:
            wo = self.sb(st, "p4wo", [128, NK, D], BF16)
            wdn = self.sb(st, "p4wdn", [128, NFF, D], BF16)
            R_wres = sc.res()
            dw = sc.dsem("p4w")
            for k in range(NK):
                sc.dma("sp", wo[:, k, :], self.wo_s[l][:, k, :], dw, reads=[self.R_wo[l][k]], writes=[R_wres])
            for j in range(NFF):
                sc.dma("sp", wdn[:, j, :], self.wdn_s[l][:, j, :], dw, reads=[self.R_wdn[l][j]], writes=[R_wres])
            NWS = 3
            wup = [self.sb(st, "p4wu%d" % i, [128, 2, 2, NK, 128], BF16) for i in range(NWS)]
            R_wu = [sc.res() for _ in range(NWS)]
            xin = [self.sb(st, "p4x%d" % i, [128, 3, D], F32) for i in range(2)]
            R_x = [sc.res() for _ in range(2)]
            oT = self.sb(st, "p4oT", [128, NK, TM], BF16)
            R_oTt = sc.res()
            hb = [self.sb(st, "p4hb%d" % i, [128, D], BF16) for i in range(3)]
            R_hb = [sc.res() for _ in range(3)]
            hT2 = [self.sb(st, "p4hT%d" % i, [128, NK, TM + 2], BF16) for i in range(2)]
            R_hT2 = [sc.res() for _ in range(2)]
            g = self.sb(st, "p4g", [128, NFF, TM], BF16)
            R_g = sc.res()
            junk = self.sb(st, "p4junk", [128, D], BF16)
            R_junk = sc.res()
            stats = [[self.sb(st, "p4st%d_%d" % (a, b), [128, 4], F32) for b in range(3)] for a in range(2)]
            R_stat = [sc.res() for _ in range(2)]
            t1 = [[self.sb(st, "p4t1_%d%d" % (a, b), [128, TM], F32) for b in range(2)] for a in range(2)]
            R_t1 = [[sc.res() for _ in range(2)] for _ in range(2)]
            t2 = [self.sb(st, "p4t2_%d" % a, [128, TM], F32) for a in range(2)]
            R_t2 = [sc.res() for _ in range(2)]
            sg = [self.sb(st, "p4sg%d" % i, [128, TM], F32) for i in range(2)]
            R_sg = [sc.res() for _ in range(2)]
            R_gj = [sc.res() for _ in range(NFF)]
            if last:
                gfin = self.sb(st, "p4gf", [128, D], F32)
                R_gf = sc.res()
                sc.dma("sp", gfin[:, :], self.gfin_d[0:1, :].partition_broadcast(128), sc.dsem("p4gf"), writes=[R_gf])
            psY = [self.ps(st, "p4Y%d" % i, [128, 512], F32) for i in range(2)]
            R_Y = [sc.pres() for _ in range(2)]
            psU = [[self.ps(st, "p4U%d%d" % (a, b), [128, 512], F32) for b in range(2)] for a in range(2)]
            R_U = [[sc.pres() for _ in range(2)] for _ in range(2)]
            psT = self.ps(st, "p4T", [128, NK, 128], BF16)
            R_T = sc.pres()
            cwl = lambda j_, ch: self.cw[:, (l * 3 + j_) * 44 + ch:(l * 3 + j_) * 44 + ch + 1]
            cbl = lambda ch: self.cb[:, l * 44 + ch:l * 44 + ch + 1]
            wctr = [0]
            uctr = [0]

            def rms_rstd(xb, nsub, eps, si_):
                ssq, rt, rstd = stats[si_]
                for s in range(nsub):
                    sc.op("act", lambda e, s=s, ssq=ssq: e.activation(out=junk[:, :], in_=xin[xb][:, s, :], func=AF.Square,
                                                                      accum_out=ssq[:, s:s + 1]),
                          reads=[R_x[xb]], writes=[R_junk, R_stat[si_]])
                sc.op("act", lambda e, ssq=ssq, rt=rt: e.activation(out=rt[:, 0:nsub], in_=ssq[:, 0:nsub], func=AF.Sqrt, scale=1.0 / D, bias=eps),
                      reads=[R_stat[si_]], writes=[R_stat[si_]])
                sc.op("dve", lambda e, rt=rt, rstd=rstd: e.reciprocal(out=rstd[:, 0:nsub], in_=rt[:, 0:nsub]), reads=[R_stat[si_]], writes=[R_stat[si_]])

            def load_tile(tj):
                t0_, T_ = tiles[tj]
                xb_ = tj % 2
                sc.dma("sp", oT[:, :, 0:T_], self.oT_s[:, t0_:t0_ + T_].rearrange("(k p) s -> p k s", p=128), sc.dsem("p4oT"),
                       reads=self.R_oT, writes=[R_oTt])
                sc.dma("sp", xin[xb_][:, 0:T_ // 128, :], xsrc[t0_:t0_ + T_, :].rearrange("(s p) d -> p s d", p=128), sc.dsem("p4x%d" % xb_),
                       reads=R_src[t0_ // 128:(t0_ + T_) // 128], writes=[R_x[xb_]])

            def pa_units(tj):
                t0_, T_ = tiles[tj]
                return [(s_, hf) for s_ in range(T_ // 128) for hf in range(2)]

            def pa_unit(tj, s, hf):
                xb_ = tj % 2
                for k in range(NK):
                    sc.op("pe", lambda e, s=s, hf=hf, k=k: e.matmul(
                        psY[hf][:, :], lhsT=oT[:, k, s * 128:(s + 1) * 128], rhs=wo[:, k, hf * 512:(hf + 1) * 512],
                        start=(k == 0), stop=(k == NK - 1)), reads=[R_oTt, R_wres], writes=[R_Y[hf]])
                sc.op("dve", lambda e, s=s, hf=hf, xb_=xb_: e.tensor_tensor(
                    out=xin[xb_][:, s, hf * 512:(hf + 1) * 512], in0=psY[hf][:, :], in1=xin[xb_][:, s, hf * 512:(hf + 1) * 512], op=ALU.add),
                    reads=[R_Y[hf], R_x[xb_]], writes=[R_x[xb_]])

            def pa_stats(tj):
                rms_rstd(tj % 2, tiles[tj][1] // 128, 1e-6, 1)

            def pb_scale(tj):
                t0_, T_ = tiles[tj]
                xb_ = tj % 2
                hi_ = tj % 2
                if tj == 0:
                    sc.op("dve", lambda e: e.memset(hT2[0][:, :, 0:2], 0.0), writes=[R_hT2[0]])
                else:
                    Tp = tiles[tj - 1][1]
                    sc.op("act", lambda e, hi_=hi_, Tp=Tp: e.activation(out=hT2[hi_][:, :, 0:2], in_=hT2[1 - hi_][:, :, Tp:Tp + 2], func=AF.Copy),
                          reads=[R_hT2[1 - hi_]], writes=[R_hT2[hi_]])
                for s in range(T_ // 128):
                    sc.op("dve", lambda e, s=s, xb_=xb_: e.tensor_scalar(
                        out=hb[s][:, :], in0=xin[xb_][:, s, :], scalar1=stats[1][2][:, s:s + 1], scalar2=None, op0=ALU.mult),
                        reads=[R_x[xb_], R_stat[1]], writes=[R_hb[s]])

            def pb_transpose(tj):
                t0_, T_ = tiles[tj]
                hi_ = tj % 2
                for s in range(T_ // 128):
                    for k in range(NK):
                        sc.op("pe", lambda e, s=s, k=k: e.transpose(out=psT[:, k, :], in_=hb[s][:, k * 128:(k + 1) * 128],
                                                                     identity=self.ident[:, :]),
                              reads=[R_hb[s], self.R_const], writes=[R_T])
                    sc.op("act", lambda e, s=s, hi_=hi_: e.activation(out=hT2[hi_][:, :, 2 + s * 128:2 + (s + 1) * 128], in_=psT[:, :, :], func=AF.Copy),
                          reads=[R_T], writes=[R_hT2[hi_]])

            def prologue_a(tj):
                for (s_, hf) in pa_units(tj):
                    pa_unit(tj, s_, hf)
                pa_stats(tj)

            def prologue_b(tj):
                pb_scale(tj)
                pb_transpose(tj)

            def load_wup(gi_):
                wi_ = wctr[0] % NWS
                wctr[0] += 1
                sc.dma("sp", wup[wi_][:, :, :, :, :], self.wup_s[l][:, 2 * gi_:2 * gi_ + 2, :, :, :], sc.dsem("p4wu%d" % wi_),
                       reads=self.R_wup[l], writes=[R_wu[wi_]])
                return wi_

            pref = {}
            load_tile(0)
            prologue_a(0)
            prologue_b(0)
            for ti, (t0, T) in enumerate(tiles):
                nsub = T // 128
                xb = ti % 2
                sub0 = t0 // 128
                hti = ti % 2
                nxt = ti + 1 < len(tiles)
                units = pa_units(ti + 1) if nxt else []
                for gi in range(NFF // 2):
                    if (ti, gi) in pref:
                        wi = pref[(ti, gi)]
                    else:
                        wi = load_wup(gi)
                    if gi == 1 and nxt:
                        load_tile(ti + 1)
                    for jj in range(2):
                        j = 2 * gi + jj
                        p_ = j - 6
                        if nxt and 0 <= p_ < len(units):
                            pa_unit(ti + 1, *units[p_])
                        if nxt and p_ == len(units):
                            pa_stats(ti + 1)
                        if nxt and p_ == len(units) + 2:
                            pb_scale(ti + 1)
                        if nxt and p_ == len(units) + 5:
                            pb_transpose(ti + 1)
                        ui = uctr[0] % 2
                        uctr[0] += 1
                        for gv in range(2):
                            for k in range(NK):
                                sc.op("pe", lambda e, ui=ui, gv=gv, k=k, wi=wi, jj=jj, T=T, hti=hti: e.matmul(
                                    psU[ui][gv][:, 0:T + 2], lhsT=wup[wi][:, jj, gv, k, :], rhs=hT2[hti][:, k, 0:T + 2],
                                    start=(k == 0), stop=(k == NK - 1)), reads=[R_wu[wi], R_hT2[hti]], writes=[R_U[ui][gv]])
                        for gv in range(2):
                            ch = gv * NFF + j
                            U = psU[ui][gv]
                            sc.op("act", lambda e, U=U, ui=ui, gv=gv, ch=ch, T=T: e.activation(
                                out=t1[ui][gv][:, 0:T], in_=U[:, 2:T + 2], func=AF.Identity, scale=cwl(2, ch), bias=cbl(ch)),
                                reads=[R_U[ui][gv], self.R_const], writes=[R_t1[ui][gv]])
                            sc.op("dve", lambda e, U=U, ui=ui, gv=gv, ch=ch, T=T: e.scalar_tensor_tensor(
                                out=t2[gv][:, 0:T], in0=U[:, 1:T + 1], scalar=cwl(1, ch), in1=t1[ui][gv][:, 0:T], op0=ALU.mult, op1=ALU.add),
                                reads=[R_U[ui][gv], R_t1[ui][gv], self.R_const], writes=[R_t2[gv]])
                            sc.op("dve", lambda e, U=U, ui=ui, gv=gv, ch=ch, T=T: e.scalar_tensor_tensor(
                                out=t1[ui][gv][:, 0:T], in0=U[:, 0:T], scalar=cwl(0, ch), in1=t2[gv][:, 0:T], op0=ALU.mult, op1=ALU.add),
                                reads=[R_U[ui][gv], R_t2[gv], self.R_const], writes=[R_t1[ui][gv]])
                        sgi = ui
                        sc.op("act", lambda e, ui=ui, T=T, sgi=sgi: e.activation(out=sg[sgi][:, 0:T], in_=t1[ui][0][:, 0:T], func=AF.Silu),
                              reads=[R_t1[ui][0]], writes=[R_sg[sgi]])
                        sc.op("pool", lambda e, ui=ui, j=j, T=T, sgi=sgi: e.tensor_tensor(out=g[:, j, 0:T], in0=sg[sgi][:, 0:T], in1=t1[ui][1][:, 0:T], op=ALU.mult),
                              reads=[R_sg[sgi], R_t1[ui][1]], writes=[R_gj[j]], pool_main=True)
                if nxt:
                    for g_ in range(2):
                        pref[(ti + 1, g_)] = load_wup(g_)
                for s in range(nsub):
                    for hf in range(2):
                        for j in range(NFF):
                            sc.op("pe", lambda e, s=s, hf=hf, j=j: e.matmul(
                                psY[hf][:, :], lhsT=g[:, j, s * 128:(s + 1) * 128], rhs=wdn[:, j, hf * 512:(hf + 1) * 512],
                                start=(j == 0), stop=(j == NFF - 1)), reads=[R_gj[j], R_wres], writes=[R_Y[hf]])
                        sc.op("dve", lambda e, s=s, hf=hf, xb=xb: e.tensor_tensor(
                            out=xin[xb][:, s, hf * 512:(hf + 1) * 512], in0=psY[hf][:, :], in1=xin[xb][:, s, hf * 512:(hf + 1) * 512], op=ALU.add),
                            reads=[R_Y[hf], R_x[xb]], writes=[R_x[xb]])
                if last:
                    rms_rstd(xb, nsub, 1e-6, 0)
                    for s in range(nsub):
                        sc.op("dve", lambda e, s=s, xb=xb: e.scalar_tensor_tensor(
                            out=xin[xb][:, s, :], in0=xin[xb][:, s, :], scalar=stats[0][2][:, s:s + 1], in1=gfin[:, :], op0=ALU.mult, op1=ALU.mult),
                            reads=[R_x[xb], R_stat[0], R_gf], writes=[R_x[xb]])
                sc.dma("sp", xdst[t0:t0 + T, :].rearrange("(s p) d -> p s d", p=128), xin[xb][:, 0:nsub, :], sc.dsem("p4st%d" % xb),
                       reads=[R_x[xb]], writes=R_dst[sub0:sub0 + nsub])


def build_program(S=SEQ, debug=False, nlayers=2, stop_after=None):
    nc = bass.Bass("TRN2", target_bir_lowering=False)
    b = Builder(nc, S, debug=debug, nlayers=nlayers)
    b.stop_after = stop_after
    b.build()
    return nc, b


def make_in_maps(inputs, S, ncores):
    f = lambda a: np.ascontiguousarray(np.asarray(a, dtype=np.float32))
    consts = host_constants(S)

    def pk(v):
        return np.ascontiguousarray(np.asarray(v, np.float32).reshape(8, 128).T)

    gin_t = np.concatenate([pk(inputs["even_norm"][0]), pk(inputs["odd_norm"][0])], axis=1)
    gffn_t = np.concatenate([pk(inputs["ffn_norm"][0]), pk(inputs["ffn_norm"][1])], axis=1)
    subl_t = f(inputs["odd_subln"][0]).reshape(128, 1)
    lamq = np.stack([f(inputs["odd_lambda_q1"][0]), f(inputs["odd_lambda_k1"][0]),
                     f(inputs["odd_lambda_q2"][0]), f(inputs["odd_lambda_k2"][0])], axis=0)
    cw = f(inputs["ffn_conv_w"])
    cw_t = np.ascontiguousarray(cw.reshape(2, 3, 44, 128).transpose(3, 0, 1, 2).reshape(128, 2 * 3 * 44))
    cb = f(inputs["ffn_conv_b"])
    cb_t = np.ascontiguousarray(cb.reshape(2, 44, 128).transpose(2, 0, 1).reshape(128, 2 * 44))
    shared = dict(
        w_in0=f(inputs["even_w_in"][0]), w_in1=f(inputs["odd_w_qkv"][0]),
        w_out0=f(inputs["even_w_out"][0]), w_out1=f(inputs["odd_w_out"][0]),
        w_up0=f(inputs["ffn_w_up"][0]), w_up1=f(inputs["ffn_w_up"][1]),
        w_dn0=f(inputs["ffn_w_down"][0]), w_dn1=f(inputs["ffn_w_down"][1]),
        gin_t=gin_t, gffn_t=gffn_t, subl_t=subl_t, lamq=lamq, cw_t=cw_t, cb_t=cb_t,
        gfin=f(inputs["final_norm"]).reshape(1, D), **consts)
    x = f(inputs["x"])
    in_maps = []
    for c in range(ncores):
        m = dict(shared)
        m["x"] = np.ascontiguousarray(x[c])
        in_maps.append(m)
    return in_maps


def kernel(**inputs):
    x = np.asarray(inputs["x"])
    Bn, S, _ = x.shape
    nc, _ = build_program(S)
    in_maps = make_in_maps(inputs, S, Bn)
    res = run_bass_kernel_spmd(nc, in_maps, core_ids=list(range(Bn)))
    out = np.stack([np.asarray(r["out"], dtype=np.float32) for r in res.results], axis=0)
    return out
```

```python
import math
from contextlib import ExitStack

import numpy as np
import ml_dtypes

import concourse.bass as bass
import concourse.mybir as mybir
from concourse.bass_utils import run_bass_kernel_spmd

F32 = mybir.dt.float32
BF16 = mybir.dt.bfloat16
AF = mybir.ActivationFunctionType
ALU = mybir.AluOpType
AX = mybir.AxisListType

D = 1024
DFF = 2816
NK = D // 128
NFF = DFF // 128
SEQ = 4096
NCORES = 8
LAMBDA_INIT = 0.8 - 0.6 * math.exp(-0.3 * 1)
MASKBIG = 30000.0
NEG = -1.0e30


class Res:
    __slots__ = ("w", "r", "excl")

    def __init__(self, excl=False):
        self.w = {}
        self.r = {}
        self.excl = excl


class DmaSem:
    def __init__(self, key, handle, main=True):
        self.key = key
        self.handle = handle
        self.count = 0
        self.main = main
        self.refs = set()


def _merge(d, s):
    for k, v in s.items():
        if d.get(k, 0) < v:
            d[k] = v


class Sched:
    CE = ("pe", "act", "dve")

    def __init__(self, nc, stack):
        self.nc = nc
        self.stack = stack
        self.eng = {"pe": nc.tensor, "act": nc.scalar, "dve": nc.vector, "pool": nc.gpsimd, "sp": nc.sync}
        self.epoch = {e: 0 for e in ("pe", "act", "dve", "pool")}
        self.cnt = {e: 0 for e in ("pe", "act", "dve", "pool")}
        self.semh = {}
        self.waited = {e: {} for e in self.eng}
        self.dsems = {}
        self.all_res = []
        self.n_inst = 0

    def res(self, excl=False):
        r = Res(excl)
        self.all_res.append(r)
        return r

    def pres(self):
        return self.res(excl=True)

    def _sem(self, key):
        h = self.semh.get(key)
        if h is None:
            h = self.stack.enter_context(self.nc.semaphore("s" + "_".join(str(x) for x in key)))
            self.semh[key] = h
        return h

    def ekey(self, e):
        return ("e", e, self.epoch[e])

    def dsem(self, name, main=True):
        d = self.dsems.get(name)
        if d is None:
            key = ("d", name)
            d = DmaSem(key, self._sem(key), main)
            self.dsems[name] = d
        return d

    def _wait(self, eng, deps):
        for k, v in deps.items():
            if eng == "pe" and k[0] == "e" and k[1] == "pe":
                continue
            if self.waited[eng].get(k, 0) >= v:
                continue
            self.waited[eng][k] = v
            self.eng[eng].wait_ge(self.semh[k], v)
            self.n_inst += 1

    def _deps(self, reads, writes):
        deps = {}
        for r in reads:
            _merge(deps, r.w)
        for w in writes:
            _merge(deps, w.w)
            _merge(deps, w.r)
        return deps

    def op(self, eng, fn, reads=(), writes=(), pool_main=False):
        if pool_main:
            self.pool_dirty = True
        deps = self._deps(reads, writes)
        for r in reads:
            if r.excl:
                for k, v in r.r.items():
                    if k[1] != eng and deps.get(k, 0) < v:
                        deps[k] = v
        self._wait(eng, deps)
        self.cnt[eng] += 1
        k = self.ekey(eng)
        v = self.cnt[eng]
        fn(self.eng[eng]).then_inc(self._sem(k), 1)
        self.n_inst += 1
        for r in reads:
            if r.r.get(k, 0) < v:
                r.r[k] = v
        for w in writes:
            w.w = {k: v}
            w.r = {}

    def dma(self, q, out, in_, dsem, reads=(), writes=()):
        self._wait(q, self._deps(reads, writes))
        dsem.count += 16
        self.eng[q].dma_start(out=out, in_=in_).then_inc(dsem.handle, 16)
        self.n_inst += 1
        k = dsem.key
        v = dsem.count
        for r in reads:
            if r.r.get(k, 0) < v:
                r.r[k] = v
            dsem.refs.add(r)
        for w in writes:
            w.w = {k: v}
            w.r = {}
            dsem.refs.add(w)
        dead = []
        for r in (dsem.refs if dsem.main else ()):
            hit = False
            if k in r.w:
                r.w[k] = v
                hit = True
            if k in r.r:
                r.r[k] = v
                hit = True
            if not hit:
                dead.append(r)
        for r in dead:
            dsem.refs.discard(r)

    def barrier(self, new_epoch=False):
        deps = {}
        for e in self.CE:
            if self.cnt[e] > 0:
                deps[self.ekey(e)] = self.cnt[e]
        for d in self.dsems.values():
            if d.main and d.count > 0:
                deps[d.key] = d.count
        if getattr(self, "pool_dirty", False):
            deps[self.ekey("pool")] = self.cnt["pool"]
            self.pool_dirty = False
        for e in ("pe", "act", "dve", "sp"):
            self._wait(e, deps)
        if new_epoch:
            old = set()
            for e in self.CE:
                old.add(self.ekey(e))
                self.epoch[e] += 1
                self.cnt[e] = 0
            for r in self.all_res:
                for k in list(r.w.keys()):
                    if k in old:
                        del r.w[k]
                for k in list(r.r.keys()):
                    if k in old:
                        del r.r[k]

    def final_wait(self, eng="sp"):
        deps = {}
        for e in ("pe", "act", "dve", "pool"):
            if self.cnt[e] > 0:
                deps[self.ekey(e)] = self.cnt[e]
        for d in self.dsems.values():
            if d.count > 0:
                deps[d.key] = d.count
        self._wait(eng, deps)


def host_constants(S):
    half = 32
    inv = (np.float32(1.0) / (np.float32(10000.0) ** (np.arange(half, dtype=np.float32) * np.float32(2.0) / np.float32(64)))).astype(np.float32)
    ang = (np.arange(S, dtype=np.float32)[:, None] * inv[None, :]).astype(np.float32)
    cos = np.cos(ang).astype(np.float32)
    sin = np.sin(ang).astype(np.float32)
    j = np.arange(128) % 32
    cosT = np.ascontiguousarray(cos[:, j].T)
    sinT = np.ascontiguousarray(sin[:, j].T)
    R = np.zeros((128, 128), np.float32)
    for m in range(128):
        d = m % 64
        base = m - d
        if d < 32:
            R[base + d + 32, m] = -1.0
        else:
            R[base + d - 32, m] = 1.0
    ident = np.eye(128, dtype=np.float32)
    n = np.arange(16)[:, None]
    kk = np.arange(S)[None, :]
    erows = (MASKBIG * ((kk // 256) == n)).astype(np.float32)
    ki = np.arange(128)[:, None]
    jj = np.arange(2048 + 384 + 512)[None, :]
    Dm = jj - 384 - ki
    c = ((Dm >= 0) & (Dm <= 128)).astype(np.float32)
    c += ((Dm >= 0) & (Dm <= 512) & (Dm % 4 == 0)).astype(np.float32)
    c += ((Dm >= 0) & (Dm <= 2048) & (Dm % 16 == 0)).astype(np.float32)
    qi = np.arange(128)[None, :]
    tri = (qi >= ki).astype(np.float32)
    bf = ml_dtypes.bfloat16
    return dict(cosT=cosT, sinT=sinT, rperm=R.astype(bf), ident=ident.astype(bf),
                erows=erows.astype(bf), strip_d=c.astype(bf), tri=tri.astype(bf))


class Builder:
    def __init__(self, nc, S, debug=False, nlayers=2):
        self.nc = nc
        self.S = S
        self.NT = S // 512
        self.NS = S // 128
        self.debug = debug
        self.nlayers = nlayers

    def dram_in(self, name, shape, dt):
        return self.nc.dram_tensor(name, list(shape), dt, kind="ExternalInput").ap()

    def dram_scr(self, name, shape, dt, dbg=False):
        kind = "ExternalOutput" if (dbg and self.debug) else "Internal"
        return self.nc.dram_tensor(name, list(shape), dt, kind=kind).ap()

    def _uid(self, name):
        self._n = getattr(self, "_n", 0) + 1
        return "t%d_%s" % (self._n, name)

    def sb(self, st, name, shape, dt):
        return st.enter_context(self.nc.sbuf_tensor(self._uid(name), list(shape), dt))

    def ps(self, st, name, shape, dt):
        return st.enter_context(self.nc.psum_tensor(self._uid(name), list(shape), dt))

    def build(self):
        nc, S = self.nc, self.S
        with ExitStack() as top:
            self.top = top
            self.sc = Sched(nc, top)
            self.declare_io()
            self.setup_constants()
            stop = getattr(self, "stop_after", None)
            if stop == "const":
                self.sc.final_wait("sp")
                return
            self.weight_prep()
            if stop == "prep":
                self.sc.final_wait("sp")
                return
            for l in range(self.nlayers):
                last = (l == self.nlayers - 1)
                xsrc = self.x if l == 0 else self.xs
                xdst = self.out if last else self.xs
                with ExitStack() as st12:
                    hT = self.sb(st12, "hT", [128, NK, S], BF16)
                    R_hT = [self.sc.res() for _ in range(self.NT)]
                    self.phase_norm(l, xsrc, hT, R_hT)
                    if stop == "norm":
                        self.sc.final_wait("sp")
                        return
                    self.phase_proj(l, hT, R_hT)
                    self.sc.barrier()
                if stop == "proj":
                    self.sc.final_wait("sp")
                    return
                self.phase_attn(l)
                if stop == "attn":
                    self.sc.final_wait("sp")
                    return
                self.sc.barrier(new_epoch=True)
                self.phase_ffn(l, xsrc, xdst, last)
                self.sc.barrier(new_epoch=True)
            self.sc.final_wait("sp")

    def declare_io(self):
        S = self.S
        sc = self.sc
        di = self.dram_in
        self.x = di("x", [S, D], F32)
        self.w_in = [di("w_in0", [D, 3 * D], F32), di("w_in1", [D, 3 * D], F32)]
        self.w_out = [di("w_out0", [D, D], F32), di("w_out1", [D, D], F32)]
        self.w_up = [di("w_up0", [D, 2 * DFF], F32), di("w_up1", [D, 2 * DFF], F32)]
        self.w_dn = [di("w_dn0", [DFF, D], F32), di("w_dn1", [DFF, D], F32)]
        self.gin_d = di("gin_t", [128, 16], F32)
        self.gffn_d = di("gffn_t", [128, 16], F32)
        self.subl_d = di("subl_t", [128, 1], F32)
        self.lamq_d = di("lamq", [4, 64], F32)
        self.cw_d = di("cw_t", [128, 2 * 3 * 44], F32)
        self.cb_d = di("cb_t", [128, 2 * 44], F32)
        self.gfin_d = di("gfin", [1, D], F32)
        self.cos_d = di("cosT", [128, S], F32)
        self.sin_d = di("sinT", [128, S], F32)
        self.rperm_d = di("rperm", [128, 128], BF16)
        self.ident_d = di("ident", [128, 128], BF16)
        self.erows_d = di("erows", [16, S], BF16)
        self.stripd_d = di("strip_d", [128, 2944], BF16)
        self.tri_d = di("tri", [128, 128], BF16)
        self.out = self.nc.dram_tensor("out", [S, D], F32, kind="ExternalOutput").ap()
        ds = self.dram_scr
        self.xs = ds("xs", [S, D], F32, dbg=True)
        self.wqkv_s = [ds("wqkv_s%d" % l, [128, NK, 3 * D], BF16) for l in range(2)]
        self.wo_s = [ds("wo_s%d" % l, [128, NK, D], BF16) for l in range(2)]
        self.wup_s = [ds("wup_s%d" % l, [128, NFF, 2, NK, 128], BF16) for l in range(2)]
        self.wdn_s = [ds("wdn_s%d" % l, [128, NFF, D], BF16) for l in range(2)]
        self.qaug_s = ds("qaug_s", [8, 80, S], BF16, dbg=True)
        self.kaug_s = ds("kaug_s", [8, 80, S], BF16, dbg=True)
        self.qT_s = ds("qT_s", [8, 128, S], BF16, dbg=True)
        self.kT_s = ds("kT_s", [8, 128, S], BF16, dbg=True)
        self.v_s = ds("v_s", [8, S, 256], BF16, dbg=True)
        self.oT_s = ds("oT_s", [D, S], BF16, dbg=True)
        self.R_x = [sc.res() for _ in range(self.NS)]
        self.R_xs = [sc.res() for _ in range(self.NS)]
        self.R_out = [sc.res() for _ in range(self.NS)]
        self.R_wqkv = [[[sc.res() for _ in range(NK)] for _ in range(3)] for _ in range(2)]
        self.R_wo = [[sc.res() for _ in range(NK)] for _ in range(2)]
        self.R_wup = [[sc.res() for _ in range(NK * 6)] for _ in range(2)]
        self.R_wdn = [[sc.res() for _ in range(NFF)] for _ in range(2)]
        self.R_qaug = [sc.res() for _ in range(8)]
        self.R_kaug = [sc.res() for _ in range(8)]
        self.R_qT = [sc.res() for _ in range(8)]
        self.R_kT = [sc.res() for _ in range(8)]
        self.R_v = [sc.res() for _ in range(8)]
        self.R_oT = [sc.res() for _ in range(8)]

    def setup_constants(self):
        sc, top = self.sc, self.top
        sb = self.sb
        self.ident = sb(top, "ident", [128, 128], BF16)
        self.rperm = sb(top, "rperm", [128, 128], BF16)
        self.gin = sb(top, "gin", [128, 16], F32)
        self.gffn = sb(top, "gffn", [128, 16], F32)
        self.subl = sb(top, "subl", [128, 1], F32)
        self.cw = sb(top, "cw", [128, 2 * 3 * 44], F32)
        self.cb = sb(top, "cb", [128, 2 * 44], F32)
        self.lamneg = sb(top, "lamneg", [128, 1], F32)
        lq = sb(top, "lq", [128, 4, 64], F32)
        lp = sb(top, "lp", [128, 2, 64], F32)
        ls = sb(top, "ls", [128, 2], F32)
        le = sb(top, "le", [128, 2], F32)
        self.R_const = sc.res()
        Rc = self.R_const
        dsm = sc.dsem("const")
        for dst, src in ((self.ident, self.ident_d), (self.rperm, self.rperm_d), (self.gin, self.gin_d),
                         (self.gffn, self.gffn_d), (self.subl, self.subl_d), (self.cw, self.cw_d),
                         (self.cb, self.cb_d)):
            sc.dma("sp", dst[:], src[:, :], dsm, writes=[Rc])
        for i in range(4):
            sc.dma("sp", lq[:, i, :], self.lamq_d[i:i + 1, :].partition_broadcast(128), dsm, writes=[Rc])
        for h in range(8):
            sc.dma("sp", self.kaug_s[h, 64:80, :], self.erows_d[:, :], sc.dsem("erow"), writes=[self.R_kaug[h]])
        Rl = sc.res()
        sc.op("dve", lambda e: e.tensor_tensor(out=lp[:, 0, :], in0=lq[:, 0, :], in1=lq[:, 1, :], op=ALU.mult), reads=[Rc], writes=[Rl])
        sc.op("dve", lambda e: e.tensor_tensor(out=lp[:, 1, :], in0=lq[:, 2, :], in1=lq[:, 3, :], op=ALU.mult), reads=[Rc, Rl], writes=[Rl])
        sc.op("dve", lambda e: e.tensor_reduce(out=ls[:, :], in_=lp[:, :, :], axis=AX.X, op=ALU.add), reads=[Rl], writes=[Rl])
        sc.op("act", lambda e: e.activation(out=le[:, :], in_=ls[:, :], func=AF.Exp), reads=[Rl], writes=[Rl])
        sc.op("dve", lambda e: e.tensor_tensor(out=ls[:, 0:1], in0=le[:, 1:2], in1=le[:, 0:1], op=ALU.subtract), reads=[Rl], writes=[Rl])
        sc.op("dve", lambda e: e.tensor_scalar(out=self.lamneg[:, :], in0=ls[:, 0:1], scalar1=-LAMBDA_INIT, scalar2=None, op0=ALU.add), reads=[Rl], writes=[Rl])
        self.R_lam = Rl

    def weight_prep(self):
        sc, top = self.sc, self.top
        NB = 3
        stf = [self.sb(top, "stf%d" % i, [128, 1024], F32) for i in range(NB)]
        stb = [self.sb(top, "stb%d" % i, [128, 1024], BF16) for i in range(NB)]
        Rf = [sc.res() for _ in range(NB)]
        Rb = [sc.res() for _ in range(NB)]
        units = []
        for l in range(self.nlayers):
            for k in range(NK):
                for b in range(3):
                    units.append((self.w_in[l][k * 128:(k + 1) * 128, b * 1024:(b + 1) * 1024], 1024,
                                  self.wqkv_s[l][:, k, b * 1024:(b + 1) * 1024], None,
                                  self.gin[:, l * 8 + k:l * 8 + k + 1], 1.0, self.R_wqkv[l][b][k]))
            for k in range(NK):
                s1 = self.subl[:, 0:1] if l == 1 else 1.0
                s2 = (1.0 - LAMBDA_INIT) if l == 1 else 1.0
                units.append((self.w_out[l][k * 128:(k + 1) * 128, :], 1024, self.wo_s[l][:, k, :], None,
                              s1, s2, self.R_wo[l][k]))
            for k in range(NK):
                ci = 0
                for isval in range(2):
                    for (c0, n) in ((0, 1024), (1024, 1024), (2048, 768)):
                        col = isval * DFF + c0
                        pj0 = c0 // 128
                        dst = self.wup_s[l][:, pj0:pj0 + n // 128, isval, k, :]
                        units.append((self.w_up[l][k * 128:(k + 1) * 128, col:col + n], n, dst, n // 128,
                                      self.gffn[:, l * 8 + k:l * 8 + k + 1], 1.0, self.R_wup[l][k * 6 + ci]))
                        ci += 1
            for j in range(NFF):
                units.append((self.w_dn[l][j * 128:(j + 1) * 128, :], 1024, self.wdn_s[l][:, j, :], None,
                              1.0, 1.0, self.R_wdn[l][j]))
        for i, (src, n, dst, nchunk, s1, s2, Rw) in enumerate(units):
            b = i % NB
            sc.dma("pool", stf[b][:, 0:n], src, sc.dsem("pl%d" % b, main=False), writes=[Rf[b]])
            rd = [Rf[b], self.R_const]
            sc.op("pool", lambda e, b=b, n=n, s1=s1, s2=s2: e.tensor_scalar(
                out=stb[b][:, 0:n], in0=stf[b][:, 0:n], scalar1=s1, scalar2=s2, op0=ALU.mult, op1=ALU.mult),
                reads=rd, writes=[Rb[b]])
            if nchunk is None:
                srcb = stb[b][:, 0:n]
            else:
                srcb = stb[b][:, 0:n].rearrange("p (c w) -> p c w", w=128)
            sc.dma("pool", dst, srcb, sc.dsem("ps%d" % b, main=False), reads=[Rb[b]], writes=[Rw])

    def phase_norm(self, l, xsrc, hT, R_hT):
        sc = self.sc
        R_src = self.R_x if l == 0 else self.R_xs
        with ExitStack() as st:
            xin = [self.sb(st, "p1x%d" % i, [128, 4, D], F32) for i in range(2)]
            hb = [self.sb(st, "p1h%d" % i, [128, D], BF16) for i in range(2)]
            junk = self.sb(st, "p1j", [128, D], BF16)
            ssq = self.sb(st, "p1ss", [128, self.NS], F32)
            rt = self.sb(st, "p1rt", [128, self.NS], F32)
            rstd = self.sb(st, "p1rs", [128, self.NS], F32)
            psT = [self.ps(st, "p1T%d" % i, [128, NK, 128], BF16) for i in range(2)]
            R_xin = [sc.res() for _ in range(2)]
            R_hb = [sc.res() for _ in range(2)]
            R_j = sc.res()
            R_st = [sc.res() for _ in range(self.NT)]
            R_psT = [sc.pres() for _ in range(2)]
            for g in range(self.NT):
                b = g % 2
                sc.dma("sp", xin[b][:, :, :], xsrc[g * 512:(g + 1) * 512, :].rearrange("(s p) d -> p s d", p=128),
                       sc.dsem("p1x%d" % b), reads=R_src[g * 4:(g + 1) * 4], writes=[R_xin[b]])
                for s in range(4):
                    sub = g * 4 + s
                    sc.op("act", lambda e, b=b, s=s, sub=sub: e.activation(
                        out=junk[:, :], in_=xin[b][:, s, :], func=AF.Square, accum_out=ssq[:, sub:sub + 1]),
                        reads=[R_xin[b]], writes=[R_j, R_st[g]])
                sc.op("act", lambda e, g=g: e.activation(out=rt[:, g * 4:(g + 1) * 4], in_=ssq[:, g * 4:(g + 1) * 4],
                                                         func=AF.Sqrt, scale=1.0 / D, bias=1e-6),
                      reads=[R_st[g]], writes=[R_st[g]])
                sc.op("dve", lambda e, g=g: e.reciprocal(out=rstd[:, g * 4:(g + 1) * 4], in_=rt[:, g * 4:(g + 1) * 4]),
                      reads=[R_st[g]], writes=[R_st[g]])
                for s in range(4):
                    sub = g * 4 + s
                    hbi = sub % 2
                    sc.op("dve", lambda e, b=b, s=s, sub=sub, hbi=hbi: e.tensor_scalar(
                        out=hb[hbi][:, :], in0=xin[b][:, s, :], scalar1=rstd[:, sub:sub + 1], scalar2=None, op0=ALU.mult),
                        reads=[R_xin[b], R_st[g]], writes=[R_hb[hbi]])
                    for k in range(NK):
                        sc.op("pe", lambda e, hbi=hbi, k=k: e.transpose(
                            out=psT[hbi][:, k, :], in_=hb[hbi][:, k * 128:(k + 1) * 128], identity=self.ident[:, :]),
                            reads=[R_hb[hbi], self.R_const], writes=[R_psT[hbi]])
                    sc.op("dve", lambda e, hbi=hbi, sub=sub: e.tensor_copy(
                        out=hT[:, :, sub * 128:(sub + 1) * 128], in_=psT[hbi][:, :, :]),
                        reads=[R_psT[hbi]], writes=[R_hT[g]])
            sc.barrier()

    def phase_proj(self, l, hT, R_hT):
        sc = self.sc
        S, NT, NS = self.S, self.NT, self.NS
        with ExitStack() as st:
            cosT = self.sb(st, "p2cos", [128, S], F32)
            sinT = self.sb(st, "p2sin", [128, S], F32)
            R_tab = sc.res()
            sc.dma("sp", cosT[:, :], self.cos_d[:, :], sc.dsem("p2tab"), writes=[R_tab])
            sc.dma("sp", sinT[:, :], self.sin_d[:, :], sc.dsem("p2tab"), writes=[R_tab])
            NW = 4
            wch = [self.sb(st, "p2w%d" % i, [128, NK, 128], BF16) for i in range(NW)]
            R_w = [sc.res() for _ in range(NW)]
            qb = [self.sb(st, "p2qb%d" % i, [128, 512], BF16) for i in range(2)]
            R_qb = [sc.res() for _ in range(2)]
            Af = [self.sb(st, "p2A%d" % i, [128, 512], F32) for i in range(2)]
            Bf = [self.sb(st, "p2B%d" % i, [128, 512], F32) for i in range(2)]
            R_A = [sc.res() for _ in range(2)]
            R_B = [sc.res() for _ in range(2)]
            tmpf = [self.sb(st, "p2t%d" % i, [128, 512], F32) for i in range(2)]
            R_tmp = [sc.res() for _ in range(2)]
            stg = [self.sb(st, "p2stg%d" % i, [128, S], BF16) for i in range(2)]
            R_stg = [sc.res() for _ in range(2)]
            vsb = [self.sb(st, "p2v%d" % i, [128, NS, 256 if l == 0 else 130], BF16) for i in range(2)]
            R_vsb = [sc.res() for _ in range(2)]
            ksum = self.sb(st, "p2ks", [128, 16], F32)
            R_ks = sc.res()
            kpad = self.sb(st, "p2kpad", [128, 2, 16], F32)
            R_kpad = sc.res()
            sc.op("dve", lambda e: e.memset(kpad[:, :, :], 0.0), writes=[R_kpad])
            gate = self.sb(st, "p2g", [128, 4, 2, 16], F32)
            top8 = self.sb(st, "p2t8", [128, 4, 2, 8], F32)
            nmp = self.sb(st, "p2nm", [128, 4, 128], BF16)
            sc.op("dve", lambda e: e.memset(nmp[:, :, :], 0.0), writes=[sc.res()])
            nm = nmp[:, :, 0:32].rearrange("p s (h n) -> p s h n", h=2)
            R_gate = sc.res()
            R_nm = sc.res()
            R_t8 = [[sc.res() for _ in range(2)] for _ in range(4)]
            R_nm2 = [[sc.res() for _ in range(2)] for _ in range(4)]
            nmT = self.sb(st, "p2nmT", [32, S], BF16)
            R_nmT = sc.res()
            ps1 = [self.ps(st, "p2ps1_%d" % i, [128, 512], F32) for i in range(2)]
            ps2 = [self.ps(st, "p2ps2_%d" % i, [128, 512], F32) for i in range(2)]
            psV = [self.ps(st, "p2psV%d" % i, [128, 4, 128], F32) for i in range(2)]
            psG = self.ps(st, "p2psG", [128, 512], F32)
            psN = self.ps(st, "p2psN", [128, 512], F32)
            R_ps1 = [sc.pres() for _ in range(2)]
            R_ps2 = [sc.pres() for _ in range(2)]
            R_psV = [sc.pres() for _ in range(2)]
            R_psG = sc.pres()
            R_psN = sc.pres()
            for i in range(2):
                if l == 0:
                    v4 = vsb[i][:, :, :].rearrange("p s (h e) -> p s h e", h=2)
                    sc.op("dve", lambda e, v4=v4: e.memset(v4[:, :, :, 64:128], 1.0), writes=[R_vsb[i]])
                else:
                    sc.op("dve", lambda e, i=i: e.memset(vsb[i][:, :, 128:129], 1.0), writes=[R_vsb[i]])

            wctr = [0]
            tctr = [0]
            sctr = [0]
            import os as _os
            dbg = _os.environ.get("KDBG", "")
            if "p2a" in dbg:
                return
            if "waitpool" in dbg:
                deps = {sc.ekey("pool"): sc.cnt["pool"]}
                for d_ in sc.dsems.values():
                    if d_.count > 0:
                        deps[d_.key] = d_.count
                for e_ in ("pe", "act", "dve", "sp"):
                    sc._wait(e_, deps)

            def load_w(blk, c):
                i = wctr[0] % NW
                wctr[0] += 1
                col = blk * 1024 + c * 128
                sc.dma("sp", wch[i][:, :, :], self.wqkv_s[l][:, :, col:col + 128], sc.dsem("p2w%d" % i),
                       reads=self.R_wqkv[l][blk], writes=[R_w[i]])
                return i

            def qk_chunk(blk, c, moba, wi):
                si = sctr[0] % 2
                sctr[0] += 1
                if moba and blk == 0:
                    sc.op("dve", lambda e: e.memset(gate[:, :, :, :], NEG), writes=[R_gate])
                    sc.op("dve", lambda e: e.tensor_copy(out=kpad[0:64, 0, :], in_=ksum[0:64, :]), reads=[R_ks], writes=[R_kpad])
                    sc.op("dve", lambda e: e.tensor_copy(out=kpad[64:128, 1, :], in_=ksum[64:128, :]), reads=[R_ks], writes=[R_kpad])
                if moba and blk == 1:
                    sc.op("dve", lambda e: e.memset(ksum[:, :], 0.0), writes=[R_ks])
                def stA(t):
                    i = t % 2
                    cols = slice(t * 512, (t + 1) * 512)
                    for k in range(NK):
                        sc.op("pe", lambda e, i=i, k=k, wi=wi, cols=cols: e.matmul(
                            ps1[i][:, :], lhsT=wch[wi][:, k, :], rhs=hT[:, k, cols], start=(k == 0), stop=(k == NK - 1)),
                            reads=[R_w[wi], R_hT[t]], writes=[R_ps1[i]])
                    sc.op("act", lambda e, i=i: e.activation(out=qb[i][:, :], in_=ps1[i][:, :], func=AF.Copy),
                          reads=[R_ps1[i]], writes=[R_qb[i]])

                def stB(t):
                    i = t % 2
                    cols = slice(t * 512, (t + 1) * 512)
                    sc.op("pe", lambda e, i=i: e.matmul(ps2[i][:, :], lhsT=self.rperm[:, :], rhs=qb[i][:, :], start=True, stop=True),
                          reads=[R_qb[i], self.R_const], writes=[R_ps2[i]])
                    sc.op("dve", lambda e, i=i, cols=cols: e.tensor_tensor(out=Af[i][:, :], in0=ps1[i][:, :], in1=cosT[:, cols], op=ALU.mult),
                          reads=[R_ps1[i], R_tab], writes=[R_A[i]])
                    sc.op("dve", lambda e, i=i, cols=cols: e.tensor_tensor(out=Bf[i][:, :], in0=ps2[i][:, :], in1=sinT[:, cols], op=ALU.mult),
                          reads=[R_ps2[i], R_tab], writes=[R_B[i]])
                    if not moba:
                        eng_ = "pool" if l == 1 else "dve"
                        sc.op(eng_, lambda e, i=i, cols=cols: e.tensor_tensor(out=stg[si][:, cols], in0=Af[i][:, :], in1=Bf[i][:, :], op=ALU.add),
                              reads=[R_A[i], R_B[i]], writes=[R_stg[si]], pool_main=(l == 1))
                        return
                    sc.op("dve", lambda e, i=i: e.tensor_tensor(out=tmpf[i][:, :], in0=Af[i][:, :], in1=Bf[i][:, :], op=ALU.add),
                          reads=[R_A[i], R_B[i]], writes=[R_tmp[i]])
                    sc.op("act", lambda e, i=i, cols=cols: e.activation(out=stg[si][:, cols], in_=tmpf[i][:, :], func=AF.Copy),
                          reads=[R_tmp[i]], writes=[R_stg[si]])
                    if blk == 1:
                        sc.op("dve", lambda e, i=i, t=t: e.tensor_reduce(
                            out=ksum[:, 2 * t:2 * t + 2], in_=tmpf[i][:, :].rearrange("p (b w) -> p b w", w=256), axis=AX.X, op=ALU.add),
                            reads=[R_tmp[i]], writes=[R_ks])

                def stC(t):
                    i = t % 2
                    for s_ in range(4):
                        for h in range(2):
                            sc.op("pe", lambda e, i=i, s_=s_, h=h: e.matmul(
                                psG[:, (s_ * 2 + h) * 16:(s_ * 2 + h) * 16 + 16], lhsT=tmpf[i][:, s_ * 128:(s_ + 1) * 128],
                                rhs=kpad[:, h, :], start=True, stop=True),
                                reads=[R_tmp[i], R_kpad], writes=[R_psG])
                    psG4 = psG[:, 0:128].rearrange("p (s h n) -> p s h n", s=4, h=2)
                    sc.op("dve", lambda e: e.memset(nm[:, :, :, :], -1.0),
                          writes=[R_nm] + [R_nm2[a][b_] for a in range(4) for b_ in range(2)])
                    for half in range(2):
                        b = 2 * t + half
                        ss = slice(2 * half, 2 * half + 2)
                        sc.op("dve", lambda e, ss=ss, b=b: e.memset(nm[:, ss, :, b:b + 1], 0.0), writes=[R_nm])
                        if b > 0:
                            sc.op("dve", lambda e, ss=ss, b=b, psG4=psG4: e.tensor_copy(out=gate[:, ss, :, 0:b], in_=psG4[:, ss, :, 0:b]),
                                  reads=[R_psG], writes=[R_gate])
                    for s_ in range(4):
                        b = 2 * t + s_ // 2
                        if b == 0:
                            continue
                        for h in range(2):
                            sc.op("dve", lambda e, s_=s_, h=h: e.max(out=top8[:, s_, h, :], in_=gate[:, s_, h, :]),
                                  reads=[R_gate], writes=[R_t8[s_][h]])
                            sc.op("dve", lambda e, s_=s_, h=h, b=b: e.tensor_scalar(
                                out=nm[:, s_, h, 0:b], in0=gate[:, s_, h, 0:b], scalar1=top8[:, s_, h, 2:3], scalar2=1.0,
                                op0=ALU.is_ge, op1=ALU.subtract), reads=[R_gate, R_t8[s_][h], R_nm], writes=[R_nm2[s_][h]])

                def stD(t):
                    cols = slice(t * 512, (t + 1) * 512)
                    allnm = [R_nm] + [R_nm2[a][b_] for a in range(4) for b_ in range(2)]
                    for s_ in range(4):
                        sc.op("pe", lambda e, s_=s_: e.matmul(
                            psN[:, s_ * 128:(s_ + 1) * 128], lhsT=nmp[:, s_, :], rhs=self.ident[:, :],
                            start=True, stop=True), reads=allnm + [self.R_const], writes=[R_psN])
                    sc.op("act", lambda e, cols=cols: e.activation(out=nmT[:, cols], in_=psN[0:32, :], func=AF.Copy),
                          reads=[R_psN], writes=[R_nmT])

                if moba and "mobaseq" in dbg:
                    stages = [(stA, 0), (stB, 0)]
                    if blk == 0:
                        stages += [(stC, 0), (stD, 0)]
                else:
                    stages = [(stA, 0), (stB, 1)]
                    if moba and blk == 0:
                        stages += [(stC, 2), (stD, 3)]
                for step in range(NT + len(stages) - 1):
                    for fn_, lag in reversed(stages):
                        t = step - lag
                        if 0 <= t < NT:
                            fn_(t)
                if "nostore" in dbg:
                    return
                ds_ = sc.dsem("p2st%d" % si)
                if moba:
                    dstt = self.qaug_s if blk == 0 else self.kaug_s
                    Rd = self.R_qaug if blk == 0 else self.R_kaug
                    for h in range(2):
                        sc.dma("sp", dstt[2 * c + h, 0:64, :], stg[si][h * 64:(h + 1) * 64, :], ds_, reads=[R_stg[si]], writes=[Rd[2 * c + h]])
                    if blk == 0:
                        for h in range(2):
                            sc.dma("sp", self.qaug_s[2 * c + h, 64:80, :], nmT[h * 16:(h + 1) * 16, :], sc.dsem("p2nm"),
                                   reads=[R_nmT], writes=[self.R_qaug[2 * c + h]])
                else:
                    dstt = self.qT_s if blk == 0 else self.kT_s
                    Rd = self.R_qT if blk == 0 else self.R_kT
                    sc.dma("sp", dstt[c, :, :], stg[si][:, :], ds_, reads=[R_stg[si]], writes=[Rd[c]])

            vctr = [0]

            def v_chunk(c, wi):
                vi = vctr[0] % 2
                vctr[0] += 1
                for g in range(NS // 4):
                    pi = g % 2
                    for s in range(4):
                        sub = g * 4 + s
                        for k in range(NK):
                            sc.op("pe", lambda e, pi=pi, s=s, k=k, sub=sub, wi=wi: e.matmul(
                                psV[pi][:, s, :], lhsT=hT[:, k, sub * 128:(sub + 1) * 128], rhs=wch[wi][:, k, :],
                                start=(k == 0), stop=(k == NK - 1)),
                                reads=[R_hT[sub // 4], R_w[wi]], writes=[R_psV[pi]])
                    if l == 0:
                        dst = vsb[vi][:, g * 4:(g + 1) * 4, :].rearrange("p s (h e) -> p s h e", h=2)[:, :, :, 0:64]
                        src = psV[pi][:, :, :].rearrange("p s (h d) -> p s h d", h=2)
                    else:
                        dst = vsb[vi][:, g * 4:(g + 1) * 4, 0:128]
                        src = psV[pi][:, :, :]
                    sc.op("act", lambda e, dst=dst, src=src: e.activation(out=dst, in_=src, func=AF.Copy),
                          reads=[R_psV[pi]], writes=[R_vsb[vi]])
                vd = self.v_s[c].rearrange("(s p) e -> p s e", p=128)
                ve = 256 if l == 0 else 129
                for q4 in range(0, NS, 8):
                    sc.dma("sp", vd[:, q4:q4 + 8, 0:ve], vsb[vi][:, q4:q4 + 8, 0:ve], sc.dsem("p2vst%d" % vi),
                           reads=[R_vsb[vi]], writes=[self.R_v[c]])

            tasks = []
            for c in range(8):
                tasks += [(1, c), (0, c), (2, c)]
            wi_of = {}
            for i_ in range(3):
                wi_of[i_] = load_w(*tasks[i_])
            for i_, (blk_, c) in enumerate(tasks):
                if i_ + 3 < len(tasks):
                    wi_of[i_ + 3] = load_w(*tasks[i_ + 3])
                moba = (l == 0 and c < 4)
                if blk_ == 2:
                    v_chunk(c, wi_of[i_])
                else:
                    qk_chunk(blk_, c, moba, wi_of[i_])

    def phase_attn(self, l):
        sc = self.sc
        S, NT, NS = self.S, self.NT, self.NS
        with ExitStack() as st:
            NSET = 2
            qa = [self.sb(st, "p3qa%d" % i, [128, S], BF16) for i in range(NSET)]
            ka = [self.sb(st, "p3ka%d" % i, [128, S], BF16) for i in range(NSET)]
            if l == 0:
                qbt = [self.sb(st, "p3qb%d" % i, [128, S], BF16) for i in range(NSET)]
            kbt = [self.sb(st, "p3kb%d" % i, [128, S], BF16) for i in range(NSET)]
            vt = [self.sb(st, "p3v%d" % i, [128, NS, 256 if l == 0 else 130], BF16) for i in range(NSET)]
            R_in = [sc.res() for _ in range(NSET)]
            if l == 1:
                osb = [self.sb(st, "p3o%d" % i, [128, NS, 128], BF16) for i in range(2)]
            R_osb = [sc.res() for _ in range(2)]
            otsb = [self.sb(st, "p3ot%d" % i, [128, S], BF16) for i in range(2)]
            R_ot = [sc.res() for _ in range(2)]
            NPT = 5
            pt = [self.sb(st, "p3pt%d" % i, [128, 512], BF16) for i in range(NPT)]
            R_pt = [sc.res() for _ in range(NPT)]
            tri = self.sb(st, "p3tri", [128, 128], BF16)
            R_msk = sc.res()
            sc.dma("sp", tri[:, :], self.tri_d[:, :], sc.dsem("p3msk"), writes=[R_msk])
            if l == 0:
                strip = self.sb(st, "p3strip", [128, 2944], BF16)
                sc.dma("sp", strip[:, :], self.stripd_d[:, :], sc.dsem("p3msk"), writes=[R_msk])
                rec = [self.sb(st, "p3rec%d" % i, [64, 512], F32) for i in range(2)]
                R_rec = [sc.res() for _ in range(2)]
            else:
                rec1 = self.sb(st, "p3r1", [128, 2, 2, 1], F32)
                rec2 = self.sb(st, "p3r2", [128, 2, 2, 1], F32)
                tb = [self.sb(st, "p3tb%d" % i, [128, 128], F32) for i in range(2)]
                R_tb = [sc.res() for _ in range(2)]
                ob = [self.sb(st, "p3ob%d" % i, [128, 4, 128], F32) for i in range(2)]
                R_ob = [sc.res() for _ in range(2)]
                junk = self.sb(st, "p3junk", [128, 128], BF16)
                R_junk = sc.res()
                ssd = self.sb(st, "p3ssd", [128, 4], F32)
                rtd = self.sb(st, "p3rtd", [128, 4], F32)
                rsd = self.sb(st, "p3rsd", [128, 4], F32)
                R_fin = sc.res()
            NSB = 3
            psS = [self.ps(st, "p3S%d" % i, [128, 512], F32) for i in range(NSB)]
            R_S = [sc.pres() for _ in range(NSB)]
            psO = [self.ps(st, "p3O%d" % i, [128, 512], F32) for i in range(4)]
            R_O = [sc.pres() for _ in range(4)]
            psT = self.ps(st, "p3T", [128, 8, 128], BF16)
            R_T = sc.pres()

            zeroed = set()

            def load_chunk(c):
                i = c % NSET
                R = R_in[i]
                dq = sc.dsem("p3in%d" % i)
                if l == 0 and c < 4:
                    sc.dma("sp", qa[i][0:80, :], self.qaug_s[2 * c, :, :], dq, reads=[self.R_qaug[2 * c]], writes=[R])
                    sc.dma("sp", qbt[i][0:80, :], self.qaug_s[2 * c + 1, :, :], dq, reads=[self.R_qaug[2 * c + 1]], writes=[R])
                    sc.dma("sp", ka[i][0:80, :], self.kaug_s[2 * c, :, :], dq, reads=[self.R_kaug[2 * c]], writes=[R])
                    sc.dma("sp", kbt[i][0:80, :], self.kaug_s[2 * c + 1, :, :], dq, reads=[self.R_kaug[2 * c + 1]], writes=[R])
                else:
                    if i not in zeroed:
                        zeroed.add(i)
                        sc.op("dve", lambda e, i=i: e.memset(ka[i][64:128, :], 0.0), writes=[R])
                        sc.op("dve", lambda e, i=i: e.memset(kbt[i][0:64, :], 0.0), writes=[R])
                    sc.dma("sp", qa[i][:, :], self.qT_s[c, :, :], dq, reads=[self.R_qT[c]], writes=[R])
                    sc.dma("sp", ka[i][0:64, :], self.kT_s[c, 0:64, :], dq, reads=[self.R_kT[c]], writes=[R])
                    sc.dma("sp", kbt[i][64:128, :], self.kT_s[c, 64:128, :], dq, reads=[self.R_kT[c]], writes=[R])
                vd = self.v_s[c].rearrange("(s p) e -> p s e", p=128)
                ve = 256 if l == 0 else 129
                for q4 in range(0, NS, 8):
                    sc.dma("sp", vt[i][:, q4:q4 + 8, 0:ve], vd[:, q4:q4 + 8, 0:ve], dq, reads=[self.R_v[c]], writes=[R])

            sctr = [0]
            pctr = [0]
            mctr = [0]

            def run_chunk(c):
                i = c % NSET
                oi = c % 2
                Rin = R_in[i]
                if l == 0 and c < 4:
                    kind = "moba"
                    maps = [dict(q=qa[i], k=ka[i], r0=0, r1=80, h=0), dict(q=qbt[i], k=kbt[i], r0=0, r1=80, h=1)]
                elif l == 0:
                    kind = "dil"
                    maps = [dict(q=qa[i], k=ka[i], r0=0, r1=128, h=0), dict(q=qa[i], k=kbt[i], r0=0, r1=128, h=1)]
                else:
                    kind = "causal"
                    maps = [dict(q=qa[i], k=ka[i], r0=0, r1=128, h=0), dict(q=qa[i], k=kbt[i], r0=0, r1=128, h=1)]
                if l == 0:
                    v4 = vt[i][:, :, :].rearrange("p s (h e) -> p s h e", h=2)
                steps = []
                for qt in range(NT):
                    for m in range(2):
                        if kind == "dil":
                            kts = list(range(max(0, 4 * qt - 16), 4 * qt + 4))
                        else:
                            kts = list(range(0, 4 * qt + 4))
                        for kt in kts:
                            steps.append((qt, m, kt, kt == kts[0], kt == kts[-1]))
                started = {}
                pendq = []
                LAG = 2

                def emit_pv(stp):
                    qt, m, kt, first, last, pi, lo, hi = stp
                    mp = maps[m]
                    if l == 0:
                        bank = m + 2 * (qt % 2)
                        key = (bank, qt, m)
                        stt = key not in started
                        started[key] = True
                        sc.op("pe", lambda e, bank=bank, kt=kt, h=mp["h"], pi=pi, lo=lo, hi=hi, stt=stt, last=last: e.matmul(
                            psO[bank][:, lo:hi], lhsT=v4[:, kt, h, :], rhs=pt[pi][:, lo:hi], start=stt, stop=last, skip_group_check=True),
                            reads=[R_pt[pi], Rin], writes=[R_O[bank]])
                        if last:
                            finalize(qt, m)
                        return
                    for qs in range(lo // 128, hi // 128):
                        if l == 0:
                            bank = m + 2 * (qt % 2)
                            outap = psO[bank][:, qs * 65:(qs + 1) * 65]
                            rhs = v4[:, kt, mp["h"], :]
                        else:
                            bank = 2 * m + qs // 2
                            outap = psO[bank][:, (qs % 2) * 129:(qs % 2) * 129 + 129]
                            rhs = vt[i][:, kt, 0:129]
                        key = (bank, qt, m)
                        stt = key not in started
                        started[key] = True
                        sc.op("pe", lambda e, outap=outap, rhs=rhs, pi=pi, qs=qs, stt=stt, last=last: e.matmul(
                            outap, lhsT=pt[pi][:, qs * 128:(qs + 1) * 128], rhs=rhs, start=stt, stop=last, skip_group_check=True),
                            reads=[R_pt[pi], Rin], writes=[R_O[bank]])
                    if last:
                        finalize(qt, m)

                def finalize(qt, m):
                    if l == 0:
                        bank = m + 2 * (qt % 2)
                        ri = (qt * 2 + m) % 2
                        sc.op("dve", lambda e, bank=bank, ri=ri: e.reciprocal(out=rec[ri][0:64, :], in_=psO[bank][64:128, :]),
                              reads=[R_O[bank]], writes=[R_rec[ri]])
                        sc.op("dve", lambda e, bank=bank, ri=ri, m=m, qt=qt: e.tensor_tensor(
                            out=otsb[oi][m * 64:(m + 1) * 64, qt * 512:(qt + 1) * 512], in0=psO[bank][0:64, :], in1=rec[ri][0:64, :],
                            op=ALU.mult), reads=[R_O[bank], R_rec[ri]], writes=[R_ot[oi]])
                    else:
                        if m == 0:
                            return
                        for bb in range(2):
                            O1 = psO[bb][:, 0:258].rearrange("p (q e) -> p q e", e=129)
                            O2 = psO[2 + bb][:, 0:258].rearrange("p (q e) -> p q e", e=129)
                            sc.op("dve", lambda e, O1=O1, bb=bb: e.reciprocal(out=rec1[:, bb, :, :], in_=O1[:, :, 128:129]),
                                  reads=[R_O[bb]], writes=[R_fin])
                            sc.op("dve", lambda e, O2=O2, bb=bb: e.reciprocal(out=rec2[:, bb, :, :], in_=O2[:, :, 128:129]),
                                  reads=[R_O[2 + bb]], writes=[R_fin])
                        sc.op("dve", lambda e: e.tensor_scalar(out=rec2[:, :, :, :], in0=rec2[:, :, :, :], scalar1=self.lamneg[:, 0:1],
                                                               scalar2=None, op0=ALU.mult), reads=[R_fin, self.R_lam], writes=[R_fin])
                        obi = qt % 2
                        for qs in range(4):
                            bb, ii = qs // 2, qs % 2
                            O1 = psO[bb][:, 0:258].rearrange("p (q e) -> p q e", e=129)
                            O2 = psO[2 + bb][:, 0:258].rearrange("p (q e) -> p q e", e=129)
                            ti = qs % 2
                            sc.op("act", lambda e, O2=O2, ii=ii, bb=bb, ti=ti: e.activation(
                                out=tb[ti][:, :], in_=O2[:, ii, 0:128], func=AF.Copy, scale=rec2[:, bb, ii, 0:1]),
                                reads=[R_O[2 + bb], R_fin], writes=[R_tb[ti]])
                            sc.op("dve", lambda e, O1=O1, ii=ii, bb=bb, ti=ti, obi=obi, qs=qs: e.scalar_tensor_tensor(
                                out=ob[obi][:, qs, :], in0=O1[:, ii, 0:128], scalar=rec1[:, bb, ii, 0:1], in1=tb[ti][:, :],
                                op0=ALU.mult, op1=ALU.add), reads=[R_O[bb], R_fin, R_tb[ti]], writes=[R_ob[obi]])
                            sc.op("act", lambda e, obi=obi, qs=qs: e.activation(
                                out=junk[:, :], in_=ob[obi][:, qs, :], func=AF.Square, accum_out=ssd[:, qs:qs + 1]),
                                reads=[R_ob[obi]], writes=[R_junk, R_fin])
                        sc.op("act", lambda e: e.activation(out=rtd[:, :], in_=ssd[:, :], func=AF.Sqrt, scale=1.0 / 128, bias=1e-5),
                              reads=[R_fin], writes=[R_fin])
                        sc.op("dve", lambda e: e.reciprocal(out=rsd[:, :], in_=rtd[:, :]), reads=[R_fin], writes=[R_fin])
                        for qs in range(4):
                            sub = qt * 4 + qs
                            sc.op("dve", lambda e, obi=obi, qs=qs, sub=sub: e.tensor_scalar(
                                out=osb[oi][:, sub, :], in0=ob[obi][:, qs, :], scalar1=rsd[:, qs:qs + 1], scalar2=None, op0=ALU.mult),
                                reads=[R_ob[obi], R_fin], writes=[R_osb[oi]])
                        transposes(qt)

                def transposes(qt):
                    for qs in range(4):
                        sub = qt * 4 + qs
                        sc.op("pe", lambda e, qs=qs, sub=sub: e.transpose(out=psT[:, qs, :], in_=osb[oi][:, sub, :], identity=self.ident[:, :]),
                              reads=[R_osb[oi], self.R_const], writes=[R_T])
                    sc.op("act", lambda e, qt=qt: e.activation(
                        out=otsb[oi][:, qt * 512:(qt + 1) * 512].rearrange("p (s w) -> p s w", w=128), in_=psT[:, 0:4, :], func=AF.Copy),
                        reads=[R_T], writes=[R_ot[oi]])

                for (qt, m, kt, first, last) in steps:
                    mp = maps[m]
                    delta = (4 * qt - kt) * 128
                    lo = max(0, -delta)
                    hi = 512
                    if kind == "dil":
                        hi = min(512, 2048 - delta + 128)
                    si = sctr[0] % NSB
                    sctr[0] += 1
                    pi = pctr[0] % NPT
                    pctr[0] += 1
                    r0, r1 = mp["r0"], mp["r1"]
                    sc.op("pe", lambda e, si=si, mp=mp, r0=r0, r1=r1, kt=kt, qt=qt: e.matmul(
                        psS[si][:, :], lhsT=mp["k"][r0:r1, kt * 128:(kt + 1) * 128], rhs=mp["q"][r0:r1, qt * 512:(qt + 1) * 512],
                        start=True, stop=True), reads=[Rin], writes=[R_S[si]])
                    if len(pendq) >= LAG:
                        emit_pv(pendq.pop(0))
                    sc.op("act", lambda e, si=si, pi=pi, lo=lo, hi=hi: e.activation(
                        out=pt[pi][:, lo:hi], in_=psS[si][:, lo:hi], func=AF.Exp, scale=0.125),
                        reads=[R_S[si]], writes=[R_pt[pi]])
                    if kind == "dil":
                        o0 = delta + 384
                        mctr[0] += 1
                        on_pool = False
                        sc.op("pool" if on_pool else "dve", lambda e, pi=pi, lo=lo, hi=hi, o0=o0: e.tensor_tensor(
                            out=pt[pi][:, lo:hi], in0=pt[pi][:, lo:hi], in1=strip[:, o0 + lo:o0 + hi], op=ALU.mult),
                            reads=[R_pt[pi], R_msk], writes=[R_pt[pi]], pool_main=on_pool)
                    elif delta <= 0:
                        sc.op("dve", lambda e, pi=pi, lo=lo: e.tensor_tensor(
                            out=pt[pi][:, lo:lo + 128], in0=pt[pi][:, lo:lo + 128], in1=tri[:, :], op=ALU.mult),
                            reads=[R_pt[pi], R_msk], writes=[R_pt[pi]])
                    pendq.append((qt, m, kt, first, last, pi, lo, hi))
                while pendq:
                    emit_pv(pendq.pop(0))
                sc.dma("sp", self.oT_s[c * 128:(c + 1) * 128, :], otsb[oi][:, :], sc.dsem("p3ost%d" % oi),
                       reads=[R_ot[oi]], writes=[self.R_oT[c]])

            load_chunk(0)
            for c in range(8):
                if c + 1 < 8:
                    load_chunk(c + 1)
                run_chunk(c)

    def phase_ffn(self, l, xsrc, xdst, last):
        sc = self.sc
        S = self.S
        R_src = self.R_x if l == 0 else self.R_xs
        R_dst = self.R_out if last else self.R_xs
        tiles = []
        t0 = 0
        while t0 < S:
            T = min(384, S - t0)
            tiles.append((t0, T))
            t0 += T
        TM = 384
        with ExitStack() as st:
            wo = self.sb(st, "p4wo", [128, NK, D], BF16)
            wdn = self.sb(st, "p4wdn", [128, NFF, D], BF16)
            R_wres = sc.res()
            dw = sc.dsem("p4w")
            for k in range(NK):
                sc.dma("sp", wo[:, k, :], self.wo_s[l][:, k, :], dw, reads=[self.R_wo[l][k]], writes=[R_wres])
            for j in range(NFF):
                sc.dma("sp", wdn[:, j, :], self.wdn_s[l][:, j, :], dw, reads=[self.R_wdn[l][j]], writes=[R_wres])
            NWS = 3
            wup = [self.sb(st, "p4wu%d" % i, [128, 2, 2, NK, 128], BF16) for i in range(NWS)]
            R_wu = [sc.res() for _ in range(NWS)]
            xin = [self.sb(st, "p4x%d" % i, [128, 3, D], F32) for i in range(2)]
            R_x = [sc.res() for _ in range(2)]
            oT = self.sb(st, "p4oT", [128, NK, TM], BF16)
            R_oTt = sc.res()
            hb = [self.sb(st, "p4hb%d" % i, [128, D], BF16) for i in range(3)]
            R_hb = [sc.res() for _ in range(3)]
            hT2 = [self.sb(st, "p4hT%d" % i, [128, NK, TM + 2], BF16) for i in range(2)]
            R_hT2 = [sc.res() for _ in range(2)]
            g = self.sb(st, "p4g", [128, NFF, TM], BF16)
            R_g = sc.res()
            junk = self.sb(st, "p4junk", [128, D], BF16)
            R_junk = sc.res()
            stats = [[self.sb(st, "p4st%d_%d" % (a, b), [128, 4], F32) for b in range(3)] for a in range(2)]
            R_stat = [sc.res() for _ in range(2)]
            t1 = [[self.sb(st, "p4t1_%d%d" % (a, b), [128, TM], F32) for b in range(2)] for a in range(2)]
            R_t1 = [[sc.res() for _ in range(2)] for _ in range(2)]
            t2 = [self.sb(st, "p4t2_%d" % a, [128, TM], F32) for a in range(2)]
            R_t2 = [sc.res() for _ in range(2)]
            sg = [self.sb(st, "p4sg%d" % i, [128, TM], F32) for i in range(2)]
            R_sg = [sc.res() for _ in range(2)]
            R_gj = [sc.res() for _ in range(NFF)]
            if last:
                gfin = self.sb(st, "p4gf", [128, D], F32)
                R_gf = sc.res()
                sc.dma("sp", gfin[:, :], self.gfin_d[0:1, :].partition_broadcast(128), sc.dsem("p4gf"), writes=[R_gf])
            psY = [self.ps(st, "p4Y%d" % i, [128, 512], F32) for i in range(2)]
            R_Y = [sc.pres() for _ in range(2)]
            psU = [[self.ps(st, "p4U%d%d" % (a, b), [128, 512], F32) for b in range(2)] for a in range(2)]
            R_U = [[sc.pres() for _ in range(2)] for _ in range(2)]
            psT = self.ps(st, "p4T", [128, NK, 128], BF16)
            R_T = sc.pres()
            cwl = lambda j_, ch: self.cw[:, (l * 3 + j_) * 44 + ch:(l * 3 + j_) * 44 + ch + 1]
            cbl = lambda ch: self.cb[:, l * 44 + ch:l * 44 + ch + 1]
            wctr = [0]
            uctr = [0]

            def rms_rstd(xb, nsub, eps, si_):
                ssq, rt, rstd = stats[si_]
                for s in range(nsub):
                    sc.op("act", lambda e, s=s, ssq=ssq: e.activation(out=junk[:, :], in_=xin[xb][:, s, :], func=AF.Square,
                                                                      accum_out=ssq[:, s:s + 1]),
                          reads=[R_x[xb]], writes=[R_junk, R_stat[si_]])
                sc.op("act", lambda e, ssq=ssq, rt=rt: e.activation(out=rt[:, 0:nsub], in_=ssq[:, 0:nsub], func=AF.Sqrt, scale=1.0 / D, bias=eps),
                      reads=[R_stat[si_]], writes=[R_stat[si_]])
                sc.op("dve", lambda e, rt=rt, rstd=rstd: e.reciprocal(out=rstd[:, 0:nsub], in_=rt[:, 0:nsub]), reads=[R_stat[si_]], writes=[R_stat[si_]])

            def load_tile(tj):
                t0_, T_ = tiles[tj]
                xb_ = tj % 2
                sc.dma("sp", oT[:, :, 0:T_], self.oT_s[:, t0_:t0_ + T_].rearrange("(k p) s -> p k s", p=128), sc.dsem("p4oT"),
                       reads=self.R_oT, writes=[R_oTt])
                sc.dma("sp", xin[xb_][:, 0:T_ // 128, :], xsrc[t0_:t0_ + T_, :].rearrange("(s p) d -> p s d", p=128), sc.dsem("p4x%d" % xb_),
                       reads=R_src[t0_ // 128:(t0_ + T_) // 128], writes=[R_x[xb_]])

            def pa_units(tj):
                t0_, T_ = tiles[tj]
                return [(s_, hf) for s_ in range(T_ // 128) for hf in range(2)]

            def pa_unit(tj, s, hf):
                xb_ = tj % 2
                for k in range(NK):
                    sc.op("pe", lambda e, s=s, hf=hf, k=k: e.matmul(
                        psY[hf][:, :], lhsT=oT[:, k, s * 128:(s + 1) * 128], rhs=wo[:, k, hf * 512:(hf + 1) * 512],
                        start=(k == 0), stop=(k == NK - 1)), reads=[R_oTt, R_wres], writes=[R_Y[hf]])
                sc.op("dve", lambda e, s=s, hf=hf, xb_=xb_: e.tensor_tensor(
                    out=xin[xb_][:, s, hf * 512:(hf + 1) * 512], in0=psY[hf][:, :], in1=xin[xb_][:, s, hf * 512:(hf + 1) * 512], op=ALU.add),
                    reads=[R_Y[hf], R_x[xb_]], writes=[R_x[xb_]])

            def pa_stats(tj):
                rms_rstd(tj % 2, tiles[tj][1] // 128, 1e-6, 1)

            def pb_scale(tj):
                t0_, T_ = tiles[tj]
                xb_ = tj % 2
                hi_ = tj % 2
                if tj == 0:
                    sc.op("dve", lambda e: e.memset(hT2[0][:, :, 0:2], 0.0), writes=[R_hT2[0]])
                else:
                    Tp = tiles[tj - 1][1]
                    sc.op("act", lambda e, hi_=hi_, Tp=Tp: e.activation(out=hT2[hi_][:, :, 0:2], in_=hT2[1 - hi_][:, :, Tp:Tp + 2], func=AF.Copy),
                          reads=[R_hT2[1 - hi_]], writes=[R_hT2[hi_]])
                for s in range(T_ // 128):
                    sc.op("dve", lambda e, s=s, xb_=xb_: e.tensor_scalar(
                        out=hb[s][:, :], in0=xin[xb_][:, s, :], scalar1=stats[1][2][:, s:s + 1], scalar2=None, op0=ALU.mult),
                        reads=[R_x[xb_], R_stat[1]], writes=[R_hb[s]])

            def pb_transpose(tj):
                t0_, T_ = tiles[tj]
                hi_ = tj % 2
                for s in range(T_ // 128):
                    for k in range(NK):
                        sc.op("pe", lambda e, s=s, k=k: e.transpose(out=psT[:, k, :], in_=hb[s][:, k * 128:(k + 1) * 128],
                                                                     identity=self.ident[:, :]),
                              reads=[R_hb[s], self.R_const], writes=[R_T])
                    sc.op("act", lambda e, s=s, hi_=hi_: e.activation(out=hT2[hi_][:, :, 2 + s * 128:2 + (s + 1) * 128], in_=psT[:, :, :], func=AF.Copy),
                          reads=[R_T], writes=[R_hT2[hi_]])

            def prologue_a(tj):
                for (s_, hf) in pa_units(tj):
                    pa_unit(tj, s_, hf)
                pa_stats(tj)

            def prologue_b(tj):
                pb_scale(tj)
                pb_transpose(tj)

            def load_wup(gi_):
                wi_ = wctr[0] % NWS
                wctr[0] += 1
                sc.dma("sp", wup[wi_][:, :, :, :, :], self.wup_s[l][:, 2 * gi_:2 * gi_ + 2, :, :, :], sc.dsem("p4wu%d" % wi_),
                       reads=self.R_wup[l], writes=[R_wu[wi_]])
                return wi_

            pref = {}
            load_tile(0)
            prologue_a(0)
            prologue_b(0)
            for ti, (t0, T) in enumerate(tiles):
                nsub = T // 128
                xb = ti % 2
                sub0 = t0 // 128
                hti = ti % 2
                nxt = ti + 1 < len(tiles)
                units = pa_units(ti + 1) if nxt else []
                for gi in range(NFF // 2):
                    if (ti, gi) in pref:
                        wi = pref[(ti, gi)]
                    else:
                        wi = load_wup(gi)
                    if gi == 1 and nxt:
                        load_tile(ti + 1)
                    for jj in range(2):
                        j = 2 * gi + jj
                        p_ = j - 6
                        if nxt and 0 <= p_ < len(units):
                            pa_unit(ti + 1, *units[p_])
                        if nxt and p_ == len(units):
                            pa_stats(ti + 1)
                        if nxt and p_ == len(units) + 2:
                            pb_scale(ti + 1)
                        if nxt and p_ == len(units) + 5:
                            pb_transpose(ti + 1)
                        ui = uctr[0] % 2
                        uctr[0] += 1
                        for gv in range(2):
                            for k in range(NK):
                                sc.op("pe", lambda e, ui=ui, gv=gv, k=k, wi=wi, jj=jj, T=T, hti=hti: e.matmul(
                                    psU[ui][gv][:, 0:T + 2], lhsT=wup[wi][:, jj, gv, k, :], rhs=hT2[hti][:, k, 0:T + 2],
                                    start=(k == 0), stop=(k == NK - 1)), reads=[R_wu[wi], R_hT2[hti]], writes=[R_U[ui][gv]])
                        for gv in range(2):
                            ch = gv * NFF + j
                            U = psU[ui][gv]
                            sc.op("act", lambda e, U=U, ui=ui, gv=gv, ch=ch, T=T: e.activation(
                                out=t1[ui][gv][:, 0:T], in_=U[:, 2:T + 2], func=AF.Identity, scale=cwl(2, ch), bias=cbl(ch)),
                                reads=[R_U[ui][gv], self.R_const], writes=[R_t1[ui][gv]])
                            sc.op("dve", lambda e, U=U, ui=ui, gv=gv, ch=ch, T=T: e.scalar_tensor_tensor(
                                out=t2[gv][:, 0:T], in0=U[:, 1:T + 1], scalar=cwl(1, ch), in1=t1[ui][gv][:, 0:T], op0=ALU.mult, op1=ALU.add),
                                reads=[R_U[ui][gv], R_t1[ui][gv], self.R_const], writes=[R_t2[gv]])
                            sc.op("dve", lambda e, U=U, ui=ui, gv=gv, ch=ch, T=T: e.scalar_tensor_tensor(
                                out=t1[ui][gv][:, 0:T], in0=U[:, 0:T], scalar=cwl(0, ch), in1=t2[gv][:, 0:T], op0=ALU.mult, op1=ALU.add),
                                reads=[R_U[ui][gv], R_t2[gv], self.R_const], writes=[R_t1[ui][gv]])
                        sgi = ui
                        sc.op("act", lambda e, ui=ui, T=T, sgi=sgi: e.activation(out=sg[sgi][:, 0:T], in_=t1[ui][0][:, 0:T], func=AF.Silu),
                              reads=[R_t1[ui][0]], writes=[R_sg[sgi]])
                        sc.op("pool", lambda e, ui=ui, j=j, T=T, sgi=sgi: e.tensor_tensor(out=g[:, j, 0:T], in0=sg[sgi][:, 0:T], in1=t1[ui][1][:, 0:T], op=ALU.mult),
                              reads=[R_sg[sgi], R_t1[ui][1]], writes=[R_gj[j]], pool_main=True)
                if nxt:
                    for g_ in range(2):
                        pref[(ti + 1, g_)] = load_wup(g_)
                for s in range(nsub):
                    for hf in range(2):
                        for j in range(NFF):
                            sc.op("pe", lambda e, s=s, hf=hf, j=j: e.matmul(
                                psY[hf][:, :], lhsT=g[:, j, s * 128:(s + 1) * 128], rhs=wdn[:, j, hf * 512:(hf + 1) * 512],
                                start=(j == 0), stop=(j == NFF - 1)), reads=[R_gj[j], R_wres], writes=[R_Y[hf]])
                        sc.op("dve", lambda e, s=s, hf=hf, xb=xb: e.tensor_tensor(
                            out=xin[xb][:, s, hf * 512:(hf + 1) * 512], in0=psY[hf][:, :], in1=xin[xb][:, s, hf * 512:(hf + 1) * 512], op=ALU.add),
                            reads=[R_Y[hf], R_x[xb]], writes=[R_x[xb]])
                if last:
                    rms_rstd(xb, nsub, 1e-6, 0)
                    for s in range(nsub):
                        sc.op("dve", lambda e, s=s, xb=xb: e.scalar_tensor_tensor(
                            out=xin[xb][:, s, :], in0=xin[xb][:, s, :], scalar=stats[0][2][:, s:s + 1], in1=gfin[:, :], op0=ALU.mult, op1=ALU.mult),
                            reads=[R_x[xb], R_stat[0], R_gf], writes=[R_x[xb]])
                sc.dma("sp", xdst[t0:t0 + T, :].rearrange("(s p) d -> p s d", p=128), xin[xb][:, 0:nsub, :], sc.dsem("p4st%d" % xb),
                       reads=[R_x[xb]], writes=R_dst[sub0:sub0 + nsub])


def build_program(S=SEQ, debug=False, nlayers=2, stop_after=None):
    nc = bass.Bass("TRN2", target_bir_lowering=False)
    b = Builder(nc, S, debug=debug, nlayers=nlayers)
    b.stop_after = stop_after
    b.build()
    return nc, b


def make_in_maps(inputs, S, ncores):
    f = lambda a: np.ascontiguousarray(np.asarray(a, dtype=np.float32))
    consts = host_constants(S)

    def pk(v):
        return np.ascontiguousarray(np.asarray(v, np.float32).reshape(8, 128).T)

    gin_t = np.concatenate([pk(inputs["even_norm"][0]), pk(inputs["odd_norm"][0])], axis=1)
    gffn_t = np.concatenate([pk(inputs["ffn_norm"][0]), pk(inputs["ffn_norm"][1])], axis=1)
    subl_t = f(inputs["odd_subln"][0]).reshape(128, 1)
    lamq = np.stack([f(inputs["odd_lambda_q1"][0]), f(inputs["odd_lambda_k1"][0]),
                     f(inputs["odd_lambda_q2"][0]), f(inputs["odd_lambda_k2"][0])], axis=0)
    cw = f(inputs["ffn_conv_w"])
    cw_t = np.ascontiguousarray(cw.reshape(2, 3, 44, 128).transpose(3, 0, 1, 2).reshape(128, 2 * 3 * 44))
    cb = f(inputs["ffn_conv_b"])
    cb_t = np.ascontiguousarray(cb.reshape(2, 44, 128).transpose(2, 0, 1).reshape(128, 2 * 44))
    shared = dict(
        w_in0=f(inputs["even_w_in"][0]), w_in1=f(inputs["odd_w_qkv"][0]),
        w_out0=f(inputs["even_w_out"][0]), w_out1=f(inputs["odd_w_out"][0]),
        w_up0=f(inputs["ffn_w_up"][0]), w_up1=f(inputs["ffn_w_up"][1]),
        w_dn0=f(inputs["ffn_w_down"][0]), w_dn1=f(inputs["ffn_w_down"][1]),
        gin_t=gin_t, gffn_t=gffn_t, subl_t=subl_t, lamq=lamq, cw_t=cw_t, cb_t=cb_t,
        gfin=f(inputs["final_norm"]).reshape(1, D), **consts)
    x = f(inputs["x"])
    in_maps = []
    for c in range(ncores):
        m = dict(shared)
        m["x"] = np.ascontiguousarray(x[c])
        in_maps.append(m)
    return in_maps


def kernel(**inputs):
    x = np.asarray(inputs["x"])
    Bn, S, _ = x.shape
    nc, _ = build_program(S)
    in_maps = make_in_maps(inputs, S, Bn)
    res = run_bass_kernel_spmd(nc, in_maps, core_ids=list(range(Bn)))
    out = np.stack([np.asarray(r["out"], dtype=np.float32) for r in res.results], axis=0)
    return out
```

```python
import math
from contextlib import ExitStack

import numpy as np
import ml_dtypes

import concourse.bass as bass
import concourse.mybir as mybir
from concourse.bass_utils import run_bass_kernel_spmd

F32 = mybir.dt.float32
BF16 = mybir.dt.bfloat16
AF = mybir.ActivationFunctionType
ALU = mybir.AluOpType
AX = mybir.AxisListType

D = 1024
DFF = 2816
NK = D // 128
NFF = DFF // 128
SEQ = 4096
NCORES = 8
LAMBDA_INIT = 0.8 - 0.6 * math.exp(-0.3 * 1)
MASKBIG = 30000.0
NEG = -1.0e30


class Res:
    __slots__ = ("w", "r", "excl")

    def __init__(self, excl=False):
        self.w = {}
        self.r = {}
        self.excl = excl


class DmaSem:
    def __init__(self, key, handle, main=True):
        self.key = key
        self.handle = handle
        self.count = 0
        self.main = main
        self.refs = set()


def _merge(d, s):
    for k, v in s.items():
        if d.get(k, 0) < v:
            d[k] = v


class Sched:
    CE = ("pe", "act", "dve")

    def __init__(self, nc, stack):
        self.nc = nc
        self.stack = stack
        self.eng = {"pe": nc.tensor, "act": nc.scalar, "dve": nc.vector, "pool": nc.gpsimd, "sp": nc.sync}
        self.epoch = {e: 0 for e in ("pe", "act", "dve", "pool")}
        self.cnt = {e: 0 for e in ("pe", "act", "dve", "pool")}
        self.semh = {}
        self.waited = {e: {} for e in self.eng}
        self.dsems = {}
        self.all_res = []
        self.n_inst = 0

    def res(self, excl=False):
        r = Res(excl)
        self.all_res.append(r)
        return r

    def pres(self):
        return self.res(excl=True)

    def _sem(self, key):
        h = self.semh.get(key)
        if h is None:
            h = self.stack.enter_context(self.nc.semaphore("s" + "_".join(str(x) for x in key)))
            self.semh[key] = h
        return h

    def ekey(self, e):
        return ("e", e, self.epoch[e])

    def dsem(self, name, main=True):
        d = self.dsems.get(name)
        if d is None:
            key = ("d", name)
            d = DmaSem(key, self._sem(key), main)
            self.dsems[name] = d
        return d

    def _wait(self, eng, deps):
        for k, v in deps.items():
            if eng == "pe" and k[0] == "e" and k[1] == "pe":
                continue
            if self.waited[eng].get(k, 0) >= v:
                continue
            self.waited[eng][k] = v
            self.eng[eng].wait_ge(self.semh[k], v)
            self.n_inst += 1

    def _deps(self, reads, writes):
        deps = {}
        for r in reads:
            _merge(deps, r.w)
        for w in writes:
            _merge(deps, w.w)
            _merge(deps, w.r)
        return deps

    def op(self, eng, fn, reads=(), writes=(), pool_main=False):
        if pool_main:
            self.pool_dirty = True
        deps = self._deps(reads, writes)
        for r in reads:
            if r.excl:
                for k, v in r.r.items():
                    if k[1] != eng and deps.get(k, 0) < v:
                        deps[k] = v
        self._wait(eng, deps)
        self.cnt[eng] += 1
        k = self.ekey(eng)
        v = self.cnt[eng]
        fn(self.eng[eng]).then_inc(self._sem(k), 1)
        self.n_inst += 1
        for r in reads:
            if r.r.get(k, 0) < v:
                r.r[k] = v
        for w in writes:
            w.w = {k: v}
            w.r = {}

    def dma(self, q, out, in_, dsem, reads=(), writes=()):
        self._wait(q, self._deps(reads, writes))
        dsem.count += 16
        self.eng[q].dma_start(out=out, in_=in_).then_inc(dsem.handle, 16)
        self.n_inst += 1
        k = dsem.key
        v = dsem.count
        for r in reads:
            if r.r.get(k, 0) < v:
                r.r[k] = v
            dsem.refs.add(r)
        for w in writes:
            w.w = {k: v}
            w.r = {}
            dsem.refs.add(w)
        dead = []
        for r in (dsem.refs if dsem.main else ()):
            hit = False
            if k in r.w:
                r.w[k] = v
                hit = True
            if k in r.r:
                r.r[k] = v
                hit = True
            if not hit:
                dead.append(r)
        for r in dead:
            dsem.refs.discard(r)

    def barrier(self, new_epoch=False):
        deps = {}
        for e in self.CE:
            if self.cnt[e] > 0:
                deps[self.ekey(e)] = self.cnt[e]
        for d in self.dsems.values():
            if d.main and d.count > 0:
                deps[d.key] = d.count
        if getattr(self, "pool_dirty", False):
            deps[self.ekey("pool")] = self.cnt["pool"]
            self.pool_dirty = False
        for e in ("pe", "act", "dve", "sp"):
            self._wait(e, deps)
        if new_epoch:
            old = set()
            for e in self.CE:
                old.add(self.ekey(e))
                self.epoch[e] += 1
                self.cnt[e] = 0
            for r in self.all_res:
                for k in list(r.w.keys()):
                    if k in old:
                        del r.w[k]
                for k in list(r.r.keys()):
                    if k in old:
                        del r.r[k]

    def final_wait(self, eng="sp"):
        deps = {}
        for e in ("pe", "act", "dve", "pool"):
            if self.cnt[e] > 0:
                deps[self.ekey(e)] = self.cnt[e]
        for d in self.dsems.values():
            if d.count > 0:
                deps[d.key] = d.count
        self._wait(eng, deps)


def host_constants(S):
    half = 32
    inv = (np.float32(1.0) / (np.float32(10000.0) ** (np.arange(half, dtype=np.float32) * np.float32(2.0) / np.float32(64)))).astype(np.float32)
    ang = (np.arange(S, dtype=np.float32)[:, None] * inv[None, :]).astype(np.float32)
    cos = np.cos(ang).astype(np.float32)
    sin = np.sin(ang).astype(np.float32)
    j = np.arange(128) % 32
    cosT = np.ascontiguousarray(cos[:, j].T)
    sinT = np.ascontiguousarray(sin[:, j].T)
    R = np.zeros((128, 128), np.float32)
    for m in range(128):
        d = m % 64
        base = m - d
        if d < 32:
            R[base + d + 32, m] = -1.0
        else:
            R[base + d - 32, m] = 1.0
    ident = np.eye(128, dtype=np.float32)
    n = np.arange(16)[:, None]
    kk = np.arange(S)[None, :]
    erows = (MASKBIG * ((kk // 256) == n)).astype(np.float32)
    ki = np.arange(128)[:, None]
    jj = np.arange(2048 + 384 + 512)[None, :]
    Dm = jj - 384 - ki
    c = ((Dm >= 0) & (Dm <= 128)).astype(np.float32)
    c += ((Dm >= 0) & (Dm <= 512) & (Dm % 4 == 0)).astype(np.float32)
    c += ((Dm >= 0) & (Dm <= 2048) & (Dm % 16 == 0)).astype(np.float32)
    qi = np.arange(128)[None, :]
    tri = (qi >= ki).astype(np.float32)
    bf = ml_dtypes.bfloat16
    return dict(cosT=cosT, sinT=sinT, rperm=R.astype(bf), ident=ident.astype(bf),
                erows=erows.astype(bf), strip_d=c.astype(bf), tri=tri.astype(bf))


class Builder:
    def __init__(self, nc, S, debug=False, nlayers=2):
        self.nc = nc
        self.S = S
        self.NT = S // 512
        self.NS = S // 128
        self.debug = debug
        self.nlayers = nlayers

    def dram_in(self, name, shape, dt):
        return self.nc.dram_tensor(name, list(shape), dt, kind="ExternalInput").ap()

    def dram_scr(self, name, shape, dt, dbg=False):
        kind = "ExternalOutput" if (dbg and self.debug) else "Internal"
        return self.nc.dram_tensor(name, list(shape), dt, kind=kind).ap()

    def _uid(self, name):
        self._n = getattr(self, "_n", 0) + 1
        return "t%d_%s" % (self._n, name)

    def sb(self, st, name, shape, dt):
        return st.enter_context(self.nc.sbuf_tensor(self._uid(name), list(shape), dt))

    def ps(self, st, name, shape, dt):
        return st.enter_context(self.nc.psum_tensor(self._uid(name), list(shape), dt))

    def build(self):
        nc, S = self.nc, self.S
        with ExitStack() as top:
            self.top = top
            self.sc = Sched(nc, top)
            self.declare_io()
            self.setup_constants()
            stop = getattr(self, "stop_after", None)
            if stop == "const":
                self.sc.final_wait("sp")
                return
            self.weight_prep()
            if stop == "prep":
                self.sc.final_wait("sp")
                return
            for l in range(self.nlayers):
                last = (l == self.nlayers - 1)
                xsrc = self.x if l == 0 else self.xs
                xdst = self.out if last else self.xs
                with ExitStack() as st12:
                    hT = self.sb(st12, "hT", [128, NK, S], BF16)
                    R_hT = [self.sc.res() for _ in range(self.NT)]
                    self.phase_norm(l, xsrc, hT, R_hT)
                    if stop == "norm":
                        self.sc.final_wait("sp")
                        return
                    self.phase_proj(l, hT, R_hT)
                    self.sc.barrier()
                if stop == "proj":
                    self.sc.final_wait("sp")
                    return
                self.phase_attn(l)
                if stop == "attn":
                    self.sc.final_wait("sp")
                    return
                self.sc.barrier(new_epoch=True)
                self.phase_ffn(l, xsrc, xdst, last)
                self.sc.barrier(new_epoch=True)
            self.sc.final_wait("sp")

    def declare_io(self):
        S = self.S
        sc = self.sc
        di = self.dram_in
        self.x = di("x", [S, D], F32)
        self.w_in = [di("w_in0", [D, 3 * D], F32), di("w_in1", [D, 3 * D], F32)]
        self.w_out = [di("w_out0", [D, D], F32), di("w_out1", [D, D], F32)]
        self.w_up = [di("w_up0", [D, 2 * DFF], F32), di("w_up1", [D, 2 * DFF], F32)]
        self.w_dn = [di("w_dn0", [DFF, D], F32), di("w_dn1", [DFF, D], F32)]
        self.gin_d = di("gin_t", [128, 16], F32)
        self.gffn_d = di("gffn_t", [128, 16], F32)
        self.subl_d = di("subl_t", [128, 1], F32)
        self.lamq_d = di("lamq", [4, 64], F32)
        self.cw_d = di("cw_t", [128, 2 * 3 * 44], F32)
        self.cb_d = di("cb_t", [128, 2 * 44], F32)
        self.gfin_d = di("gfin", [1, D], F32)
        self.cos_d = di("cosT", [128, S], F32)
        self.sin_d = di("sinT", [128, S], F32)
        self.rperm_d = di("rperm", [128, 128], BF16)
        self.ident_d = di("ident", [128, 128], BF16)
        self.erows_d = di("erows", [16, S], BF16)
        self.stripd_d = di("strip_d", [128, 2944], BF16)
        self.tri_d = di("tri", [128, 128], BF16)
        self.out = self.nc.dram_tensor("out", [S, D], F32, kind="ExternalOutput").ap()
        ds = self.dram_scr
        self.xs = ds("xs", [S, D], F32, dbg=True)
        self.wqkv_s = [ds("wqkv_s%d" % l, [128, NK, 3 * D], BF16) for l in range(2)]
        self.wo_s = [ds("wo_s%d" % l, [128, NK, D], BF16) for l in range(2)]
        self.wup_s = [ds("wup_s%d" % l, [128, NFF, 2, NK, 128], BF16) for l in range(2)]
        self.wdn_s = [ds("wdn_s%d" % l, [128, NFF, D], BF16) for l in range(2)]
        self.qaug_s = ds("qaug_s", [8, 80, S], BF16, dbg=True)
        self.kaug_s = ds("kaug_s", [8, 80, S], BF16, dbg=True)
        self.qT_s = ds("qT_s", [8, 128, S], BF16, dbg=True)
        self.kT_s = ds("kT_s", [8, 128, S], BF16, dbg=True)
        self.v_s = ds("v_s", [8, S, 256], BF16, dbg=True)
        self.oT_s = ds("oT_s", [D, S], BF16, dbg=True)
        self.R_x = [sc.res() for _ in range(self.NS)]
        self.R_xs = [sc.res() for _ in range(self.NS)]
        self.R_out = [sc.res() for _ in range(self.NS)]
        self.R_wqkv = [[[sc.res() for _ in range(NK)] for _ in range(3)] for _ in range(2)]
        self.R_wo = [[sc.res() for _ in range(NK)] for _ in range(2)]
        self.R_wup = [[sc.res() for _ in range(NK * 6)] for _ in range(2)]
        self.R_wdn = [[sc.res() for _ in range(NFF)] for _ in range(2)]
        self.R_qaug = [sc.res() for _ in range(8)]
        self.R_kaug = [sc.res() for _ in range(8)]
        self.R_qT = [sc.res() for _ in range(8)]
        self.R_kT = [sc.res() for _ in range(8)]
        self.R_v = [sc.res() for _ in range(8)]
        self.R_oT = [sc.res() for _ in range(8)]

    def setup_constants(self):
        sc, top = self.sc, self.top
        sb = self.sb
        self.ident = sb(top, "ident", [128, 128], BF16)
        self.rperm = sb(top, "rperm", [128, 128], BF16)
        self.gin = sb(top, "gin", [128, 16], F32)
        self.gffn = sb(top, "gffn", [128, 16], F32)
        self.subl = sb(top, "subl", [128, 1], F32)
        self.cw = sb(top, "cw", [128, 2 * 3 * 44], F32)
        self.cb = sb(top, "cb", [128, 2 * 44], F32)
        self.lamneg = sb(top, "lamneg", [128, 1], F32)
        lq = sb(top, "lq", [128, 4, 64], F32)
        lp = sb(top, "lp", [128, 2, 64], F32)
        ls = sb(top, "ls", [128, 2], F32)
        le = sb(top, "le", [128, 2], F32)
        self.R_const = sc.res()
        Rc = self.R_const
        dsm = sc.dsem("const")
        for dst, src in ((self.ident, self.ident_d), (self.rperm, self.rperm_d), (self.gin, self.gin_d),
                         (self.gffn, self.gffn_d), (self.subl, self.subl_d), (self.cw, self.cw_d),
                         (self.cb, self.cb_d)):
            sc.dma("sp", dst[:], src[:, :], dsm, writes=[Rc])
        for i in range(4):
            sc.dma("sp", lq[:, i, :], self.lamq_d[i:i + 1, :].partition_broadcast(128), dsm, writes=[Rc])
        for h in range(8):
            sc.dma("sp", self.kaug_s[h, 64:80, :], self.erows_d[:, :], sc.dsem("erow"), writes=[self.R_kaug[h]])
        Rl = sc.res()
        sc.op("dve", lambda e: e.tensor_tensor(out=lp[:, 0, :], in0=lq[:, 0, :], in1=lq[:, 1, :], op=ALU.mult), reads=[Rc], writes=[Rl])
        sc.op("dve", lambda e: e.tensor_tensor(out=lp[:, 1, :], in0=lq[:, 2, :], in1=lq[:, 3, :], op=ALU.mult), reads=[Rc, Rl], writes=[Rl])
        sc.op("dve", lambda e: e.tensor_reduce(out=ls[:, :], in_=lp[:, :, :], axis=AX.X, op=ALU.add), reads=[Rl], writes=[Rl])
        sc.op("act", lambda e: e.activation(out=le[:, :], in_=ls[:, :], func=AF.Exp), reads=[Rl], writes=[Rl])
        sc.op("dve", lambda e: e.tensor_tensor(out=ls[:, 0:1], in0=le[:, 1:2], in1=le[:, 0:1], op=ALU.subtract), reads=[Rl], writes=[Rl])
        sc.op("dve", lambda e: e.tensor_scalar(out=self.lamneg[:, :], in0=ls[:, 0:1], scalar1=-LAMBDA_INIT, scalar2=None, op0=ALU.add), reads=[Rl], writes=[Rl])
        self.R_lam = Rl

    def weight_prep(self):
        sc, top = self.sc, self.top
        NB = 3
        stf = [self.sb(top, "stf%d" % i, [128, 1024], F32) for i in range(NB)]
        stb = [self.sb(top, "stb%d" % i, [128, 1024], BF16) for i in range(NB)]
        Rf = [sc.res() for _ in range(NB)]
        Rb = [sc.res() for _ in range(NB)]
        units = []
        for l in range(self.nlayers):
            for k in range(NK):
                for b in range(3):
                    units.append((self.w_in[l][k * 128:(k + 1) * 128, b * 1024:(b + 1) * 1024], 1024,
                                  self.wqkv_s[l][:, k, b * 1024:(b + 1) * 1024], None,
                                  self.gin[:, l * 8 + k:l * 8 + k + 1], 1.0, self.R_wqkv[l][b][k]))
            for k in range(NK):
                s1 = self.subl[:, 0:1] if l == 1 else 1.0
                s2 = (1.0 - LAMBDA_INIT) if l == 1 else 1.0
                units.append((self.w_out[l][k * 128:(k + 1) * 128, :], 1024, self.wo_s[l][:, k, :], None,
                              s1, s2, self.R_wo[l][k]))
            for k in range(NK):
                ci = 0
                for isval in range(2):
                    for (c0, n) in ((0, 1024), (1024, 1024), (2048, 768)):
                        col = isval * DFF + c0
                        pj0 = c0 // 128
                        dst = self.wup_s[l][:, pj0:pj0 + n // 128, isval, k, :]
                        units.append((self.w_up[l][k * 128:(k + 1) * 128, col:col + n], n, dst, n // 128,
                                      self.gffn[:, l * 8 + k:l * 8 + k + 1], 1.0, self.R_wup[l][k * 6 + ci]))
                        ci += 1
            for j in range(NFF):
                units.append((self.w_dn[l][j * 128:(j + 1) * 128, :], 1024, self.wdn_s[l][:, j, :], None,
                              1.0, 1.0, self.R_wdn[l][j]))
        for i, (src, n, dst, nchunk, s1, s2, Rw) in enumerate(units):
            b = i % NB
            sc.dma("pool", stf[b][:, 0:n], src, sc.dsem("pl%d" % b, main=False), writes=[Rf[b]])
            rd = [Rf[b], self.R_const]
            sc.op("pool", lambda e, b=b, n=n, s1=s1, s2=s2: e.tensor_scalar(
                out=stb[b][:, 0:n], in0=stf[b][:, 0:n], scalar1=s1, scalar2=s2, op0=ALU.mult, op1=ALU.mult),
                reads=rd, writes=[Rb[b]])
            if nchunk is None:
                srcb = stb[b][:, 0:n]
            else:
                srcb = stb[b][:, 0:n].rearrange("p (c w) -> p c w", w=128)
            sc.dma("pool", dst, srcb, sc.dsem("ps%d" % b, main=False), reads=[Rb[b]], writes=[Rw])

    def phase_norm(self, l, xsrc, hT, R_hT):
        sc = self.sc
        R_src = self.R_x if l == 0 else self.R_xs
        with ExitStack() as st:
            xin = [self.sb(st, "p1x%d" % i, [128, 4, D], F32) for i in range(2)]
            hb = [self.sb(st, "p1h%d" % i, [128, D], BF16) for i in range(2)]
            junk = self.sb(st, "p1j", [128, D], BF16)
            ssq = self.sb(st, "p1ss", [128, self.NS], F32)
            rt = self.sb(st, "p1rt", [128, self.NS], F32)
            rstd = self.sb(st, "p1rs", [128, self.NS], F32)
            psT = [self.ps(st, "p1T%d" % i, [128, NK, 128], BF16) for i in range(2)]
            R_xin = [sc.res() for _ in range(2)]
            R_hb = [sc.res() for _ in range(2)]
            R_j = sc.res()
            R_st = [sc.res() for _ in range(self.NT)]
            R_psT = [sc.pres() for _ in range(2)]
            for g in range(self.NT):
                b = g % 2
                sc.dma("sp", xin[b][:, :, :], xsrc[g * 512:(g + 1) * 512, :].rearrange("(s p) d -> p s d", p=128),
                       sc.dsem("p1x%d" % b), reads=R_src[g * 4:(g + 1) * 4], writes=[R_xin[b]])
                for s in range(4):
                    sub = g * 4 + s
                    sc.op("act", lambda e, b=b, s=s, sub=sub: e.activation(
                        out=junk[:, :], in_=xin[b][:, s, :], func=AF.Square, accum_out=ssq[:, sub:sub + 1]),
                        reads=[R_xin[b]], writes=[R_j, R_st[g]])
                sc.op("act", lambda e, g=g: e.activation(out=rt[:, g * 4:(g + 1) * 4], in_=ssq[:, g * 4:(g + 1) * 4],
                                                         func=AF.Sqrt, scale=1.0 / D, bias=1e-6),
                      reads=[R_st[g]], writes=[R_st[g]])
                sc.op("dve", lambda e, g=g: e.reciprocal(out=rstd[:, g * 4:(g + 1) * 4], in_=rt[:, g * 4:(g + 1) * 4]),
                      reads=[R_st[g]], writes=[R_st[g]])
                for s in range(4):
                    sub = g * 4 + s
                    hbi = sub % 2
                    sc.op("dve", lambda e, b=b, s=s, sub=sub, hbi=hbi: e.tensor_scalar(
                        out=hb[hbi][:, :], in0=xin[b][:, s, :], scalar1=rstd[:, sub:sub + 1], scalar2=None, op0=ALU.mult),
                        reads=[R_xin[b], R_st[g]], writes=[R_hb[hbi]])
                    for k in range(NK):
                        sc.op("pe", lambda e, hbi=hbi, k=k: e.transpose(
                            out=psT[hbi][:, k, :], in_=hb[hbi][:, k * 128:(k + 1) * 128], identity=self.ident[:, :]),
                            reads=[R_hb[hbi], self.R_const], writes=[R_psT[hbi]])
                    sc.op("act", lambda e, hbi=hbi, sub=sub: e.activation(
                        out=hT[:, :, sub * 128:(sub + 1) * 128], in_=psT[hbi][:, :, :], func=AF.Copy),
                        reads=[R_psT[hbi]], writes=[R_hT[g]])
            sc.barrier()

    def phase_proj(self, l, hT, R_hT):
        sc = self.sc
        S, NT, NS = self.S, self.NT, self.NS
        with ExitStack() as st:
            cosT = self.sb(st, "p2cos", [128, S], F32)
            sinT = self.sb(st, "p2sin", [128, S], F32)
            R_tab = sc.res()
            sc.dma("sp", cosT[:, :], self.cos_d[:, :], sc.dsem("p2tab"), writes=[R_tab])
            sc.dma("sp", sinT[:, :], self.sin_d[:, :], sc.dsem("p2tab"), writes=[R_tab])
            NW = 4
            wch = [self.sb(st, "p2w%d" % i, [128, NK, 128], BF16) for i in range(NW)]
            R_w = [sc.res() for _ in range(NW)]
            qb = [self.sb(st, "p2qb%d" % i, [128, 512], BF16) for i in range(2)]
            R_qb = [sc.res() for _ in range(2)]
            Af = [self.sb(st, "p2A%d" % i, [128, 512], F32) for i in range(2)]
            Bf = [self.sb(st, "p2B%d" % i, [128, 512], F32) for i in range(2)]
            R_A = [sc.res() for _ in range(2)]
            R_B = [sc.res() for _ in range(2)]
            tmpf = [self.sb(st, "p2t%d" % i, [128, 512], F32) for i in range(2)]
            R_tmp = [sc.res() for _ in range(2)]
            stg = [self.sb(st, "p2stg%d" % i, [128, S], BF16) for i in range(2)]
            R_stg = [sc.res() for _ in range(2)]
            vsb = [self.sb(st, "p2v%d" % i, [128, NS, 256 if l == 0 else 130], BF16) for i in range(2)]
            R_vsb = [sc.res() for _ in range(2)]
            ksum = self.sb(st, "p2ks", [128, 16], F32)
            R_ks = sc.res()
            kpad = self.sb(st, "p2kpad", [128, 2, 16], F32)
            R_kpad = sc.res()
            sc.op("dve", lambda e: e.memset(kpad[:, :, :], 0.0), writes=[R_kpad])
            gate = self.sb(st, "p2g", [128, 4, 2, 16], F32)
            top8 = self.sb(st, "p2t8", [128, 4, 2, 8], F32)
            nmp = self.sb(st, "p2nm", [128, 4, 128], BF16)
            sc.op("dve", lambda e: e.memset(nmp[:, :, :], 0.0), writes=[sc.res()])
            nm = nmp[:, :, 0:32].rearrange("p s (h n) -> p s h n", h=2)
            R_gate = sc.res()
            R_nm = sc.res()
            R_t8 = [[sc.res() for _ in range(2)] for _ in range(4)]
            R_nm2 = [[sc.res() for _ in range(2)] for _ in range(4)]
            nmT = self.sb(st, "p2nmT", [32, S], BF16)
            R_nmT = sc.res()
            ps1 = [self.ps(st, "p2ps1_%d" % i, [128, 512], F32) for i in range(2)]
            ps2 = [self.ps(st, "p2ps2_%d" % i, [128, 512], F32) for i in range(2)]
            psV = [self.ps(st, "p2psV%d" % i, [128, 4, 128], F32) for i in range(2)]
            psG = self.ps(st, "p2psG", [128, 512], F32)
            psN = self.ps(st, "p2psN", [128, 512], F32)
            R_ps1 = [sc.pres() for _ in range(2)]
            R_ps2 = [sc.pres() for _ in range(2)]
            R_psV = [sc.pres() for _ in range(2)]
            R_psG = sc.pres()
            R_psN = sc.pres()
            for i in range(2):
                if l == 0:
                    v4 = vsb[i][:, :, :].rearrange("p s (h e) -> p s h e", h=2)
                    sc.op("dve", lambda e, v4=v4: e.memset(v4[:, :, :, 64:128], 1.0), writes=[R_vsb[i]])
                else:
                    sc.op("dve", lambda e, i=i: e.memset(vsb[i][:, :, 128:129], 1.0), writes=[R_vsb[i]])

            wctr = [0]
            tctr = [0]
            sctr = [0]
            import os as _os
            dbg = _os.environ.get("KDBG", "")
            if "p2a" in dbg:
                return
            if "waitpool" in dbg:
                deps = {sc.ekey("pool"): sc.cnt["pool"]}
                for d_ in sc.dsems.values():
                    if d_.count > 0:
                        deps[d_.key] = d_.count
                for e_ in ("pe", "act", "dve", "sp"):
                    sc._wait(e_, deps)

            def load_w(blk, c):
                i = wctr[0] % NW
                wctr[0] += 1
                col = blk * 1024 + c * 128
                sc.dma("sp", wch[i][:, :, :], self.wqkv_s[l][:, :, col:col + 128], sc.dsem("p2w%d" % i),
                       reads=self.R_wqkv[l][blk], writes=[R_w[i]])
                return i

            def qk_chunk(blk, c, moba, wi):
                si = sctr[0] % 2
                sctr[0] += 1
                if moba and blk == 0:
                    sc.op("dve", lambda e: e.memset(gate[:, :, :, :], NEG), writes=[R_gate])
                    sc.op("dve", lambda e: e.tensor_copy(out=kpad[0:64, 0, :], in_=ksum[0:64, :]), reads=[R_ks], writes=[R_kpad])
                    sc.op("dve", lambda e: e.tensor_copy(out=kpad[64:128, 1, :], in_=ksum[64:128, :]), reads=[R_ks], writes=[R_kpad])
                if moba and blk == 1:
                    sc.op("dve", lambda e: e.memset(ksum[:, :], 0.0), writes=[R_ks])
                def stA(t):
                    i = t % 2
                    cols = slice(t * 512, (t + 1) * 512)
                    for k in range(NK):
                        sc.op("pe", lambda e, i=i, k=k, wi=wi, cols=cols: e.matmul(
                            ps1[i][:, :], lhsT=wch[wi][:, k, :], rhs=hT[:, k, cols], start=(k == 0), stop=(k == NK - 1)),
                            reads=[R_w[wi], R_hT[t]], writes=[R_ps1[i]])
                    sc.op("act", lambda e, i=i: e.activation(out=qb[i][:, :], in_=ps1[i][:, :], func=AF.Copy),
                          reads=[R_ps1[i]], writes=[R_qb[i]])

                def stB(t):
                    i = t % 2
                    cols = slice(t * 512, (t + 1) * 512)
                    sc.op("pe", lambda e, i=i: e.matmul(ps2[i][:, :], lhsT=self.rperm[:, :], rhs=qb[i][:, :], start=True, stop=True),
                          reads=[R_qb[i], self.R_const], writes=[R_ps2[i]])
                    sc.op("dve", lambda e, i=i, cols=cols: e.tensor_tensor(out=Af[i][:, :], in0=ps1[i][:, :], in1=cosT[:, cols], op=ALU.mult),
                          reads=[R_ps1[i], R_tab], writes=[R_A[i]])
                    sc.op("dve", lambda e, i=i, cols=cols: e.tensor_tensor(out=Bf[i][:, :], in0=ps2[i][:, :], in1=sinT[:, cols], op=ALU.mult),
                          reads=[R_ps2[i], R_tab], writes=[R_B[i]])
                    if not moba:
                        sc.op("dve", lambda e, i=i, cols=cols: e.tensor_tensor(out=stg[si][:, cols], in0=Af[i][:, :], in1=Bf[i][:, :], op=ALU.add),
                              reads=[R_A[i], R_B[i]], writes=[R_stg[si]])
                        return
                    sc.op("dve", lambda e, i=i: e.tensor_tensor(out=tmpf[i][:, :], in0=Af[i][:, :], in1=Bf[i][:, :], op=ALU.add),
                          reads=[R_A[i], R_B[i]], writes=[R_tmp[i]])
                    sc.op("act", lambda e, i=i, cols=cols: e.activation(out=stg[si][:, cols], in_=tmpf[i][:, :], func=AF.Copy),
                          reads=[R_tmp[i]], writes=[R_stg[si]])
                    if blk == 1:
                        sc.op("dve", lambda e, i=i, t=t: e.tensor_reduce(
                            out=ksum[:, 2 * t:2 * t + 2], in_=tmpf[i][:, :].rearrange("p (b w) -> p b w", w=256), axis=AX.X, op=ALU.add),
                            reads=[R_tmp[i]], writes=[R_ks])

                def stC(t):
                    i = t % 2
                    for s_ in range(4):
                        for h in range(2):
                            sc.op("pe", lambda e, i=i, s_=s_, h=h: e.matmul(
                                psG[:, (s_ * 2 + h) * 16:(s_ * 2 + h) * 16 + 16], lhsT=tmpf[i][:, s_ * 128:(s_ + 1) * 128],
                                rhs=kpad[:, h, :], start=True, stop=True),
                                reads=[R_tmp[i], R_kpad], writes=[R_psG])
                    psG4 = psG[:, 0:128].rearrange("p (s h n) -> p s h n", s=4, h=2)
                    sc.op("dve", lambda e: e.memset(nm[:, :, :, :], -1.0),
                          writes=[R_nm] + [R_nm2[a][b_] for a in range(4) for b_ in range(2)])
                    for half in range(2):
                        b = 2 * t + half
                        ss = slice(2 * half, 2 * half + 2)
                        sc.op("dve", lambda e, ss=ss, b=b: e.memset(nm[:, ss, :, b:b + 1], 0.0), writes=[R_nm])
                        if b > 0:
                            sc.op("dve", lambda e, ss=ss, b=b, psG4=psG4: e.tensor_copy(out=gate[:, ss, :, 0:b], in_=psG4[:, ss, :, 0:b]),
                                  reads=[R_psG], writes=[R_gate])
                    for s_ in range(4):
                        b = 2 * t + s_ // 2
                        if b == 0:
                            continue
                        for h in range(2):
                            sc.op("dve", lambda e, s_=s_, h=h: e.max(out=top8[:, s_, h, :], in_=gate[:, s_, h, :]),
                                  reads=[R_gate], writes=[R_t8[s_][h]])
                            sc.op("dve", lambda e, s_=s_, h=h, b=b: e.tensor_scalar(
                                out=nm[:, s_, h, 0:b], in0=gate[:, s_, h, 0:b], scalar1=top8[:, s_, h, 2:3], scalar2=1.0,
                                op0=ALU.is_ge, op1=ALU.subtract), reads=[R_gate, R_t8[s_][h], R_nm], writes=[R_nm2[s_][h]])

                def stD(t):
                    cols = slice(t * 512, (t + 1) * 512)
                    allnm = [R_nm] + [R_nm2[a][b_] for a in range(4) for b_ in range(2)]
                    for s_ in range(4):
                        sc.op("pe", lambda e, s_=s_: e.matmul(
                            psN[:, s_ * 128:(s_ + 1) * 128], lhsT=nmp[:, s_, :], rhs=self.ident[:, :],
                            start=True, stop=True), reads=allnm + [self.R_const], writes=[R_psN])
                    sc.op("act", lambda e, cols=cols: e.activation(out=nmT[:, cols], in_=psN[0:32, :], func=AF.Copy),
                          reads=[R_psN], writes=[R_nmT])

                if moba and "mobaseq" in dbg:
                    stages = [(stA, 0), (stB, 0)]
                    if blk == 0:
                        stages += [(stC, 0), (stD, 0)]
                else:
                    stages = [(stA, 0), (stB, 1)]
                    if moba and blk == 0:
                        stages += [(stC, 2), (stD, 3)]
                for step in range(NT + len(stages) - 1):
                    for fn_, lag in reversed(stages):
                        t = step - lag
                        if 0 <= t < NT:
                            fn_(t)
                if "nostore" in dbg:
                    return
                ds_ = sc.dsem("p2st%d" % si)
                if moba:
                    dstt = self.qaug_s if blk == 0 else self.kaug_s
                    Rd = self.R_qaug if blk == 0 else self.R_kaug
                    for h in range(2):
                        sc.dma("sp", dstt[2 * c + h, 0:64, :], stg[si][h * 64:(h + 1) * 64, :], ds_, reads=[R_stg[si]], writes=[Rd[2 * c + h]])
                    if blk == 0:
                        for h in range(2):
                            sc.dma("sp", self.qaug_s[2 * c + h, 64:80, :], nmT[h * 16:(h + 1) * 16, :], sc.dsem("p2nm"),
                                   reads=[R_nmT], writes=[self.R_qaug[2 * c + h]])
                else:
                    dstt = self.qT_s if blk == 0 else self.kT_s
                    Rd = self.R_qT if blk == 0 else self.R_kT
                    sc.dma("sp", dstt[c, :, :], stg[si][:, :], ds_, reads=[R_stg[si]], writes=[Rd[c]])

            vctr = [0]

            def v_chunk(c, wi):
                vi = vctr[0] % 2
                vctr[0] += 1
                for g in range(NS // 4):
                    pi = g % 2
                    for s in range(4):
                        sub = g * 4 + s
                        for k in range(NK):
                            sc.op("pe", lambda e, pi=pi, s=s, k=k, sub=sub, wi=wi: e.matmul(
                                psV[pi][:, s, :], lhsT=hT[:, k, sub * 128:(sub + 1) * 128], rhs=wch[wi][:, k, :],
                                start=(k == 0), stop=(k == NK - 1)),
                                reads=[R_hT[sub // 4], R_w[wi]], writes=[R_psV[pi]])
                    if l == 0:
                        dst = vsb[vi][:, g * 4:(g + 1) * 4, :].rearrange("p s (h e) -> p s h e", h=2)[:, :, :, 0:64]
                        src = psV[pi][:, :, :].rearrange("p s (h d) -> p s h d", h=2)
                    else:
                        dst = vsb[vi][:, g * 4:(g + 1) * 4, 0:128]
                        src = psV[pi][:, :, :]
                    sc.op("act", lambda e, dst=dst, src=src: e.activation(out=dst, in_=src, func=AF.Copy),
                          reads=[R_psV[pi]], writes=[R_vsb[vi]])
                vd = self.v_s[c].rearrange("(s p) e -> p s e", p=128)
                ve = 256 if l == 0 else 129
                for q4 in range(0, NS, 8):
                    sc.dma("sp", vd[:, q4:q4 + 8, 0:ve], vsb[vi][:, q4:q4 + 8, 0:ve], sc.dsem("p2vst%d" % vi),
                           reads=[R_vsb[vi]], writes=[self.R_v[c]])

            tasks = []
            for c in range(8):
                tasks += [(1, c), (0, c), (2, c)]
            wi_of = {}
            for i_ in range(3):
                wi_of[i_] = load_w(*tasks[i_])
            for i_, (blk_, c) in enumerate(tasks):
                if i_ + 3 < len(tasks):
                    wi_of[i_ + 3] = load_w(*tasks[i_ + 3])
                moba = (l == 0 and c < 4)
                if blk_ == 2:
                    v_chunk(c, wi_of[i_])
                else:
                    qk_chunk(blk_, c, moba, wi_of[i_])

    def phase_attn(self, l):
        sc = self.sc
        S, NT, NS = self.S, self.NT, self.NS
        with ExitStack() as st:
            NSET = 2
            qa = [self.sb(st, "p3qa%d" % i, [128, S], BF16) for i in range(NSET)]
            ka = [self.sb(st, "p3ka%d" % i, [128, S], BF16) for i in range(NSET)]
            if l == 0:
                qbt = [self.sb(st, "p3qb%d" % i, [128, S], BF16) for i in range(NSET)]
            kbt = [self.sb(st, "p3kb%d" % i, [128, S], BF16) for i in range(NSET)]
            vt = [self.sb(st, "p3v%d" % i, [128, NS, 256 if l == 0 else 130], BF16) for i in range(NSET)]
            R_in = [sc.res() for _ in range(NSET)]
            if l == 1:
                osb = [self.sb(st, "p3o%d" % i, [128, NS, 128], BF16) for i in range(2)]
            R_osb = [sc.res() for _ in range(2)]
            otsb = [self.sb(st, "p3ot%d" % i, [128, S], BF16) for i in range(2)]
            R_ot = [sc.res() for _ in range(2)]
            NPT = 5
            pt = [self.sb(st, "p3pt%d" % i, [128, 512], BF16) for i in range(NPT)]
            R_pt = [sc.res() for _ in range(NPT)]
            tri = self.sb(st, "p3tri", [128, 128], BF16)
            R_msk = sc.res()
            sc.dma("sp", tri[:, :], self.tri_d[:, :], sc.dsem("p3msk"), writes=[R_msk])
            if l == 0:
                strip = self.sb(st, "p3strip", [128, 2944], BF16)
                sc.dma("sp", strip[:, :], self.stripd_d[:, :], sc.dsem("p3msk"), writes=[R_msk])
                rec = [self.sb(st, "p3rec%d" % i, [64, 512], F32) for i in range(2)]
                R_rec = [sc.res() for _ in range(2)]
            else:
                rec1 = self.sb(st, "p3r1", [128, 2, 2, 1], F32)
                rec2 = self.sb(st, "p3r2", [128, 2, 2, 1], F32)
                tb = [self.sb(st, "p3tb%d" % i, [128, 128], F32) for i in range(2)]
                R_tb = [sc.res() for _ in range(2)]
                ob = [self.sb(st, "p3ob%d" % i, [128, 4, 128], F32) for i in range(2)]
                R_ob = [sc.res() for _ in range(2)]
                junk = self.sb(st, "p3junk", [128, 128], BF16)
                R_junk = sc.res()
                ssd = self.sb(st, "p3ssd", [128, 4], F32)
                rtd = self.sb(st, "p3rtd", [128, 4], F32)
                rsd = self.sb(st, "p3rsd", [128, 4], F32)
                R_fin = sc.res()
            NSB = 3
            psS = [self.ps(st, "p3S%d" % i, [128, 512], F32) for i in range(NSB)]
            R_S = [sc.pres() for _ in range(NSB)]
            psO = [self.ps(st, "p3O%d" % i, [128, 512], F32) for i in range(4)]
            R_O = [sc.pres() for _ in range(4)]
            psT = self.ps(st, "p3T", [128, 8, 128], BF16)
            R_T = sc.pres()

            zeroed = set()

            def load_chunk(c):
                i = c % NSET
                R = R_in[i]
                dq = sc.dsem("p3in%d" % i)
                if l == 0 and c < 4:
                    sc.dma("sp", qa[i][0:80, :], self.qaug_s[2 * c, :, :], dq, reads=[self.R_qaug[2 * c]], writes=[R])
                    sc.dma("sp", qbt[i][0:80, :], self.qaug_s[2 * c + 1, :, :], dq, reads=[self.R_qaug[2 * c + 1]], writes=[R])
                    sc.dma("sp", ka[i][0:80, :], self.kaug_s[2 * c, :, :], dq, reads=[self.R_kaug[2 * c]], writes=[R])
                    sc.dma("sp", kbt[i][0:80, :], self.kaug_s[2 * c + 1, :, :], dq, reads=[self.R_kaug[2 * c + 1]], writes=[R])
                else:
                    if i not in zeroed:
                        zeroed.add(i)
                        sc.op("dve", lambda e, i=i: e.memset(ka[i][64:128, :], 0.0), writes=[R])
                        sc.op("dve", lambda e, i=i: e.memset(kbt[i][0:64, :], 0.0), writes=[R])
                    sc.dma("sp", qa[i][:, :], self.qT_s[c, :, :], dq, reads=[self.R_qT[c]], writes=[R])
                    sc.dma("sp", ka[i][0:64, :], self.kT_s[c, 0:64, :], dq, reads=[self.R_kT[c]], writes=[R])
                    sc.dma("sp", kbt[i][64:128, :], self.kT_s[c, 64:128, :], dq, reads=[self.R_kT[c]], writes=[R])
                vd = self.v_s[c].rearrange("(s p) e -> p s e", p=128)
                ve = 256 if l == 0 else 129
                for q4 in range(0, NS, 8):
                    sc.dma("sp", vt[i][:, q4:q4 + 8, 0:ve], vd[:, q4:q4 + 8, 0:ve], dq, reads=[self.R_v[c]], writes=[R])

            sctr = [0]
            pctr = [0]
            mctr = [0]

            def run_chunk(c):
                i = c % NSET
                oi = c % 2
                Rin = R_in[i]
                if l == 0 and c < 4:
                    kind = "moba"
                    maps = [dict(q=qa[i], k=ka[i], r0=0, r1=80, h=0), dict(q=qbt[i], k=kbt[i], r0=0, r1=80, h=1)]
                elif l == 0:
                    kind = "dil"
                    maps = [dict(q=qa[i], k=ka[i], r0=0, r1=128, h=0), dict(q=qa[i], k=kbt[i], r0=0, r1=128, h=1)]
                else:
                    kind = "causal"
                    maps = [dict(q=qa[i], k=ka[i], r0=0, r1=128, h=0), dict(q=qa[i], k=kbt[i], r0=0, r1=128, h=1)]
                if l == 0:
                    v4 = vt[i][:, :, :].rearrange("p s (h e) -> p s h e", h=2)
                steps = []
                for qt in range(NT):
                    for m in range(2):
                        if kind == "dil":
                            kts = list(range(max(0, 4 * qt - 16), 4 * qt + 4))
                        else:
                            kts = list(range(0, 4 * qt + 4))
                        for kt in kts:
                            steps.append((qt, m, kt, kt == kts[0], kt == kts[-1]))
                started = {}
                pendq = []
                LAG = 3

                def emit_pv(stp):
                    qt, m, kt, first, last, pi, lo, hi = stp
                    mp = maps[m]
                    if l == 0:
                        bank = m + 2 * (qt % 2)
                        key = (bank, qt, m)
                        stt = key not in started
                        started[key] = True
                        sc.op("pe", lambda e, bank=bank, kt=kt, h=mp["h"], pi=pi, lo=lo, hi=hi, stt=stt, last=last: e.matmul(
                            psO[bank][:, lo:hi], lhsT=v4[:, kt, h, :], rhs=pt[pi][:, lo:hi], start=stt, stop=last, skip_group_check=True),
                            reads=[R_pt[pi], Rin], writes=[R_O[bank]])
                        if last:
                            finalize(qt, m)
                        return
                    for qs in range(lo // 128, hi // 128):
                        if l == 0:
                            bank = m + 2 * (qt % 2)
                            outap = psO[bank][:, qs * 65:(qs + 1) * 65]
                            rhs = v4[:, kt, mp["h"], :]
                        else:
                            bank = 2 * m + qs // 2
                            outap = psO[bank][:, (qs % 2) * 129:(qs % 2) * 129 + 129]
                            rhs = vt[i][:, kt, 0:129]
                        key = (bank, qt, m)
                        stt = key not in started
                        started[key] = True
                        sc.op("pe", lambda e, outap=outap, rhs=rhs, pi=pi, qs=qs, stt=stt, last=last: e.matmul(
                            outap, lhsT=pt[pi][:, qs * 128:(qs + 1) * 128], rhs=rhs, start=stt, stop=last, skip_group_check=True),
                            reads=[R_pt[pi], Rin], writes=[R_O[bank]])
                    if last:
                        finalize(qt, m)

                def finalize(qt, m):
                    if l == 0:
                        bank = m + 2 * (qt % 2)
                        ri = (qt * 2 + m) % 2
                        sc.op("dve", lambda e, bank=bank, ri=ri: e.reciprocal(out=rec[ri][0:64, :], in_=psO[bank][64:128, :]),
                              reads=[R_O[bank]], writes=[R_rec[ri]])
                        sc.op("dve", lambda e, bank=bank, ri=ri, m=m, qt=qt: e.tensor_tensor(
                            out=otsb[oi][m * 64:(m + 1) * 64, qt * 512:(qt + 1) * 512], in0=psO[bank][0:64, :], in1=rec[ri][0:64, :],
                            op=ALU.mult), reads=[R_O[bank], R_rec[ri]], writes=[R_ot[oi]])
                    else:
                        if m == 0:
                            return
                        for bb in range(2):
                            O1 = psO[bb][:, 0:258].rearrange("p (q e) -> p q e", e=129)
                            O2 = psO[2 + bb][:, 0:258].rearrange("p (q e) -> p q e", e=129)
                            sc.op("dve", lambda e, O1=O1, bb=bb: e.reciprocal(out=rec1[:, bb, :, :], in_=O1[:, :, 128:129]),
                                  reads=[R_O[bb]], writes=[R_fin])
                            sc.op("dve", lambda e, O2=O2, bb=bb: e.reciprocal(out=rec2[:, bb, :, :], in_=O2[:, :, 128:129]),
                                  reads=[R_O[2 + bb]], writes=[R_fin])
                        sc.op("dve", lambda e: e.tensor_scalar(out=rec2[:, :, :, :], in0=rec2[:, :, :, :], scalar1=self.lamneg[:, 0:1],
                                                               scalar2=None, op0=ALU.mult), reads=[R_fin, self.R_lam], writes=[R_fin])
                        obi = qt % 2
                        for qs in range(4):
                            bb, ii = qs // 2, qs % 2
                            O1 = psO[bb][:, 0:258].rearrange("p (q e) -> p q e", e=129)
                            O2 = psO[2 + bb][:, 0:258].rearrange("p (q e) -> p q e", e=129)
                            ti = qs % 2
                            sc.op("act", lambda e, O2=O2, ii=ii, bb=bb, ti=ti: e.activation(
                                out=tb[ti][:, :], in_=O2[:, ii, 0:128], func=AF.Copy, scale=rec2[:, bb, ii, 0:1]),
                                reads=[R_O[2 + bb], R_fin], writes=[R_tb[ti]])
                            sc.op("dve", lambda e, O1=O1, ii=ii, bb=bb, ti=ti, obi=obi, qs=qs: e.scalar_tensor_tensor(
                                out=ob[obi][:, qs, :], in0=O1[:, ii, 0:128], scalar=rec1[:, bb, ii, 0:1], in1=tb[ti][:, :],
                                op0=ALU.mult, op1=ALU.add), reads=[R_O[bb], R_fin, R_tb[ti]], writes=[R_ob[obi]])
                            sc.op("act", lambda e, obi=obi, qs=qs: e.activation(
                                out=junk[:, :], in_=ob[obi][:, qs, :], func=AF.Square, accum_out=ssd[:, qs:qs + 1]),
                                reads=[R_ob[obi]], writes=[R_junk, R_fin])
                        sc.op("act", lambda e: e.activation(out=rtd[:, :], in_=ssd[:, :], func=AF.Sqrt, scale=1.0 / 128, bias=1e-5),
                              reads=[R_fin], writes=[R_fin])
                        sc.op("dve", lambda e: e.reciprocal(out=rsd[:, :], in_=rtd[:, :]), reads=[R_fin], writes=[R_fin])
                        for qs in range(4):
                            sub = qt * 4 + qs
                            sc.op("dve", lambda e, obi=obi, qs=qs, sub=sub: e.tensor_scalar(
                                out=osb[oi][:, sub, :], in0=ob[obi][:, qs, :], scalar1=rsd[:, qs:qs + 1], scalar2=None, op0=ALU.mult),
                                reads=[R_ob[obi], R_fin], writes=[R_osb[oi]])
                        transposes(qt)

                def transposes(qt):
                    for qs in range(4):
                        sub = qt * 4 + qs
                        sc.op("pe", lambda e, qs=qs, sub=sub: e.transpose(out=psT[:, qs, :], in_=osb[oi][:, sub, :], identity=self.ident[:, :]),
                              reads=[R_osb[oi], self.R_const], writes=[R_T])
                    sc.op("act", lambda e, qt=qt: e.activation(
                        out=otsb[oi][:, qt * 512:(qt + 1) * 512].rearrange("p (s w) -> p s w", w=128), in_=psT[:, 0:4, :], func=AF.Copy),
                        reads=[R_T], writes=[R_ot[oi]])

                for (qt, m, kt, first, last) in steps:
                    mp = maps[m]
                    delta = (4 * qt - kt) * 128
                    lo = max(0, -delta)
                    hi = 512
                    if kind == "dil":
                        hi = min(512, 2048 - delta + 128)
                    si = sctr[0] % NSB
                    sctr[0] += 1
                    pi = pctr[0] % NPT
                    pctr[0] += 1
                    r0, r1 = mp["r0"], mp["r1"]
                    sc.op("pe", lambda e, si=si, mp=mp, r0=r0, r1=r1, kt=kt, qt=qt: e.matmul(
                        psS[si][:, :], lhsT=mp["k"][r0:r1, kt * 128:(kt + 1) * 128], rhs=mp["q"][r0:r1, qt * 512:(qt + 1) * 512],
                        start=True, stop=True), reads=[Rin], writes=[R_S[si]])
                    if len(pendq) >= LAG:
                        emit_pv(pendq.pop(0))
                    sc.op("act", lambda e, si=si, pi=pi, lo=lo, hi=hi: e.activation(
                        out=pt[pi][:, lo:hi], in_=psS[si][:, lo:hi], func=AF.Exp, scale=0.125),
                        reads=[R_S[si]], writes=[R_pt[pi]])
                    if kind == "dil":
                        o0 = delta + 384
                        mctr[0] += 1
                        on_pool = False
                        sc.op("pool" if on_pool else "dve", lambda e, pi=pi, lo=lo, hi=hi, o0=o0: e.tensor_tensor(
                            out=pt[pi][:, lo:hi], in0=pt[pi][:, lo:hi], in1=strip[:, o0 + lo:o0 + hi], op=ALU.mult),
                            reads=[R_pt[pi], R_msk], writes=[R_pt[pi]], pool_main=on_pool)
                    elif delta <= 0:
                        sc.op("dve", lambda e, pi=pi, lo=lo: e.tensor_tensor(
                            out=pt[pi][:, lo:lo + 128], in0=pt[pi][:, lo:lo + 128], in1=tri[:, :], op=ALU.mult),
                            reads=[R_pt[pi], R_msk], writes=[R_pt[pi]])
                    pendq.append((qt, m, kt, first, last, pi, lo, hi))
                while pendq:
                    emit_pv(pendq.pop(0))
                sc.dma("sp", self.oT_s[c * 128:(c + 1) * 128, :], otsb[oi][:, :], sc.dsem("p3ost%d" % oi),
                       reads=[R_ot[oi]], writes=[self.R_oT[c]])

            load_chunk(0)
            for c in range(8):
                if c + 1 < 8:
                    load_chunk(c + 1)
                run_chunk(c)

    def phase_ffn(self, l, xsrc, xdst, last):
        sc = self.sc
        S = self.S
        R_src = self.R_x if l == 0 else self.R_xs
        R_dst = self.R_out if last else self.R_xs
        tiles = []
        t0 = 0
        while t0 < S:
            T = min(384, S - t0)
            tiles.append((t0, T))
            t0 += T
        TM = 384
        with ExitStack() as st:
            wo = self.sb(st, "p4wo", [128, NK, D], BF16)
            wdn = self.sb(st, "p4wdn", [128, NFF, D], BF16)
            R_wres = sc.res()
            R_wdnres = sc.res()
            for k in range(NK):
                sc.dma("sp", wo[:, k, :], self.wo_s[l][:, k, :], sc.dsem("p4w"), reads=[self.R_wo[l][k]], writes=[R_wres])

            def load_wdn():
                for j in range(NFF):
                    sc.dma("sp", wdn[:, j, :], self.wdn_s[l][:, j, :], sc.dsem("p4wd"), reads=[self.R_wdn[l][j]], writes=[R_wdnres])
            NWS = 3
            wup = [self.sb(st, "p4wu%d" % i, [128, 2, 2, NK, 128], BF16) for i in range(NWS)]
            R_wu = [sc.res() for _ in range(NWS)]
            xin = [self.sb(st, "p4x%d" % i, [128, 3, D], F32) for i in range(2)]
            R_x = [sc.res() for _ in range(2)]
            oT = self.sb(st, "p4oT", [128, NK, TM], BF16)
            R_oTt = sc.res()
            hb = [self.sb(st, "p4hb%d" % i, [128, D], BF16) for i in range(3)]
            R_hb = [sc.res() for _ in range(3)]
            hT2 = [self.sb(st, "p4hT%d" % i, [128, NK, TM + 2], BF16) for i in range(2)]
            R_hT2 = [sc.res() for _ in range(2)]
            g = self.sb(st, "p4g", [128, NFF, TM], BF16)
            R_g = sc.res()
            junk = self.sb(st, "p4junk", [128, D], BF16)
            R_junk = sc.res()
            stats = [[self.sb(st, "p4st%d_%d" % (a, b), [128, 4], F32) for b in range(3)] for a in range(2)]
            R_stat = [sc.res() for _ in range(2)]
            t1 = [[self.sb(st, "p4t1_%d%d" % (a, b), [128, TM], F32) for b in range(2)] for a in range(2)]
            R_t1 = [[sc.res() for _ in range(2)] for _ in range(2)]
            t2 = [self.sb(st, "p4t2_%d" % a, [128, TM], F32) for a in range(2)]
            R_t2 = [sc.res() for _ in range(2)]
            sg = [self.sb(st, "p4sg%d" % i, [128, TM], F32) for i in range(2)]
            R_sg = [sc.res() for _ in range(2)]
            R_gj = [sc.res() for _ in range(NFF)]
            if last:
                gfin = self.sb(st, "p4gf", [128, D], F32)
                R_gf = sc.res()
                sc.dma("sp", gfin[:, :], self.gfin_d[0:1, :].partition_broadcast(128), sc.dsem("p4gf"), writes=[R_gf])
            psY = [self.ps(st, "p4Y%d" % i, [128, 512], F32) for i in range(2)]
            R_Y = [sc.pres() for _ in range(2)]
            psU = [[self.ps(st, "p4U%d%d" % (a, b), [128, 512], F32) for b in range(2)] for a in range(2)]
            R_U = [[sc.pres() for _ in range(2)] for _ in range(2)]
            psT = self.ps(st, "p4T", [128, NK, 128], BF16)
            R_T = sc.pres()
            cwl = lambda j_, ch: self.cw[:, (l * 3 + j_) * 44 + ch:(l * 3 + j_) * 44 + ch + 1]
            cbl = lambda ch: self.cb[:, l * 44 + ch:l * 44 + ch + 1]
            wctr = [0]
            uctr = [0]

            def rms_rstd(xb, nsub, eps, si_):
                ssq, rt, rstd = stats[si_]
                for s in range(nsub):
                    sc.op("act", lambda e, s=s, ssq=ssq: e.activation(out=junk[:, :], in_=xin[xb][:, s, :], func=AF.Square,
                                                                      accum_out=ssq[:, s:s + 1]),
                          reads=[R_x[xb]], writes=[R_junk, R_stat[si_]])
                sc.op("act", lambda e, ssq=ssq, rt=rt: e.activation(out=rt[:, 0:nsub], in_=ssq[:, 0:nsub], func=AF.Sqrt, scale=1.0 / D, bias=eps),
                      reads=[R_stat[si_]], writes=[R_stat[si_]])
                sc.op("dve", lambda e, rt=rt, rstd=rstd: e.reciprocal(out=rstd[:, 0:nsub], in_=rt[:, 0:nsub]), reads=[R_stat[si_]], writes=[R_stat[si_]])

            def load_tile(tj):
                t0_, T_ = tiles[tj]
                xb_ = tj % 2
                sc.dma("sp", oT[:, :, 0:T_], self.oT_s[:, t0_:t0_ + T_].rearrange("(k p) s -> p k s", p=128), sc.dsem("p4oT"),
                       reads=self.R_oT, writes=[R_oTt])
                sc.dma("sp", xin[xb_][:, 0:T_ // 128, :], xsrc[t0_:t0_ + T_, :].rearrange("(s p) d -> p s d", p=128), sc.dsem("p4x%d" % xb_),
                       reads=R_src[t0_ // 128:(t0_ + T_) // 128], writes=[R_x[xb_]])

            def pa_units(tj):
                t0_, T_ = tiles[tj]
                return [(s_, hf) for s_ in range(T_ // 128) for hf in range(2)]

            def pa_unit(tj, s, hf):
                xb_ = tj % 2
                for k in range(NK):
                    sc.op("pe", lambda e, s=s, hf=hf, k=k: e.matmul(
                        psY[hf][:, :], lhsT=oT[:, k, s * 128:(s + 1) * 128], rhs=wo[:, k, hf * 512:(hf + 1) * 512],
                        start=(k == 0), stop=(k == NK - 1)), reads=[R_oTt, R_wres], writes=[R_Y[hf]])
                sc.op("dve", lambda e, s=s, hf=hf, xb_=xb_: e.tensor_tensor(
                    out=xin[xb_][:, s, hf * 512:(hf + 1) * 512], in0=psY[hf][:, :], in1=xin[xb_][:, s, hf * 512:(hf + 1) * 512], op=ALU.add),
                    reads=[R_Y[hf], R_x[xb_]], writes=[R_x[xb_]])

            def pa_stats(tj):
                rms_rstd(tj % 2, tiles[tj][1] // 128, 1e-6, 1)

            def pb_scale(tj):
                t0_, T_ = tiles[tj]
                xb_ = tj % 2
                hi_ = tj % 2
                if tj == 0:
                    sc.op("dve", lambda e: e.memset(hT2[0][:, :, 0:2], 0.0), writes=[R_hT2[0]])
                else:
                    Tp = tiles[tj - 1][1]
                    sc.op("act", lambda e, hi_=hi_, Tp=Tp: e.activation(out=hT2[hi_][:, :, 0:2], in_=hT2[1 - hi_][:, :, Tp:Tp + 2], func=AF.Copy),
                          reads=[R_hT2[1 - hi_]], writes=[R_hT2[hi_]])
                for s in range(T_ // 128):
                    sc.op("dve", lambda e, s=s, xb_=xb_: e.tensor_scalar(
                        out=hb[s][:, :], in0=xin[xb_][:, s, :], scalar1=stats[1][2][:, s:s + 1], scalar2=None, op0=ALU.mult),
                        reads=[R_x[xb_], R_stat[1]], writes=[R_hb[s]])

            def pb_transpose(tj):
                t0_, T_ = tiles[tj]
                hi_ = tj % 2
                for s in range(T_ // 128):
                    for k in range(NK):
                        sc.op("pe", lambda e, s=s, k=k: e.transpose(out=psT[:, k, :], in_=hb[s][:, k * 128:(k + 1) * 128],
                                                                     identity=self.ident[:, :]),
                              reads=[R_hb[s], self.R_const], writes=[R_T])
                    sc.op("act", lambda e, s=s, hi_=hi_: e.activation(out=hT2[hi_][:, :, 2 + s * 128:2 + (s + 1) * 128], in_=psT[:, :, :], func=AF.Copy),
                          reads=[R_T], writes=[R_hT2[hi_]])

            def prologue_a(tj):
                for (s_, hf) in pa_units(tj):
                    pa_unit(tj, s_, hf)
                pa_stats(tj)

            def prologue_b(tj):
                pb_scale(tj)
                pb_transpose(tj)

            def load_wup(gi_):
                wi_ = wctr[0] % NWS
                wctr[0] += 1
                sc.dma("sp", wup[wi_][:, :, :, :, :], self.wup_s[l][:, 2 * gi_:2 * gi_ + 2, :, :, :], sc.dsem("p4wu%d" % wi_),
                       reads=self.R_wup[l], writes=[R_wu[wi_]])
                return wi_

            pref = {}
            load_tile(0)
            for g_ in range(2):
                pref[(0, g_)] = load_wup(g_)
            load_wdn()
            prologue_a(0)
            prologue_b(0)
            for ti, (t0, T) in enumerate(tiles):
                nsub = T // 128
                xb = ti % 2
                sub0 = t0 // 128
                hti = ti % 2
                nxt = ti + 1 < len(tiles)
                units = pa_units(ti + 1) if nxt else []
                for gi in range(NFF // 2):
                    if (ti, gi) in pref:
                        wi = pref[(ti, gi)]
                    else:
                        wi = load_wup(gi)
                    if gi == 1 and nxt:
                        load_tile(ti + 1)
                    for jj in range(2):
                        j = 2 * gi + jj
                        p_ = j - 6
                        if nxt and 0 <= p_ < len(units):
                            pa_unit(ti + 1, *units[p_])
                        if nxt and p_ == len(units):
                            pa_stats(ti + 1)
                        if nxt and p_ == len(units) + 2:
                            pb_scale(ti + 1)
                        if nxt and p_ == len(units) + 5:
                            pb_transpose(ti + 1)
                        ui = uctr[0] % 2
                        uctr[0] += 1
                        for gv in range(2):
                            for k in range(NK):
                                sc.op("pe", lambda e, ui=ui, gv=gv, k=k, wi=wi, jj=jj, T=T, hti=hti: e.matmul(
                                    psU[ui][gv][:, 0:T + 2], lhsT=wup[wi][:, jj, gv, k, :], rhs=hT2[hti][:, k, 0:T + 2],
                                    start=(k == 0), stop=(k == NK - 1)), reads=[R_wu[wi], R_hT2[hti]], writes=[R_U[ui][gv]])
                        for gv in range(2):
                            ch = gv * NFF + j
                            U = psU[ui][gv]
                            sc.op("act", lambda e, U=U, ui=ui, gv=gv, ch=ch, T=T: e.activation(
                                out=t1[ui][gv][:, 0:T], in_=U[:, 2:T + 2], func=AF.Identity, scale=cwl(2, ch), bias=cbl(ch)),
                                reads=[R_U[ui][gv], self.R_const], writes=[R_t1[ui][gv]])
                            sc.op("dve", lambda e, U=U, ui=ui, gv=gv, ch=ch, T=T: e.scalar_tensor_tensor(
                                out=t2[gv][:, 0:T], in0=U[:, 1:T + 1], scalar=cwl(1, ch), in1=t1[ui][gv][:, 0:T], op0=ALU.mult, op1=ALU.add),
                                reads=[R_U[ui][gv], R_t1[ui][gv], self.R_const], writes=[R_t2[gv]])
                            sc.op("dve", lambda e, U=U, ui=ui, gv=gv, ch=ch, T=T: e.scalar_tensor_tensor(
                                out=t1[ui][gv][:, 0:T], in0=U[:, 0:T], scalar=cwl(0, ch), in1=t2[gv][:, 0:T], op0=ALU.mult, op1=ALU.add),
                                reads=[R_U[ui][gv], R_t2[gv], self.R_const], writes=[R_t1[ui][gv]])
                        sgi = ui
                        sc.op("act", lambda e, ui=ui, T=T, sgi=sgi: e.activation(out=sg[sgi][:, 0:T], in_=t1[ui][0][:, 0:T], func=AF.Silu),
                              reads=[R_t1[ui][0]], writes=[R_sg[sgi]])
                        sc.op("pool", lambda e, ui=ui, j=j, T=T, sgi=sgi: e.tensor_tensor(out=g[:, j, 0:T], in0=sg[sgi][:, 0:T], in1=t1[ui][1][:, 0:T], op=ALU.mult),
                              reads=[R_sg[sgi], R_t1[ui][1]], writes=[R_gj[j]], pool_main=True)
                if nxt:
                    for g_ in range(2):
                        pref[(ti + 1, g_)] = load_wup(g_)
                for s in range(nsub):
                    for hf in range(2):
                        for j in range(NFF):
                            sc.op("pe", lambda e, s=s, hf=hf, j=j: e.matmul(
                                psY[hf][:, :], lhsT=g[:, j, s * 128:(s + 1) * 128], rhs=wdn[:, j, hf * 512:(hf + 1) * 512],
                                start=(j == 0), stop=(j == NFF - 1)), reads=[R_gj[j], R_wdnres], writes=[R_Y[hf]])
                        sc.op("dve", lambda e, s=s, hf=hf, xb=xb: e.tensor_tensor(
                            out=xin[xb][:, s, hf * 512:(hf + 1) * 512], in0=psY[hf][:, :], in1=xin[xb][:, s, hf * 512:(hf + 1) * 512], op=ALU.add),
                            reads=[R_Y[hf], R_x[xb]], writes=[R_x[xb]])
                if last:
                    rms_rstd(xb, nsub, 1e-6, 0)
                    for s in range(nsub):
                        sc.op("dve", lambda e, s=s, xb=xb: e.scalar_tensor_tensor(
                            out=xin[xb][:, s, :], in0=xin[xb][:, s, :], scalar=stats[0][2][:, s:s + 1], in1=gfin[:, :], op0=ALU.mult, op1=ALU.mult),
                            reads=[R_x[xb], R_stat[0], R_gf], writes=[R_x[xb]])
                sc.dma("sp", xdst[t0:t0 + T, :].rearrange("(s p) d -> p s d", p=128), xin[xb][:, 0:nsub, :], sc.dsem("p4st%d" % xb),
                       reads=[R_x[xb]], writes=R_dst[sub0:sub0 + nsub])


def build_program(S=SEQ, debug=False, nlayers=2, stop_after=None):
    nc = bass.Bass("TRN2", target_bir_lowering=False)
    b = Builder(nc, S, debug=debug, nlayers=nlayers)
    b.stop_after = stop_after
    b.build()
    return nc, b


def make_in_maps(inputs, S, ncores):
    f = lambda a: np.ascontiguousarray(np.asarray(a, dtype=np.float32))
    consts = host_constants(S)

    def pk(v):
        return np.ascontiguousarray(np.asarray(v, np.float32).reshape(8, 128).T)

    gin_t = np.concatenate([pk(inputs["even_norm"][0]), pk(inputs["odd_norm"][0])], axis=1)
    gffn_t = np.concatenate([pk(inputs["ffn_norm"][0]), pk(inputs["ffn_norm"][1])], axis=1)
    subl_t = f(inputs["odd_subln"][0]).reshape(128, 1)
    lamq = np.stack([f(inputs["odd_lambda_q1"][0]), f(inputs["odd_lambda_k1"][0]),
                     f(inputs["odd_lambda_q2"][0]), f(inputs["odd_lambda_k2"][0])], axis=0)
    cw = f(inputs["ffn_conv_w"])
    cw_t = np.ascontiguousarray(cw.reshape(2, 3, 44, 128).transpose(3, 0, 1, 2).reshape(128, 2 * 3 * 44))
    cb = f(inputs["ffn_conv_b"])
    cb_t = np.ascontiguousarray(cb.reshape(2, 44, 128).transpose(2, 0, 1).reshape(128, 2 * 44))
    shared = dict(
        w_in0=f(inputs["even_w_in"][0]), w_in1=f(inputs["odd_w_qkv"][0]),
        w_out0=f(inputs["even_w_out"][0]), w_out1=f(inputs["odd_w_out"][0]),
        w_up0=f(inputs["ffn_w_up"][0]), w_up1=f(inputs["ffn_w_up"][1]),
        w_dn0=f(inputs["ffn_w_down"][0]), w_dn1=f(inputs["ffn_w_down"][1]),
        gin_t=gin_t, gffn_t=gffn_t, subl_t=subl_t, lamq=lamq, cw_t=cw_t, cb_t=cb_t,
        gfin=f(inputs["final_norm"]).reshape(1, D), **consts)
    x = f(inputs["x"])
    in_maps = []
    for c in range(ncores):
        m = dict(shared)
        m["x"] = np.ascontiguousarray(x[c])
        in_maps.append(m)
    return in_maps


def kernel(**inputs):
    x = np.asarray(inputs["x"])
    Bn, S, _ = x.shape
    nc, _ = build_program(S)
    in_maps = make_in_maps(inputs, S, Bn)
    res = run_bass_kernel_spmd(nc, in_maps, core_ids=list(range(Bn)))
    out = np.stack([np.asarray(r["out"], dtype=np.float32) for r in res.results], axis=0)
    return out
```
